# Optimizing a Trainium2 kernel written in Bass

```python
import math
import jax
import jax.numpy as jnp
from jax import lax
import numpy as np

D_MODEL = 4096
BATCH = 2
SEQ = 4096
DEPTH = 2

HEAD_DIM = 128
D_MIX = D_MODEL
DN_WIDTH = 3 * D_MIX // 8
LRU_WIDTH = 3 * D_MIX // 8
SG_WIDTH = D_MIX - DN_WIDTH - LRU_WIDTH
DN_HEADS = DN_WIDTH // HEAD_DIM
LRU_BLOCKS = LRU_WIDTH // HEAD_DIM
SG_GROUPS = SG_WIDTH // HEAD_DIM
DN_CHUNK = 64
SG_CHUNK = 128
SHORT_CONV = 4
FFN_CONV = 3
D_FF = ((8 * D_MODEL // 3 + 255) // 256) * 256
LRU_C = 8.0
EPS = 1e-6
IN_SIZES = (DN_WIDTH, DN_WIDTH, DN_WIDTH, DN_WIDTH, DN_HEADS, DN_HEADS,
            LRU_WIDTH, LRU_WIDTH, SG_WIDTH, SG_WIDTH)
D_IN = sum(IN_SIZES)

kernel_name = "hybrid_deltanet_rglru_sgmlp_block"


def rms_norm(x, w):
    xf = x.astype(jnp.float32)
    y = xf * lax.rsqrt(jnp.mean(xf * xf, axis=-1, keepdims=True) + EPS)
    return (y * w.astype(jnp.float32)).astype(x.dtype)


def group_rms_norm(x, w, groups):
    b, s, c = x.shape
    xg = x.astype(jnp.float32).reshape(b, s, groups, c // groups)
    y = xg * lax.rsqrt(jnp.mean(xg * xg, axis=-1, keepdims=True) + EPS)
    return y.reshape(b, s, c) * w.astype(jnp.float32)


def causal_dwconv(x, w):
    k_w = w.shape[0]
    s = x.shape[1]
    xp = jnp.pad(x, ((0, 0), (k_w - 1, 0), (0, 0)))
    return sum(xp[:, j:j + s] * w[j] for j in range(k_w))


def chunk_heads(t):
    b, s, h = t.shape[:3]
    t = t.reshape(b, s // DN_CHUNK, DN_CHUNK, h, *t.shape[3:])
    return jnp.moveaxis(t, 3, 1)


def gated_delta_rule(q, k, v, g, beta):
    b, s, h, d = q.shape
    q, k, v = chunk_heads(q) * d ** -0.5, chunk_heads(k), chunk_heads(v)
    g, beta = chunk_heads(g), chunk_heads(beta)
    gc = jnp.cumsum(g, axis=-1)
    causal = jnp.tril(jnp.ones((DN_CHUNK, DN_CHUNK), dtype=bool))
    strict = jnp.tril(jnp.ones((DN_CHUNK, DN_CHUNK), dtype=bool), k=-1)
    decay = jnp.exp(jnp.where(causal, gc[..., :, None] - gc[..., None, :], -jnp.inf))
    k_beta = k * beta[..., None]
    a_low = jnp.where(strict, jnp.einsum('bhnid,bhnjd->bhnij', k_beta, k) * decay, 0.0)
    rhs = jnp.concatenate([v * beta[..., None], k_beta * jnp.exp(gc)[..., None]], axis=-1)
    sol = lax.linalg.triangular_solve(a_low, rhs, left_side=True, lower=True, unit_diagonal=True)
    u, w = sol[..., :d], sol[..., d:]
    attn = jnp.einsum('bhnid,bhnjd->bhnij', q, k) * decay
    q_dec = q * jnp.exp(gc)[..., None]
    k_dec = k * jnp.exp(gc[..., -1:] - gc)[..., None]
    chunk_decay = jnp.exp(gc[..., -1])

    def step(state, inp):
        u_n, w_n, q_n, k_n, a_n, d_n = inp
        v_new = u_n - jnp.einsum('bhcd,bhde->bhce', w_n, state)
        o_n = (jnp.einsum('bhcd,bhde->bhce', q_n, state)
               + jnp.einsum('bhcj,bhje->bhce', a_n, v_new))
        state = state * d_n[..., None, None] + jnp.einsum('bhcd,bhce->bhde', k_n, v_new)
        return state, o_n

    xs = tuple(jnp.moveaxis(t, 2, 0) for t in (u, w, q_dec, k_dec, attn, chunk_decay))
    state0 = jnp.zeros((b, h, d, d), jnp.float32)
    _, o = lax.scan(step, state0, xs)
    return jnp.transpose(o, (1, 0, 3, 2, 4)).reshape(b, s, h, d)


def deltanet_group(q, k, v, z, b_raw, a_raw, conv_w, a_log, dt_bias, norm_w):
    bsz, s = q.shape[:2]
    qkv = jax.nn.silu(causal_dwconv(jnp.concatenate([q, k, v], axis=-1), conv_w))
    qkv = qkv.astype(jnp.float32).reshape(bsz, s, 3, DN_HEADS, HEAD_DIM)
    q, k, v = qkv[:, :, 0], qkv[:, :, 1], qkv[:, :, 2]
    q = q * lax.rsqrt(jnp.sum(q * q, axis=-1, keepdims=True) + EPS)
    k = k * lax.rsqrt(jnp.sum(k * k, axis=-1, keepdims=True) + EPS)
    beta = jax.nn.sigmoid(b_raw.astype(jnp.float32))
    g = -jnp.exp(a_log.astype(jnp.float32)) * jax.nn.softplus(
        a_raw.astype(jnp.float32) + dt_bias.astype(jnp.float32))
    o = gated_delta_rule(q, k, v, g, beta)
    zg = jax.nn.silu(z.astype(jnp.float32).reshape(bsz, s, DN_HEADS, HEAD_DIM))
    o = o * lax.rsqrt(jnp.mean(o * o, axis=-1, keepdims=True) + EPS) * norm_w.astype(jnp.float32) * zg
    return o.reshape(bsz, s, DN_WIDTH)


def _linear_combine(left, right):
    a_l, b_l = left
    a_r, b_r = right
    return a_l * a_r, a_r * b_l + b_r


def rglru_group(xb, gate, conv_w, conv_b, w_a, b_a, w_x, b_x, lam, norm_w):
    bsz, s, _ = xb.shape
    xc = (causal_dwconv(xb, conv_w) + conv_b).astype(jnp.float32)
    xh = xc.reshape(bsz, s, LRU_BLOCKS, HEAD_DIM)
    r = jax.nn.sigmoid(jnp.einsum('bsgi,gij->bsgj', xh, w_a.astype(jnp.float32)).reshape(bsz, s, LRU_WIDTH)
                       + b_a.astype(jnp.float32))
    i_g = jax.nn.sigmoid(jnp.einsum('bsgi,gij->bsgj', xh, w_x.astype(jnp.float32)).reshape(bsz, s, LRU_WIDTH)
                         + b_x.astype(jnp.float32))
    log_a = -LRU_C * r * jax.nn.softplus(-lam.astype(jnp.float32))
    a = jnp.exp(log_a)
    inp = jnp.sqrt(-jnp.expm1(2.0 * log_a)) * (i_g * xc)
    _, h = lax.associative_scan(_linear_combine, (a, inp), axis=1)
    y = h * jax.nn.gelu(gate.astype(jnp.float32))
    return group_rms_norm(y, norm_w, LRU_BLOCKS)


def spatial_gating_group(u, v, ln_w, ln_b, w_s, b_s, norm_w):
    bsz, s, _ = u.shape
    u = jax.nn.gelu(u.astype(jnp.float32))
    v = jax.nn.gelu(v.astype(jnp.float32))
    mu = jnp.mean(v, axis=-1, keepdims=True)
    var = jnp.mean(jnp.square(v - mu), axis=-1, keepdims=True)
    v = (v - mu) * lax.rsqrt(var + EPS) * ln_w.astype(jnp.float32) + ln_b.astype(jnp.float32)
    v = v.reshape(bsz, s // SG_CHUNK, SG_CHUNK, SG_GROUPS, HEAD_DIM)
    w_causal = jnp.tril(w_s.astype(jnp.float32))
    z = jnp.einsum('gts,bnsgc->bntgc', w_causal, v) + b_s.astype(jnp.float32).T[:, :, None]
    y = u * z.reshape(bsz, s, SG_WIDTH)
    return group_rms_norm(y, norm_w, SG_GROUPS)


def setup_inputs(seed: int = 0) -> dict:
    key = jax.random.key(seed)
    ks = jax.random.split(key, 27)
    L = DEPTH

    def nrm(k, shape, scale):
        return jax.random.normal(k, shape, jnp.float32) * scale

    def gain(k, shape):
        return 1.0 + nrm(k, shape, 0.02)

    dt = jnp.exp(jax.random.uniform(ks[5], (L, DN_HEADS), jnp.float32, math.log(1e-3), math.log(1e-1)))
    a_target = jax.random.uniform(ks[13], (L, LRU_WIDTH), jnp.float32, 0.9, 0.999)
    sig = a_target ** (1.0 / LRU_C)
    return {
        "x": nrm(ks[0], (BATCH, SEQ, D_MODEL), 1.0),
        "norm_mix": gain(ks[1], (L, D_MODEL)),
        "w_in": nrm(ks[2], (L, D_MODEL, D_IN), D_MODEL ** -0.5),
        "dn_conv_w": nrm(ks[3], (L, SHORT_CONV, 3 * DN_WIDTH), SHORT_CONV ** -0.5),
        "dn_a_log": jnp.log(jax.random.uniform(ks[4], (L, DN_HEADS), jnp.float32, 1.0, 16.0)),
        "dn_dt_bias": dt + jnp.log(-jnp.expm1(-dt)),
        "dn_norm_w": gain(ks[6], (L, HEAD_DIM)),
        "lru_conv_w": nrm(ks[7], (L, SHORT_CONV, LRU_WIDTH), SHORT_CONV ** -0.5),
        "lru_conv_b": nrm(ks[8], (L, LRU_WIDTH), 0.02),
        "lru_w_a": nrm(ks[9], (L, LRU_BLOCKS, HEAD_DIM, HEAD_DIM), HEAD_DIM ** -0.5),
        "lru_b_a": nrm(ks[10], (L, LRU_WIDTH), 0.02),
        "lru_w_x": nrm(ks[11], (L, LRU_BLOCKS, HEAD_DIM, HEAD_DIM), HEAD_DIM ** -0.5),
        "lru_b_x": nrm(ks[12], (L, LRU_WIDTH), 0.02),
        "lru_lambda": jnp.log(sig) - jnp.log1p(-sig),
        "lru_norm_w": gain(ks[14], (L, LRU_WIDTH)),
        "sg_ln_w": gain(ks[15], (L, SG_WIDTH)),
        "sg_ln_b": nrm(ks[16], (L, SG_WIDTH), 0.02),
        "sg_w_s": nrm(ks[17], (L, SG_GROUPS, SG_CHUNK, SG_CHUNK), SG_CHUNK ** -0.5),
        "sg_b_s": gain(ks[18], (L, SG_GROUPS, SG_CHUNK)),
        "sg_norm_w": gain(ks[19], (L, SG_WIDTH)),
        "w_out": nrm(ks[20], (L, D_MIX, D_MODEL), D_MIX ** -0.5),
        "norm_ffn": gain(ks[21], (L, D_MODEL)),
        "w_up": nrm(ks[22], (L, D_MODEL, 2 * D_FF), D_MODEL ** -0.5),
        "ffn_conv_w": nrm(ks[23], (L, FFN_CONV, 2 * D_FF), FFN_CONV ** -0.5),
        "ffn_conv_b": nrm(ks[24], (L, 2 * D_FF), 0.02),
        "w_down": nrm(ks[25], (L, D_FF, D_MODEL), D_FF ** -0.5),
        "norm_final": gain(ks[26], (D_MODEL,)),
    }


def reference(x, norm_mix, w_in, dn_conv_w, dn_a_log, dn_dt_bias, dn_norm_w,
              lru_conv_w, lru_conv_b, lru_w_a, lru_b_a, lru_w_x, lru_b_x, lru_lambda, lru_norm_w,
              sg_ln_w, sg_ln_b, sg_w_s, sg_b_s, sg_norm_w, w_out,
              norm_ffn, w_up, ffn_conv_w, ffn_conv_b, w_down, norm_final):
    split_idx = [int(i) for i in np.cumsum(IN_SIZES)[:-1]]
    for l in range(DEPTH):
        h = rms_norm(x, norm_mix[l])
        proj = h @ w_in[l]
        q, k, v, z, b_raw, a_raw, lx, lg, su, sv = jnp.split(proj, split_idx, axis=-1)
        y_a = deltanet_group(q, k, v, z, b_raw, a_raw, dn_conv_w[l], dn_a_log[l], dn_dt_bias[l], dn_norm_w[l])
        y_b = rglru_group(lx, lg, lru_conv_w[l], lru_conv_b[l], lru_w_a[l], lru_b_a[l],
                          lru_w_x[l], lru_b_x[l], lru_lambda[l], lru_norm_w[l])
        y_c = spatial_gating_group(su, sv, sg_ln_w[l], sg_ln_b[l], sg_w_s[l], sg_b_s[l], sg_norm_w[l])
        mix = jnp.concatenate([y_a, y_b, y_c], axis=-1).astype(x.dtype)
        x = x + mix @ w_out[l]
        h = rms_norm(x, norm_ffn[l])
        hid = causal_dwconv(h @ w_up[l], ffn_conv_w[l]) + ffn_conv_b[l]
        gate, up = jnp.split(hid, 2, axis=-1)
        x = x + (jax.nn.silu(gate) * up) @ w_down[l]
    return rms_norm(x, norm_final)
```

```python
import contextlib
import numpy as np
import concourse.bass as bass
import concourse.mybir as mybir
from concourse.bass_utils import run_bass_kernel_spmd

F32 = mybir.dt.float32
BF16 = mybir.dt.bfloat16
AF = mybir.ActivationFunctionType
ALU = mybir.AluOpType
AX = mybir.AxisListType

D_MODEL = 4096
SEQ = 4096
BATCH = 2
DEPTH = 2
HD = 128
DN_W = 1536
LRU_W = 1536
SG_W = 1024
DN_H = 12
D_IN = 11288
D_FF = 11008
EPS = 1e-6
NCORES = 8
TOK = 1024


class Buf:
    __slots__ = ("name", "last_w", "reads", "dsem", "dcnt")

    def __init__(self, name=""):
        self.name = name
        self.last_w = None
        self.reads = {}
        self.dsem = None
        self.dcnt = 0


class Sched:
    ENGS = ("pe", "act", "dve", "pool", "sp")

    def __init__(self, nc, stack):
        self.nc = nc
        self.stack = stack
        self.sem = {}
        self.count = {}
        self.prog = {e: [] for e in self.ENGS}
        self.known = {e: {} for e in self.ENGS}
        for e in self.ENGS:
            self.sem[e] = stack.enter_context(nc.semaphore("s_" + e))
            self.count[e] = 0
        self.ndsem = 0
        self.dpool = {e: [] for e in self.ENGS}
        self.dq = {}
        self.dcount = {}
        self.downers = []
        self.out_events = []
        self.ninstr = 0

    def _need(self, eng, ev, waits):
        if ev is None:
            return
        k, v = ev
        if eng == "pe" and k == "pe":
            return
        if self.known[eng].get(k, 0) >= v:
            return
        if waits.get(k, 0) < v:
            waits[k] = v

    def _waits(self, eng, reads, writes):
        waits = {}
        for b in reads:
            self._need(eng, b.last_w, waits)
        for b in writes:
            self._need(eng, b.last_w, waits)
            for k, v in b.reads.items():
                self._need(eng, (k, v), waits)
        for k, v in waits.items():
            self.known[eng][k] = v
        return waits

    def _commit(self, ev, reads, writes):
        for b in writes:
            b.last_w = ev
            b.reads = {}
        k, v = ev
        for b in reads:
            if b.reads.get(k, 0) < v:
                b.reads[k] = v

    def op(self, eng, fn, reads=(), writes=(), inc=True):
        waits = self._waits(eng, reads, writes)
        self.ninstr += 1
        if inc:
            self.count[eng] += 1
            ev = (eng, self.count[eng])
            self.prog[eng].append((list(waits.items()), fn, (eng, 1)))
        else:
            ev = (eng, self.count[eng] + 1)
            self.prog[eng].append((list(waits.items()), fn, None))
        self._commit(ev, reads, writes)
        return ev

    def dma(self, q, fn, reads=(), writes=(), owner=None, is_out=False):
        if owner is None:
            owner = writes[0] if writes else reads[0]
        if owner.dsem is None:
            if self.dpool[q]:
                key = self.dpool[q].pop()
            else:
                key = "d%d" % self.ndsem
                self.ndsem += 1
                self.sem[key] = self.stack.enter_context(self.nc.semaphore("s_" + key))
                self.dcount[key] = 0
            owner.dsem = key
            self.dq[key] = q
            self.downers.append(owner)
        key = owner.dsem
        assert self.dq[key] == q, "a DMA semaphore is bound to one DMA queue type"
        waits = self._waits(q, reads, writes)
        if self.dcount[key] > 0:
            w2 = {}
            self._need(q, (key, 16 * self.dcount[key]), w2)
            for k, v in w2.items():
                if waits.get(k, 0) < v:
                    waits[k] = v
                self.known[q][k] = max(self.known[q].get(k, 0), v)
        self.dcount[key] += 1
        self.ninstr += 1
        ev = (key, 16 * self.dcount[key])
        self.prog[q].append((list(waits.items()), fn, (key, 16)))
        self._commit(ev, reads, writes)
        if is_out:
            self.out_events.append(ev)
        return ev

    def barrier(self):
        evs = [(e, self.count[e]) for e in self.ENGS if self.count[e] > 0]
        evs += [(k, 16 * c) for k, c in self.dcount.items() if c > 0]
        for eng in self.ENGS:
            waits = {}
            for ev in evs:
                self._need(eng, ev, waits)
            for k, v in waits.items():
                self.known[eng][k] = v
            if waits:
                self.prog[eng].append((list(waits.items()), None, None))
        for b in self.downers:
            self.dpool[self.dq[b.dsem]].append(b.dsem)
            b.dsem = None
        self.downers = []

    def finish(self, eng="sp"):
        waits = {}
        for ev in self.out_events:
            self._need(eng, ev, waits)
        self.prog[eng].append((list(waits.items()), None, None))

    def emit(self):
        nc = self.nc
        sems = self.sem

        def run(engobj, items):
            for waits, fn, inc in items:
                for k, v in waits:
                    engobj.wait_ge(sems[k], v)
                if fn is not None:
                    ins = fn(engobj)
                    if inc is not None:
                        ins.then_inc(sems[inc[0]], inc[1])

        with nc.Block() as block:
            @block.sync
            def _(e):
                run(e, self.prog["sp"])

            @block.tensor
            def _(e):
                run(e, self.prog["pe"])

            @block.scalar
            def _(e):
                run(e, self.prog["act"])

            @block.vector
            def _(e):
                run(e, self.prog["dve"])

            @block.gpsimd
            def _(e):
                run(e, self.prog["pool"])


class Ctx:
    def __init__(self, nc, stack):
        self.nc = nc
        self.st = stack
        self.S = Sched(nc, stack)
        self.ps = []
        self.Bps = []
        for i in range(8):
            self.ps.append(stack.enter_context(nc.psum_tensor("psb%d" % i, [128, 512], F32)))
            self.Bps.append(Buf("ps%d" % i))
        self.bank = 0
        self.ARENA = 206 * 1024
        self.arena = stack.enter_context(nc.sbuf_tensor("arena", [128, self.ARENA // 4], F32))
        self.off = 0

    def sb(self, name, shape, dt):
        esz = 2 if dt == BF16 else 4
        n = 1
        for d in shape[1:]:
            n *= d
        nbytes = (n * esz + 31) // 32 * 32
        assert self.off + nbytes <= self.ARENA, ("SBUF arena overflow", name, self.off, nbytes)
        v = self.arena[0:shape[0], self.off // 4:(self.off + nbytes) // 4]
        self.off += nbytes
        if dt != F32:
            v = v.bitcast(dt)
        v = v[:, 0:n]
        if len(shape) == 3:
            v = v.rearrange("p (a b) -> p a b", a=shape[1])
        elif len(shape) == 4:
            v = v.rearrange("p (a b c) -> p a b c", a=shape[1], b=shape[2])
        return v

    def mark(self):
        return self.off

    def release(self, mark):
        self.S.barrier()
        self.off = mark

    def next_bank(self, lo=0, hi=8):
        b = lo + self.bank % (hi - lo)
        self.bank += 1
        return b


def p1_blocks():
    blocks = []
    c = 0
    while c < 6144:
        blocks.append((c, [(c, 128), (c + 128, 128)]))
        c += 256
    blocks.append((6144, [(6144, 24), (6168, 128), (6296, 128)]))
    c = 6424
    while c < D_IN:
        blocks.append((c, [(c, 128), (c + 128, 128)]))
        c += 256
    assert c == D_IN
    return blocks


def emit_p1(cx, xT, gain, w, projT, KC, T, blocks, tag="p1"):
    nc, S = cx.nc, cx.S
    m_p1 = cx.mark()
    NT = T // 512
    G = 4
    NG = KC // G
    dmodel = KC * 128
    xv = xT.ap.rearrange("(k p) t -> p k t", p=128)
    wv = w.ap.rearrange("(k p) c -> p k c", p=128)

    gain_sb = cx.sb(tag + "gain", [128, KC], F32)
    Bgain = Buf("gain")
    S.dma("sp", lambda e: e.dma_start(out=gain_sb[:], in_=gain.ap), reads=[gain.buf], writes=[Bgain])
    ones = cx.sb(tag + "ones", [128, 128], BF16)
    Bones = Buf("ones")
    S.op("dve", lambda e: e.memset(ones[:], 1.0), writes=[Bones])
    epsb = cx.sb(tag + "eps", [128, 1], F32)
    Beps = Buf("eps")
    S.op("dve", lambda e: e.memset(epsb[:], EPS), writes=[Beps])

    hT = cx.sb(tag + "hT", [128, KC, T], BF16)
    BhT = [[Buf("hT%d_%d" % (tt, g)) for g in range(NG)] for tt in range(NT)]
    rstd = cx.sb(tag + "rstd", [128, T], F32)
    Brstd = [Buf("rstd%d" % tt) for tt in range(NT)]
    xst = [cx.sb(tag + "xst%d" % i, [128, G, 512], F32) for i in range(2)]
    Bxst = [Buf("xst%d" % i) for i in range(2)]
    sq = [cx.sb(tag + "sq%d" % i, [128, G, 512], BF16) for i in range(2)]
    Bsq = [Buf("sq%d" % i) for i in range(2)]

    it = 0
    for tt in range(NT):
        tsl = slice(tt * 512, (tt + 1) * 512)
        bss = 7
        for g in range(NG):
            s = it % 2
            it += 1
            ksl = slice(g * G, (g + 1) * G)
            S.dma("sp", lambda e, s=s, ksl=ksl, tsl=tsl: e.dma_start(out=xst[s][:], in_=xv[:, ksl, tsl]),
                  reads=[xT.buf], writes=[Bxst[s]])
            S.op("act", lambda e, s=s: e.activation(out=sq[s][:], in_=xst[s][:], func=AF.Square),
                 reads=[Bxst[s]], writes=[Bsq[s]])
            S.op("dve", lambda e, s=s, ksl=ksl, tsl=tsl: e.tensor_tensor(
                out=hT[:, ksl, tsl], in0=xst[s][:],
                in1=gain_sb[:, ksl].unsqueeze(2).to_broadcast([128, G, 512]), op=ALU.mult),
                reads=[Bxst[s], Bgain], writes=[BhT[tt][g]])
            for i in range(G):
                first = (g == 0 and i == 0)
                last = (g == NG - 1 and i == G - 1)
                S.op("pe", lambda e, s=s, i=i, first=first, last=last: e.matmul(
                    cx.ps[bss][:], lhsT=ones[:], rhs=sq[s][:, i, :], start=first, stop=last),
                    reads=[Bones, Bsq[s]], writes=[cx.Bps[bss]], inc=(i == G - 1))
        S.op("act", lambda e, tsl=tsl: e.activation(out=rstd[:, tsl], in_=cx.ps[bss][:], func=AF.Sqrt,
                                                      bias=epsb[:, 0:1], scale=1.0 / dmodel),
             reads=[cx.Bps[bss], Beps], writes=[Brstd[tt]])
        S.op("dve", lambda e, tsl=tsl: e.reciprocal(out=rstd[:, tsl], in_=rstd[:, tsl]),
             reads=[Brstd[tt]], writes=[Brstd[tt]])

    WMAX = max(sum(wd for _, wd in chunks) for _, chunks in blocks)
    NSLOT = 3
    wbuf = [cx.sb(tag + "w%d" % i, [128, KC, WMAX], BF16) for i in range(NSLOT)]
    Bw = [Buf("w%d" % i) for i in range(NSLOT)]
    NOST = 3
    ost = [cx.sb(tag + "ost%d" % i, [128, T], F32) for i in range(NOST)]
    Bost = [Buf("ost%d" % i) for i in range(NOST)]
    oi = 0
    for bi, (c0, chunks) in enumerate(blocks):
        s = bi % NSLOT
        wblk = sum(wd for _, wd in chunks)
        S.dma("pool", lambda e, s=s, c0=c0, wblk=wblk: e.dma_start(out=wbuf[s][:, :, 0:wblk], in_=wv[:, :, c0:c0 + wblk]),
              reads=[w.buf], writes=[Bw[s]])
        for (cs, cw) in chunks:
            o = oi % NOST
            oi += 1
            for tt in range(NT):
                tsl = slice(tt * 512, (tt + 1) * 512)
                b = cx.next_bank(0, 7)
                for kc in range(KC):
                    S.op("pe", lambda e, b=b, s=s, kc=kc, cs=cs, cw=cw, c0=c0, tsl=tsl: e.matmul(
                        cx.ps[b][0:cw, :], lhsT=wbuf[s][:, kc, cs - c0:cs - c0 + cw], rhs=hT[:, kc, tsl],
                        start=(kc == 0), stop=(kc == KC - 1)),
                        reads=[Bw[s], BhT[tt][kc // G]], writes=[cx.Bps[b]], inc=(kc == KC - 1))
                S.op("dve", lambda e, b=b, o=o, cw=cw, tsl=tsl: e.tensor_tensor(
                    out=ost[o][0:cw, tsl], in0=cx.ps[b][0:cw, :], in1=rstd[0:cw, tsl], op=ALU.mult),
                    reads=[cx.Bps[b], Brstd[tt]], writes=[Bost[o]])
            S.dma("sp", lambda e, o=o, cs=cs, cw=cw: e.dma_start(out=projT.ap[cs:cs + cw, :], in_=ost[o][0:cw, :]),
                  reads=[Bost[o]], writes=[projT.b(cs)], owner=Bost[o], is_out=projT.is_out)
    cx.release(m_p1)


class DT:
    def __init__(self, nc, name, shape, dt, kind):
        self.t = nc.dram_tensor(name, list(shape), dt, kind=kind)
        self.ap = self.t.ap()
        self.buf = Buf(name)
        self.is_out = (kind == "ExternalOutput")
        self.name = name
        self._b = {}

    def b(self, key):
        if key not in self._b:
            self._b[key] = Buf("%s_%s" % (self.name, key))
        return self._b[key]

    def get(self, *idx):
        a = self.ap
        for i in idx:
            a = a[i]
        return a


class V:
    def __init__(self, ap, is_out=False, get=None, get_row=None, name="v"):
        self.ap = ap
        self.buf = Buf(name)
        self.is_out = is_out
        self.name = name
        self._b = {}
        if get is not None:
            self.get = get
        if get_row is not None:
            self.get_row = get_row

    def b(self, key):
        if key not in self._b:
            self._b[key] = Buf("%s_%s" % (self.name, key))
        return self._b[key]

    def get(self, *idx):
        a = self.ap
        for i in idx:
            a = a[i]
        return a


def build_p1(KC=32, T=TOK, blocks=None, ncol=D_IN):
    if blocks is None:
        blocks = p1_blocks()
    nc = bass.Bass("TRN2", target_bir_lowering=False)
    xT = DT(nc, "xT", [KC * 128, T], F32, "ExternalInput")
    gain = DT(nc, "gain", [128, KC], F32, "ExternalInput")
    w = DT(nc, "w", [KC * 128, ncol], F32, "ExternalInput")
    projT = DT(nc, "projT", [ncol, T], F32, "ExternalOutput")
    with contextlib.ExitStack() as st:
        cx = Ctx(nc, st)
        emit_p1(cx, xT, gain, w, projT, KC, T, blocks)
        cx.S.finish("sp")
        cx.S.emit()
    return nc


def emit_p3(cx, xT, mixT, w_out, gain2, w_up, cw, cb, w_down, outT, xmid, actT, T, gainf=None, tag="p3",
            KC=32, KD=86):
    nc, S = cx.nc, cx.S
    TH = T + 2
    TW = TH // 3
    assert TW * 3 == TH and TW <= 512
    dmodel = KC * 128
    dff = KD * 128
    m_all = cx.mark()

    gain_sb = cx.sb(tag + "gain", [128, KC], F32)
    Bgain = Buf("gain2")
    S.dma("sp", lambda e: e.dma_start(out=gain_sb, in_=gain2.ap), reads=[gain2.buf], writes=[Bgain])
    ones = cx.sb(tag + "ones", [128, 128], BF16)
    Bones = Buf("ones")
    S.op("dve", lambda e: e.memset(ones, 1.0), writes=[Bones])
    epsb = cx.sb(tag + "eps", [128, 1], F32)
    Beps = Buf("eps")
    S.op("dve", lambda e: e.memset(epsb, EPS), writes=[Beps])
    cwsb = cx.sb(tag + "cw", [128, 2 * KD, 3], F32)
    cbsb = cx.sb(tag + "cb", [128, 2 * KD], F32)
    Bcw = Buf("cw")
    S.dma("sp", lambda e: e.dma_start(out=cwsb, in_=cw.ap), reads=[cw.buf], writes=[Bcw])
    Bcb = Buf("cb")
    S.dma("sp", lambda e: e.dma_start(out=cbsb, in_=cb.ap), reads=[cb.buf], writes=[Bcb])

    h2T = cx.sb(tag + "h2T", [128, KC, TH], BF16)
    Bh2 = [Buf("h2_%d" % c) for c in range(KC)]
    rstd2 = cx.sb(tag + "rstd2", [128, TH], F32)
    Brstd2 = Buf("rstd2")

    m1 = cx.mark()
    mT = cx.sb(tag + "mT", [128, KC, TH], BF16)
    NMG = KC // 4
    BmT = [Buf("mT%d" % g) for g in range(NMG)]
    mv = mixT.ap.rearrange("(k p) t -> p k t", p=128)
    for g in range(NMG):
        S.dma("pool", lambda e, g=g: e.dma_start(out=mT[:, 4 * g:4 * g + 4, :], in_=mv[:, 4 * g:4 * g + 4, :]),
              reads=[mixT.buf], writes=[BmT[g]])
    xst = [cx.sb(tag + "xst%d" % i, [128, TH], F32) for i in range(2)]
    Bxst = [Buf("xst%d" % i) for i in range(2)]
    xm = [cx.sb(tag + "xm%d" % i, [128, TH], F32) for i in range(2)]
    Bxm = [Buf("xm%d" % i) for i in range(2)]
    sq = [cx.sb(tag + "sq%d" % i, [128, TH], BF16) for i in range(2)]
    Bsq = [Buf("sq%d" % i) for i in range(2)]
    NSLOT = 3
    NSLOT1 = 2
    wbuf = [cx.sb(tag + "w1_%d" % i, [128, KC, 256], BF16) for i in range(NSLOT1)]
    Bw = [Buf("w1_%d" % i) for i in range(NSLOT1)]
    wv = w_out.ap.rearrange("(k p) c -> p k c", p=128)
    for blk in range(KC // 2):
        c0 = blk * 256
        s = blk % NSLOT1
        S.dma("pool", lambda e, s=s, c0=c0: e.dma_start(out=wbuf[s], in_=wv[:, :, c0:c0 + 256]),
              reads=[w_out.buf], writes=[Bw[s]])
        for ci in range(2):
            chunk = blk * 2 + ci
            i2 = chunk % 2
            rsl = slice(chunk * 128, (chunk + 1) * 128)
            S.dma("sp", lambda e, i2=i2, rsl=rsl: e.dma_start(out=xst[i2], in_=xT.ap[rsl, :]),
                  reads=[xT.buf], writes=[Bxst[i2]])
            for tt in range(3):
                tsl = slice(tt * TW, (tt + 1) * TW)
                b = cx.next_bank(0, 5)
                for kc in range(KC):
                    S.op("pe", lambda e, b=b, s=s, kc=kc, ci=ci, tsl=tsl: e.matmul(
                        cx.ps[b][:, 0:TW], lhsT=wbuf[s][:, kc, ci * 128:(ci + 1) * 128], rhs=mT[:, kc, tsl],
                        start=(kc == 0), stop=(kc == KC - 1)),
                        reads=[Bw[s], BmT[kc // 4]], writes=[cx.Bps[b]], inc=(kc == KC - 1))
                S.op("dve", lambda e, b=b, i2=i2, tsl=tsl: e.tensor_tensor(
                    out=xm[i2][:, tsl], in0=cx.ps[b][:, 0:TW], in1=xst[i2][:, tsl], op=ALU.add),
                    reads=[cx.Bps[b], Bxst[i2]], writes=[Bxm[i2]])
            S.op("act", lambda e, i2=i2: e.activation(out=sq[i2], in_=xm[i2], func=AF.Square),
                 reads=[Bxm[i2]], writes=[Bsq[i2]])
            for tt in range(3):
                tsl = slice(tt * TW, (tt + 1) * TW)
                S.op("pe", lambda e, tt=tt, i2=i2, tsl=tsl, chunk=chunk: e.matmul(
                    cx.ps[5 + tt][:, 0:TW], lhsT=ones, rhs=sq[i2][:, tsl], start=(chunk == 0), stop=(chunk == KC - 1)),
                    reads=[Bones, Bsq[i2]], writes=[cx.Bps[5 + tt]], inc=(tt == 2))
            S.op("dve", lambda e, i2=i2, chunk=chunk: e.tensor_scalar(
                out=h2T[:, chunk, :], in0=xm[i2], scalar1=gain_sb[:, chunk:chunk + 1], scalar2=None, op0=ALU.mult),
                reads=[Bxm[i2], Bgain], writes=[Bh2[chunk]])
            S.dma("sp", lambda e, i2=i2, rsl=rsl: e.dma_start(out=xmid.ap[rsl, :], in_=xm[i2]),
                  reads=[Bxm[i2]], writes=[xmid.b(chunk)], owner=Bxm[i2])
    for tt in range(3):
        tsl = slice(tt * TW, (tt + 1) * TW)
        S.op("act", lambda e, tt=tt, tsl=tsl: e.activation(out=rstd2[:, tsl], in_=cx.ps[5 + tt][:, 0:TW], func=AF.Sqrt,
                                                             bias=epsb[:, 0:1], scale=1.0 / dmodel),
             reads=[cx.Bps[5 + tt], Beps], writes=[Brstd2])
    S.op("dve", lambda e: e.reciprocal(out=rstd2, in_=rstd2), reads=[Brstd2], writes=[Brstd2])

    cx.release(m1)
    wbuf2 = [cx.sb(tag + "w2_%d" % i, [128, KC, 256], BF16) for i in range(NSLOT)]
    Bw2 = [[Buf("w2_%d_%d" % (i, h)) for h in range(2)] for i in range(NSLOT)]
    pre = [[cx.sb(tag + "pre%d%d" % (i, h), [128, TH], F32) for h in range(2)] for i in range(2)]
    Bpre = [[Buf("pre%d%d" % (i, h)) for h in range(2)] for i in range(2)]
    hid = [[cx.sb(tag + "hid%d%d" % (i, h), [128, T], F32) for h in range(2)] for i in range(2)]
    Bhid = [[Buf("hid%d%d" % (i, h)) for h in range(2)] for i in range(2)]
    asb = [cx.sb(tag + "asb%d" % i, [128, T], BF16) for i in range(2)]
    Basb = [Buf("asb%d" % i) for i in range(2)]
    wv2 = w_up.ap.rearrange("(k p) c -> p k c", p=128)
    for j in range(KD):
        s = j % NSLOT
        i2 = j % 2
        for half in range(2):
            S.dma("pool", lambda e, s=s, j=j, half=half: e.dma_start(
                out=wbuf2[s][:, :, half * 128:(half + 1) * 128],
                in_=wv2[:, :, half * dff + j * 128:half * dff + (j + 1) * 128]),
                reads=[w_up.buf], writes=[Bw2[s][half]])
        for half in range(2):
            for tt in range(3):
                tsl = slice(tt * TW, (tt + 1) * TW)
                b = cx.next_bank(0, 8)
                for kc in range(KC):
                    S.op("pe", lambda e, b=b, s=s, kc=kc, half=half, tsl=tsl: e.matmul(
                        cx.ps[b][:, 0:TW], lhsT=wbuf2[s][:, kc, half * 128:(half + 1) * 128], rhs=h2T[:, kc, tsl],
                        start=(kc == 0), stop=(kc == KC - 1)),
                        reads=[Bw2[s][half], Bh2[kc]], writes=[cx.Bps[b]], inc=(kc == KC - 1))
                S.op("dve", lambda e, b=b, i2=i2, half=half, tsl=tsl: e.tensor_tensor(
                    out=pre[i2][half][:, tsl], in0=cx.ps[b][:, 0:TW], in1=rstd2[:, tsl], op=ALU.mult),
                    reads=[cx.Bps[b], Brstd2], writes=[Bpre[i2][half]])
            col = half * KD + j
            P_, H_ = pre[i2][half], hid[i2][half]
            S.op("dve", lambda e, P_=P_, H_=H_, col=col: e.tensor_scalar(
                out=H_, in0=P_[:, 0:T], scalar1=cwsb[:, col, 0:1], scalar2=cbsb[:, col:col + 1], op0=ALU.mult, op1=ALU.add),
                reads=[Bpre[i2][half], Bcw, Bcb], writes=[Bhid[i2][half]])
            S.op("dve", lambda e, P_=P_, H_=H_, col=col: e.scalar_tensor_tensor(
                out=H_, in0=P_[:, 1:T + 1], scalar=cwsb[:, col, 1:2], in1=H_, op0=ALU.mult, op1=ALU.add),
                reads=[Bpre[i2][half], Bcw], writes=[Bhid[i2][half]])
            S.op("dve", lambda e, P_=P_, H_=H_, col=col: e.scalar_tensor_tensor(
                out=H_, in0=P_[:, 2:T + 2], scalar=cwsb[:, col, 2:3], in1=H_, op0=ALU.mult, op1=ALU.add),
                reads=[Bpre[i2][half], Bcw], writes=[Bhid[i2][half]])
        S.op("act", lambda e, i2=i2: e.activation(out=hid[i2][0], in_=hid[i2][0], func=AF.Silu),
             reads=[Bhid[i2][0]], writes=[Bhid[i2][0]])
        S.op("dve", lambda e, i2=i2: e.tensor_tensor(out=asb[i2], in0=hid[i2][0], in1=hid[i2][1], op=ALU.mult),
             reads=[Bhid[i2][0], Bhid[i2][1]], writes=[Basb[i2]])
        S.dma("sp", lambda e, i2=i2, j=j: e.dma_start(out=actT.ap[j * 128:(j + 1) * 128, :], in_=asb[i2]),
              reads=[Basb[i2]], writes=[actT.b(j)], owner=Basb[i2])

    cx.release(m_all)
    gainf_sb = None
    if gainf is not None:
        gainf_sb = cx.sb(tag + "gainf", [128, KC], F32)
        Bgf = Buf("gainf")
        S.dma("sp", lambda e: e.dma_start(out=gainf_sb, in_=gainf.ap), reads=[gainf.buf], writes=[Bgf])
        ones = cx.sb(tag + "ones3", [128, 128], BF16)
        Bones = Buf("ones3")
        S.op("dve", lambda e: e.memset(ones, 1.0), writes=[Bones])
        epsb = cx.sb(tag + "eps3", [128, 1], F32)
        Beps = Buf("eps3")
        S.op("dve", lambda e: e.memset(epsb, EPS), writes=[Beps])
        rstdf = cx.sb(tag + "rstdf", [128, 512], F32)
        Brf = Buf("rstdf")
        sq3 = [cx.sb(tag + "sq3_%d" % i, [128, 512], BF16) for i in range(2)]
        Bsq3 = [Buf("sq3_%d" % i) for i in range(2)]
    GK = 8
    NAG = (KD + GK - 1) // GK
    aT = cx.sb(tag + "aT", [128, KD, 512], BF16)
    BaT = [Buf("aT%d" % g) for g in range(NAG)]
    wbuf3 = [cx.sb(tag + "w3_%d" % i, [128, KD, 256], BF16) for i in range(2)]
    Bw3 = [Buf("w3_%d" % i) for i in range(2)]
    xms = [cx.sb(tag + "xms%d" % i, [128, 512], F32) for i in range(2)]
    Bxms = [Buf("xms%d" % i) for i in range(2)]
    ost = [cx.sb(tag + "ost%d" % i, [128, 512], F32) for i in range(2)]
    Bost = [Buf("ost%d" % i) for i in range(2)]
    av = actT.ap.rearrange("(k p) t -> p k t", p=128)
    wv3 = w_down.ap.rearrange("(k p) c -> p k c", p=128)
    for th in range(T // 512):
        csl = slice(th * 512, (th + 1) * 512)
        for g in range(NAG):
            k0, k1 = g * GK, min(KD, (g + 1) * GK)
            S.dma("sp", lambda e, k0=k0, k1=k1, csl=csl: e.dma_start(out=aT[:, k0:k1, :], in_=av[:, k0:k1, csl]),
                  reads=[actT.b(j) for j in range(k0, k1)], writes=[BaT[g]])
        for blk in range(KC // 2):
            c0 = blk * 256
            s = (th * (KC // 2) + blk) % 2
            S.dma("pool", lambda e, s=s, c0=c0: e.dma_start(out=wbuf3[s], in_=wv3[:, :, c0:c0 + 256]),
                  reads=[w_down.buf], writes=[Bw3[s]])
            for ci in range(2):
                chunk = blk * 2 + ci
                i2 = chunk % 2
                rsl = slice(chunk * 128, (chunk + 1) * 128)
                S.dma("sp", lambda e, i2=i2, rsl=rsl, th=th: e.dma_start(
                    out=xms[i2], in_=xmid.ap[rsl, 2 + th * 512:2 + (th + 1) * 512]),
                    reads=[xmid.b(chunk)], writes=[Bxms[i2]])
                b = cx.next_bank(0, 7)
                for k in range(KD):
                    S.op("pe", lambda e, b=b, s=s, k=k, ci=ci: e.matmul(
                        cx.ps[b][:], lhsT=wbuf3[s][:, k, ci * 128:(ci + 1) * 128], rhs=aT[:, k, :],
                        start=(k == 0), stop=(k == KD - 1)),
                        reads=[Bw3[s], BaT[k // GK]], writes=[cx.Bps[b]], inc=(k == KD - 1))
                S.op("dve", lambda e, b=b, i2=i2: e.tensor_tensor(out=ost[i2], in0=cx.ps[b][:], in1=xms[i2], op=ALU.add),
                     reads=[cx.Bps[b], Bxms[i2]], writes=[Bost[i2]])
                if gainf is not None:
                    S.op("act", lambda e, i2=i2: e.activation(out=sq3[i2], in_=ost[i2], func=AF.Square),
                         reads=[Bost[i2]], writes=[Bsq3[i2]])
                    S.op("pe", lambda e, i2=i2, chunk=chunk: e.matmul(
                        cx.ps[7][:], lhsT=ones, rhs=sq3[i2], start=(chunk == 0), stop=(chunk == KC - 1)),
                        reads=[Bones, Bsq3[i2]], writes=[cx.Bps[7]])
                S.dma("sp", lambda e, i2=i2, rsl=rsl, csl=csl: e.dma_start(out=outT.ap[rsl, csl], in_=ost[i2]),
                      reads=[Bost[i2]], writes=[outT.b((chunk, th))], owner=Bost[i2], is_out=outT.is_out)
        if gainf is not None:
            S.op("act", lambda e: e.activation(out=rstdf, in_=cx.ps[7][:], func=AF.Sqrt, bias=epsb[:, 0:1], scale=1.0 / dmodel),
                 reads=[cx.Bps[7], Beps], writes=[Brf])
            S.op("dve", lambda e: e.reciprocal(out=rstdf, in_=rstdf), reads=[Brf], writes=[Brf])
            for chunk in range(KC):
                i2 = chunk % 2
                rsl = slice(chunk * 128, (chunk + 1) * 128)
                S.dma("sp", lambda e, i2=i2, rsl=rsl, csl=csl: e.dma_start(out=xms[i2], in_=outT.ap[rsl, csl]),
                      reads=[outT.b((chunk, th))], writes=[Bxms[i2]])
                S.op("dve", lambda e, i2=i2, chunk=chunk: e.scalar_tensor_tensor(
                    out=ost[i2], in0=xms[i2], scalar=gainf_sb[:, chunk:chunk + 1], in1=rstdf, op0=ALU.mult, op1=ALU.mult),
                    reads=[Bxms[i2], Bgf, Brf], writes=[Bost[i2]])
                S.dma("sp", lambda e, i2=i2, rsl=rsl, csl=csl: e.dma_start(out=outT.ap[rsl, csl], in_=ost[i2]),
                      reads=[Bost[i2]], writes=[outT.b((chunk, th))], owner=Bost[i2], is_out=outT.is_out)
    cx.release(m_all)


def build_p3(T=TOK, KC=32, KD=86, final=False):
    nc = bass.Bass("TRN2", target_bir_lowering=False)
    d, dff = KC * 128, KD * 128
    xT = DT(nc, "xT", [d, T + 2], F32, "ExternalInput")
    mixT = DT(nc, "mixT", [d, T + 2], F32, "ExternalInput")
    w_out = DT(nc, "w_out", [d, d], F32, "ExternalInput")
    gain2 = DT(nc, "gain2", [128, KC], F32, "ExternalInput")
    w_up = DT(nc, "w_up", [d, 2 * dff], F32, "ExternalInput")
    cw = DT(nc, "cw", [128, 2 * KD, 3], F32, "ExternalInput")
    cb = DT(nc, "cb", [128, 2 * KD], F32, "ExternalInput")
    w_down = DT(nc, "w_down", [dff, d], F32, "ExternalInput")
    gainf = DT(nc, "gainf", [128, KC], F32, "ExternalInput") if final else None
    outT = DT(nc, "outT", [d, T], F32, "ExternalOutput")
    xmid = DT(nc, "xmid", [d, T + 2], F32, "Internal")
    actT = DT(nc, "actT", [dff, T], BF16, "Internal")
    with contextlib.ExitStack() as st:
        cx = Ctx(nc, st)
        emit_p3(cx, xT, mixT, w_out, gain2, w_up, cw, cb, w_down, outT, xmid, actT, T, gainf=gainf, KC=KC, KD=KD)
        cx.S.finish("sp")
        cx.S.emit()
    return nc


NCONST = 9
C_ID, C_ONES, C_TRI, C_SELEND, C_SEL63, C_SEL127, C_MLS, C_MU, C_UP = range(NCONST)


def make_consts():
    c = np.zeros((128, NCONST, 128), np.float32)
    i = np.arange(128)
    blk = i // 64
    same = blk[:, None] == blk[None, :]
    c[:, C_ID, :] = np.eye(128)
    c[:, C_ONES, :] = 1.0
    c[:, C_TRI, :] = (same & (i[:, None] <= i[None, :]))
    c[:, C_SELEND, :] = (i[:, None] == (blk[None, :] * 64 + 63))
    c[:, C_SEL63, :] = (i[:, None] == 63)
    c[:, C_SEL127, :] = (i[:, None] == 127)
    c[:, C_MLS, :] = np.where(same & (i[None, :] < i[:, None]), 0.0, 1e30)
    c[:, C_MU, :] = np.where(same & (i[None, :] >= i[:, None]), 0.0, -1e30)
    c[:, C_UP, :] = (i[None, :] >= i[:, None])
    return c


def emit_lru(cx, lru_in, lru_pv, lru_w, lru_out, consts_sb, Bconst, NU=3, L=SEQ, tag="lru"):
    nc, S = cx.nc, cx.S
    m0 = cx.mark()
    NTL = L // 512
    lxp = cx.sb(tag + "lxp", [128, L + 3], F32)
    xc = cx.sb(tag + "xc", [128, L], F32)
    ra = cx.sb(tag + "ra", [128, L], F32)
    ig = cx.sb(tag + "ig", [128, L], F32)
    tmp = cx.sb(tag + "tmp", [128, L], F32)
    pv = cx.sb(tag + "pv", [128, 9], F32)
    wg = cx.sb(tag + "wg", [128, 2, 128], F32)
    sc = cx.sb(tag + "sc", [128, 4], F32)
    Blxp, Bxc, Bra, Big, Btmp, Bpv, Bwg, Bsc = [Buf(n) for n in ("lxp", "xc", "ra", "ig", "tmp", "pv", "wg", "sc")]
    ones = consts_sb[:, C_ONES, :]
    for u in range(NU):
        S.dma("sp", lambda e, u=u: e.dma_start(out=pv, in_=lru_pv.ap[u]), reads=[lru_pv.buf], writes=[Bpv])
        S.dma("sp", lambda e, u=u: e.dma_start(out=wg, in_=lru_w.ap[u].rearrange("g i j -> i g j")), reads=[lru_w.buf], writes=[Bwg])
        S.op("dve", lambda e: e.memset(lxp[:, 0:3], 0.0), writes=[Blxp])
        S.dma("sp", lambda e, u=u: e.dma_start(out=lxp[:, 3:L + 3], in_=lru_in.get(u, 0)), reads=[lru_in.buf], writes=[Blxp])
        S.op("dve", lambda e: e.tensor_scalar(out=xc, in0=lxp[:, 0:L], scalar1=pv[:, 0:1], scalar2=pv[:, 4:5], op0=ALU.mult, op1=ALU.add),
             reads=[Blxp, Bpv], writes=[Bxc])
        for j in range(1, 4):
            S.op("dve", lambda e, j=j: e.scalar_tensor_tensor(out=xc, in0=lxp[:, j:j + L], scalar=pv[:, j:j + 1], in1=xc, op0=ALU.mult, op1=ALU.add),
                 reads=[Blxp, Bpv, Bxc], writes=[Bxc])
        S.dma("sp", lambda e, u=u: e.dma_start(out=lxp[:, 0:L], in_=lru_in.get(u, 1)), reads=[lru_in.buf], writes=[Blxp])
        S.op("act", lambda e: e.activation(out=sc[:, 0:1], in_=pv[:, 7:8], func=AF.Exp, scale=-1.0), reads=[Bpv], writes=[Bsc])
        S.op("act", lambda e: e.activation(out=sc[:, 0:1], in_=sc[:, 0:1], func=AF.Ln, bias=ones[:, 0:1], scale=1.0), reads=[Bsc, Bconst], writes=[Bsc])
        S.op("dve", lambda e: e.tensor_scalar(out=sc[:, 1:2], in0=sc[:, 0:1], scalar1=-16.0, scalar2=None, op0=ALU.mult), reads=[Bsc], writes=[Bsc])
        S.op("dve", lambda e: e.tensor_scalar(out=sc[:, 0:1], in0=sc[:, 0:1], scalar1=-8.0, scalar2=None, op0=ALU.mult), reads=[Bsc], writes=[Bsc])
        for gi, (dst, Bdst, bcol) in enumerate(((ra, Bra, 5), (ig, Big, 6))):
            for tt in range(NTL):
                tsl = slice(tt * 512, (tt + 1) * 512)
                b = cx.next_bank(0, 8)
                S.op("pe", lambda e, b=b, gi=gi, tsl=tsl: e.matmul(cx.ps[b][:], lhsT=wg[:, gi, :], rhs=xc[:, tsl], start=True, stop=True),
                     reads=[Bwg, Bxc], writes=[cx.Bps[b]])
                S.op("act", lambda e, b=b, dst=dst, tsl=tsl, bcol=bcol: e.activation(
                    out=dst[:, tsl], in_=cx.ps[b][:], func=AF.Sigmoid, bias=pv[:, bcol:bcol + 1], scale=1.0),
                    reads=[cx.Bps[b], Bpv], writes=[Bdst])
        S.op("act", lambda e: e.activation(out=tmp, in_=ra, func=AF.Exp, scale=sc[:, 1:2]), reads=[Bra, Bsc], writes=[Btmp])
        S.op("act", lambda e: e.activation(out=ra, in_=ra, func=AF.Exp, scale=sc[:, 0:1]), reads=[Bra, Bsc], writes=[Bra])
        S.op("dve", lambda e: e.tensor_scalar(out=tmp, in0=tmp, scalar1=-1.0, scalar2=1.0, op0=ALU.mult, op1=ALU.add), reads=[Btmp], writes=[Btmp])
        S.op("act", lambda e: e.activation(out=tmp, in_=tmp, func=AF.Sqrt), reads=[Btmp], writes=[Btmp])
        S.op("dve", lambda e: e.tensor_tensor(out=ig, in0=ig, in1=tmp, op=ALU.mult), reads=[Big, Btmp], writes=[Big])
        S.op("dve", lambda e: e.tensor_tensor(out=ig, in0=ig, in1=xc, op=ALU.mult), reads=[Big, Bxc], writes=[Big])
        S.op("dve", lambda e: e.tensor_tensor_scan(out=tmp, data0=ra, data1=ig, initial=0.0, op0=ALU.mult, op1=ALU.add),
             reads=[Bra, Big], writes=[Btmp])
        S.op("act", lambda e: e.activation(out=lxp[:, 0:L], in_=lxp[:, 0:L], func=AF.Gelu), reads=[Blxp], writes=[Blxp])
        S.op("dve", lambda e: e.tensor_tensor(out=tmp, in0=tmp, in1=lxp[:, 0:L], op=ALU.mult), reads=[Btmp, Blxp], writes=[Btmp])
        S.op("act", lambda e: e.activation(out=xc, in_=tmp, func=AF.Square), reads=[Btmp], writes=[Bxc])
        for tt in range(NTL):
            tsl = slice(tt * 512, (tt + 1) * 512)
            b = cx.next_bank(0, 8)
            S.op("pe", lambda e, b=b, tsl=tsl: e.matmul(cx.ps[b][:], lhsT=ones, rhs=xc[:, tsl], start=True, stop=True),
                 reads=[Bconst, Bxc], writes=[cx.Bps[b]])
            S.op("act", lambda e, b=b, tsl=tsl: e.activation(out=ig[:, tsl], in_=cx.ps[b][:], func=AF.Sqrt, bias=consts_sb[:, C_ID, 0:1] if False else cx.epsb[:, 0:1], scale=1.0 / 128),
                 reads=[cx.Bps[b], cx.Beps], writes=[Big])
        S.op("dve", lambda e: e.reciprocal(out=ig, in_=ig), reads=[Big], writes=[Big])
        S.op("dve", lambda e: e.scalar_tensor_tensor(out=tmp, in0=tmp, scalar=pv[:, 8:9], in1=ig, op0=ALU.mult, op1=ALU.mult),
             reads=[Btmp, Bpv, Big], writes=[Btmp])
        S.dma("sp", lambda e, u=u: e.dma_start(out=lru_out.get(u), in_=tmp), reads=[Btmp], writes=[lru_out.b(u)], owner=Btmp, is_out=lru_out.is_out)
    cx.release(m0)


def p2_common(cx, consts):
    S = cx.S
    consts_sb = cx.sb("consts", [128, NCONST, 128], F32)
    Bconst = Buf("consts")
    S.dma("sp", lambda e: e.dma_start(out=consts_sb, in_=consts.ap), reads=[consts.buf], writes=[Bconst])
    cx.epsb = cx.sb("epsb", [128, 1], F32)
    cx.Beps = Buf("eps")
    S.op("dve", lambda e: e.memset(cx.epsb, EPS), writes=[cx.Beps])
    return consts_sb, Bconst


def build_lru_test(NU=1, L=SEQ):
    nc = bass.Bass("TRN2", target_bir_lowering=False)
    lru_in = DT(nc, "lru_in", [NU, 2, 128, L], F32, "ExternalInput")
    lru_pv = DT(nc, "lru_pv", [NU, 128, 9], F32, "ExternalInput")
    lru_w = DT(nc, "lru_w", [NU, 2, 128, 128], F32, "ExternalInput")
    consts = DT(nc, "consts", [128, NCONST, 128], F32, "ExternalInput")
    lru_out = DT(nc, "lru_out", [NU, 128, L], F32, "ExternalOutput")
    with contextlib.ExitStack() as st:
        cx = Ctx(nc, st)
        consts_sb, Bconst = p2_common(cx, consts)
        emit_lru(cx, lru_in, lru_pv, lru_w, lru_out, consts_sb, Bconst, NU=NU, L=L)
        cx.S.finish("sp")
        cx.S.emit()
    return nc


class QPool:
    def __init__(self, cx, banks):
        self.tiles = []
        for b in banks:
            self.tiles.append((cx.ps[b][:, 0:128], cx.Bps[b]))
        self.i = 0

    def next(self):
        t = self.tiles[self.i % len(self.tiles)]
        self.i += 1
        return t


class Rot:
    def __init__(self, cx, name, shape, dt, n):
        self.items = [(cx.sb("%s%d" % (name, i), shape, dt), Buf("%s%d" % (name, i))) for i in range(n)]
        self.i = 0

    def next(self):
        t = self.items[self.i % len(self.items)]
        self.i += 1
        return t


def emit_dn(cx, dn_in, dn_ab, dn_cw, dn_hp, dn_nw, dn_out, consts_sb, Bconst, NU=3, L=SEQ, tag="dn"):
    nc, S = cx.nc, cx.S
    S.barrier()
    m0 = cx.mark()
    SEG = 512
    NP = SEG // 128
    NSEG = L // SEG
    ident = consts_sb[:, C_ID, :]
    ones = consts_sb[:, C_ONES, :]
    QP = QPool(cx, [0, 1, 2, 3, 4, 5])
    BANK = [6, 7]
    bank_i = [0]

    def full_bank():
        b = BANK[bank_i[0] % 2]
        bank_i[0] += 1
        return cx.ps[b], cx.Bps[b]

    cwsb = cx.sb(tag + "cw", [128, 3, 4], F32)
    hp = cx.sb(tag + "hp", [128, 4], F32)
    nw = cx.sb(tag + "nw", [128, 1], F32)
    Bcw, Bhp, Bnw = Buf("dcw"), Buf("dhp"), Buf("dnw")
    S.dma("sp", lambda e: e.dma_start(out=nw, in_=dn_nw.ap), reads=[dn_nw.buf], writes=[Bnw])
    S_sb = cx.sb(tag + "S", [128, 128], F32)
    BS = Buf("S")
    NSC = 14
    (A_, BETA, NBETA, G_, GC, GL, GL0, GL1, EG, KBD, KDEC, D0, D1, TMP) = range(NSC)

    def mkset(i):
        d = {}
        for nm in ("qf", "kf", "vf", "zs", "sqb", "rn", "oT"):
            d[nm] = cx.sb("%s%s%d" % (tag, nm, i), [128, SEG], F32)
            d["B" + nm] = Buf(nm)
        for nm in ("Kbd", "Kdec", "Vb", "attnT", "U", "WT", "otm", "sqo"):
            d[nm] = cx.sb("%s%s%d" % (tag, nm, i), [128, NP, 128], F32)
            d["B" + nm] = [Buf("%s%d" % (nm, n)) for n in range(NP)]
        d["QdT"] = cx.sb("%sQdT%d" % (tag, i), [128, SEG], F32)
        d["BQdT"] = [Buf("QdT%d" % n) for n in range(NP)]
        d["sc"] = cx.sb("%ssc%d" % (tag, i), [128, NSC, NP], F32)
        d["Bsc"] = [Buf("sc%d" % k) for k in range(NSC)]
        d["ab"] = cx.sb("%sab%d" % (tag, i), [128, 2, NP], F32)
        d["Bab"] = Buf("ab")
        d["sso"] = cx.sb("%ssso%d" % (tag, i), [128, NP], F32)
        d["Bsso"] = Buf("sso")
        return d

    sets = [mkset(0), mkset(1)]
    pads = Rot(cx, tag + "pad", [128, SEG + 3], F32, 2)
    T128 = {nm: Rot(cx, tag + nm, [128, 128], F32, 2) for nm in ("Dg", "tL", "dL", "tU", "dU", "EG", "Bm", "Bt", "Pt")}
    PW = Rot(cx, tag + "pw", [128, 128], F32, 6)
    VN = Rot(cx, tag + "vn", [128, 128], F32, 2)

    for u in range(NU):
        S.dma("sp", lambda e, u=u: e.dma_start(out=cwsb, in_=dn_cw.ap[u]), reads=[dn_cw.buf], writes=[Bcw])
        S.dma("sp", lambda e, u=u: e.dma_start(out=hp[:, 0:2], in_=dn_hp.ap[u]), reads=[dn_hp.buf], writes=[Bhp])
        S.op("act", lambda e: e.activation(out=hp[:, 2:3], in_=hp[:, 0:1], func=AF.Exp), reads=[Bhp], writes=[Bhp])
        S.op("dve", lambda e: e.tensor_scalar(out=hp[:, 2:3], in0=hp[:, 2:3], scalar1=-1.0, scalar2=None, op0=ALU.mult), reads=[Bhp], writes=[Bhp])
        S.op("dve", lambda e: e.memset(S_sb, 0.0), writes=[BS])
        for seg in range(NSEG):
            d = sets[seg % 2]
            s0 = seg * SEG
            sc, Bsc = d["sc"], d["Bsc"]
            for part, nm in ((0, "qf"), (1, "kf"), (2, "vf")):
                pad, Bpad = pads.next()
                dst, Bdst = d[nm], d["B" + nm]
                if seg == 0:
                    S.op("dve", lambda e, pad=pad: e.memset(pad[:, 0:3], 0.0), writes=[Bpad])
                    S.dma("sp", lambda e, pad=pad, u=u, part=part: e.dma_start(out=pad[:, 3:SEG + 3], in_=dn_in.get(u, part)[:, 0:SEG]),
                          reads=[dn_in.buf], writes=[Bpad])
                else:
                    S.dma("sp", lambda e, pad=pad, u=u, part=part, s0=s0: e.dma_start(out=pad, in_=dn_in.get(u, part)[:, s0 - 3:s0 + SEG]),
                          reads=[dn_in.buf], writes=[Bpad])
                S.op("dve", lambda e, pad=pad, dst=dst, part=part: e.tensor_scalar(
                    out=dst, in0=pad[:, 0:SEG], scalar1=cwsb[:, part, 0:1], scalar2=None, op0=ALU.mult),
                    reads=[Bpad, Bcw], writes=[Bdst])
                for j in range(1, 4):
                    S.op("dve", lambda e, pad=pad, dst=dst, part=part, j=j: e.scalar_tensor_tensor(
                        out=dst, in0=pad[:, j:j + SEG], scalar=cwsb[:, part, j:j + 1], in1=dst, op0=ALU.mult, op1=ALU.add),
                        reads=[Bpad, Bcw, Bdst], writes=[Bdst])
                S.op("act", lambda e, dst=dst: e.activation(out=dst, in_=dst, func=AF.Silu), reads=[Bdst], writes=[Bdst])
                if part < 2:
                    S.op("act", lambda e, dst=dst, d=d: e.activation(out=d["sqb"], in_=dst, func=AF.Square), reads=[Bdst], writes=[d["Bsqb"]])
                    ps, Bp = full_bank()
                    S.op("pe", lambda e, ps=ps, d=d: e.matmul(ps[:], lhsT=ones, rhs=d["sqb"], start=True, stop=True),
                         reads=[Bconst, d["Bsqb"]], writes=[Bp])
                    S.op("act", lambda e, ps=ps, d=d: e.activation(out=d["rn"], in_=ps[:], func=AF.Sqrt, bias=cx.epsb[:, 0:1], scale=1.0),
                         reads=[Bp, cx.Beps], writes=[d["Brn"]])
                    S.op("dve", lambda e, d=d: e.reciprocal(out=d["rn"], in_=d["rn"]), reads=[d["Brn"]], writes=[d["Brn"]])
                    qs = (HD ** -0.5) if part == 0 else 1.0
                    S.op("dve", lambda e, dst=dst, d=d, qs=qs: e.scalar_tensor_tensor(
                        out=dst, in0=dst, scalar=qs, in1=d["rn"], op0=ALU.mult, op1=ALU.mult),
                        reads=[Bdst, d["Brn"]], writes=[Bdst])
            S.dma("sp", lambda e, d=d, u=u, s0=s0: e.dma_start(out=d["zs"], in_=dn_in.get(u, 3)[:, s0:s0 + SEG]),
                  reads=[dn_in.buf], writes=[d["Bzs"]])
            S.op("act", lambda e, d=d: e.activation(out=d["zs"], in_=d["zs"], func=AF.Silu), reads=[d["Bzs"]], writes=[d["Bzs"]])
            if hasattr(dn_ab, "get_row"):
                for ri in range(2):
                    S.dma("sp", lambda e, d=d, u=u, s0=s0, ri=ri: e.dma_start(
                        out=d["ab"][:, ri, :], in_=dn_ab.get_row(u, ri)[s0:s0 + SEG].rearrange("(n p) -> p n", p=128),
                        allow_slow_non_contiguous=True),
                        reads=[dn_ab.buf], writes=[d["Bab"]])
            else:
                S.dma("sp", lambda e, d=d, u=u, seg=seg: e.dma_start(out=d["ab"], in_=dn_ab.ap[u][:, :, seg * NP:(seg + 1) * NP]),
                      reads=[dn_ab.buf], writes=[d["Bab"]])
            S.op("act", lambda e, d=d, sc=sc: e.activation(out=sc[:, TMP, :], in_=d["ab"][:, 0, :], func=AF.Exp, bias=hp[:, 1:2], scale=1.0),
                 reads=[d["Bab"], Bhp], writes=[Bsc[TMP]])
            S.op("act", lambda e, sc=sc: e.activation(out=sc[:, TMP, :], in_=sc[:, TMP, :], func=AF.Ln, bias=ones[:, 0:1], scale=1.0),
                 reads=[Bsc[TMP], Bconst], writes=[Bsc[TMP]])
            S.op("dve", lambda e, sc=sc: e.tensor_scalar(out=sc[:, G_, :], in0=sc[:, TMP, :], scalar1=hp[:, 2:3], scalar2=None, op0=ALU.mult),
                 reads=[Bsc[TMP], Bhp], writes=[Bsc[G_]])
            S.op("act", lambda e, d=d, sc=sc: e.activation(out=sc[:, BETA, :], in_=d["ab"][:, 1, :], func=AF.Sigmoid),
                 reads=[d["Bab"]], writes=[Bsc[BETA]])
            S.op("dve", lambda e, sc=sc: e.tensor_scalar(out=sc[:, NBETA, :], in0=sc[:, BETA, :], scalar1=-1.0, scalar2=None, op0=ALU.mult),
                 reads=[Bsc[BETA]], writes=[Bsc[NBETA]])
            pq, Bq = QP.next()
            S.op("pe", lambda e, pq=pq, sc=sc: e.matmul(pq[:, 0:NP], lhsT=consts_sb[:, C_TRI, :], rhs=sc[:, G_, :], start=True, stop=True),
                 reads=[Bconst, Bsc[G_]], writes=[Bq])
            S.op("dve", lambda e, pq=pq, sc=sc: e.tensor_copy(out=sc[:, GC, :], in_=pq[:, 0:NP]), reads=[Bq], writes=[Bsc[GC]])
            for ci, slot in ((C_SELEND, GL), (C_SEL63, GL0), (C_SEL127, GL1)):
                pq, Bq = QP.next()
                S.op("pe", lambda e, pq=pq, sc=sc, ci=ci: e.matmul(pq[:, 0:NP], lhsT=consts_sb[:, ci, :], rhs=sc[:, GC, :], start=True, stop=True),
                     reads=[Bconst, Bsc[GC]], writes=[Bq])
                S.op("dve", lambda e, pq=pq, sc=sc, slot=slot: e.tensor_copy(out=sc[:, slot, :], in_=pq[:, 0:NP]), reads=[Bq], writes=[Bsc[slot]])
            S.op("act", lambda e, sc=sc: e.activation(out=sc[:, EG, :], in_=sc[:, GC, :], func=AF.Exp), reads=[Bsc[GC]], writes=[Bsc[EG]])
            S.op("dve", lambda e, sc=sc: e.tensor_tensor(out=sc[:, KBD, :], in0=sc[:, BETA, :], in1=sc[:, EG, :], op=ALU.mult),
                 reads=[Bsc[BETA], Bsc[EG]], writes=[Bsc[KBD]])
            S.op("dve", lambda e, sc=sc: e.tensor_tensor(out=sc[:, KDEC, :], in0=sc[:, GL, :], in1=sc[:, GC, :], op=ALU.subtract),
                 reads=[Bsc[GL], Bsc[GC]], writes=[Bsc[KDEC]])
            S.op("act", lambda e, sc=sc: e.activation(out=sc[:, KDEC, :], in_=sc[:, KDEC, :], func=AF.Exp), reads=[Bsc[KDEC]], writes=[Bsc[KDEC]])
            S.op("act", lambda e, sc=sc: e.activation(out=sc[:, D0, :], in_=sc[:, GL0, :], func=AF.Exp), reads=[Bsc[GL0]], writes=[Bsc[D0]])
            S.op("act", lambda e, sc=sc: e.activation(out=sc[:, D1, :], in_=sc[:, GL1, :], func=AF.Exp), reads=[Bsc[GL1]], writes=[Bsc[D1]])
            for n in range(NP):
                blk = slice(n * 128, (n + 1) * 128)
                pq, Bq = QP.next()
                S.op("pe", lambda e, pq=pq, d=d, blk=blk: e.transpose(out=pq, in_=d["kf"][:, blk], identity=ident),
                     reads=[d["Bkf"], Bconst], writes=[Bq])
                S.op("dve", lambda e, pq=pq, d=d, n=n, sc=sc: e.tensor_scalar(out=d["Kbd"][:, n, :], in0=pq, scalar1=sc[:, KBD, n:n + 1], scalar2=None, op0=ALU.mult),
                     reads=[Bq, Bsc[KBD]], writes=[d["BKbd"][n]])
                S.op("dve", lambda e, pq=pq, d=d, n=n, sc=sc: e.tensor_scalar(out=d["Kdec"][:, n, :], in0=pq, scalar1=sc[:, KDEC, n:n + 1], scalar2=None, op0=ALU.mult),
                     reads=[Bq, Bsc[KDEC]], writes=[d["BKdec"][n]])
                pq, Bq = QP.next()
                S.op("pe", lambda e, pq=pq, d=d, blk=blk: e.transpose(out=pq, in_=d["vf"][:, blk], identity=ident),
                     reads=[d["Bvf"], Bconst], writes=[Bq])
                S.op("dve", lambda e, pq=pq, d=d, n=n, sc=sc: e.tensor_scalar(out=d["Vb"][:, n, :], in0=pq, scalar1=sc[:, BETA, n:n + 1], scalar2=None, op0=ALU.mult),
                     reads=[Bq, Bsc[BETA]], writes=[d["BVb"][n]])
            for n in range(NP):
                blk = slice(n * 128, (n + 1) * 128)
                gcc = sc[:, GC, n:n + 1]
                Dg, BDg = T128["Dg"].next()
                S.op("dve", lambda e, Dg=Dg, gcc=gcc: e.tensor_scalar(out=Dg, in0=ident, scalar1=gcc, scalar2=None, op0=ALU.mult),
                     reads=[Bconst, Bsc[GC]], writes=[BDg])
                pA, BpA = QP.next()
                S.op("pe", lambda e, pA=pA, Dg=Dg: e.matmul(pA, lhsT=ones, rhs=Dg, start=True, stop=True), reads=[Bconst, BDg], writes=[BpA])
                tL, BtL = T128["tL"].next()
                S.op("dve", lambda e, tL=tL, pA=pA, gcc=gcc: e.scalar_tensor_tensor(
                    out=tL, in0=pA, scalar=gcc, in1=consts_sb[:, C_MLS, :], op0=ALU.subtract, op1=ALU.add),
                    reads=[BpA, Bsc[GC], Bconst], writes=[BtL])
                dL, BdL = T128["dL"].next()
                S.op("act", lambda e, dL=dL, tL=tL: e.activation(out=dL, in_=tL, func=AF.Exp, scale=-1.0), reads=[BtL], writes=[BdL])
                tU, BtU = T128["tU"].next()
                S.op("dve", lambda e, tU=tU, pA=pA, gcc=gcc: e.scalar_tensor_tensor(
                    out=tU, in0=pA, scalar=gcc, in1=consts_sb[:, C_MU, :], op0=ALU.subtract, op1=ALU.add),
                    reads=[BpA, Bsc[GC], Bconst], writes=[BtU])
                dU, BdU = T128["dU"].next()
                S.op("act", lambda e, dU=dU, tU=tU: e.activation(out=dU, in_=tU, func=AF.Exp), reads=[BtU], writes=[BdU])
                EGt, BEG = T128["EG"].next()
                S.op("act", lambda e, EGt=EGt, pA=pA: e.activation(out=EGt, in_=pA, func=AF.Exp), reads=[BpA], writes=[BEG])
                S.op("dve", lambda e, d=d, blk=blk, EGt=EGt: e.tensor_tensor(out=d["QdT"][:, blk], in0=d["qf"][:, blk], in1=EGt, op=ALU.mult),
                     reads=[d["Bqf"], BEG], writes=[d["BQdT"][n]])
                pG, BpG = QP.next()
                S.op("pe", lambda e, pG=pG, d=d, blk=blk: e.matmul(pG, lhsT=d["kf"][:, blk], rhs=d["kf"][:, blk], start=True, stop=True),
                     reads=[d["Bkf"]], writes=[BpG])
                Bm, BBm = T128["Bm"].next()
                S.op("dve", lambda e, Bm=Bm, pG=pG, sc=sc, n=n, dL=dL: e.scalar_tensor_tensor(
                    out=Bm, in0=pG, scalar=sc[:, NBETA, n:n + 1], in1=dL, op0=ALU.mult, op1=ALU.mult),
                    reads=[BpG, Bsc[NBETA], BdL], writes=[BBm])
                pQ, BpQ = QP.next()
                S.op("pe", lambda e, pQ=pQ, d=d, blk=blk: e.matmul(pQ, lhsT=d["kf"][:, blk], rhs=d["qf"][:, blk], start=True, stop=True),
                     reads=[d["Bkf"], d["Bqf"]], writes=[BpQ])
                S.op("dve", lambda e, d=d, n=n, pQ=pQ, dU=dU: e.tensor_tensor(out=d["attnT"][:, n, :], in0=pQ, in1=dU, op=ALU.mult),
                     reads=[BpQ, BdU], writes=[d["BattnT"][n]])
                pT, BpT = QP.next()
                S.op("pe", lambda e, pT=pT, Bm=Bm: e.transpose(out=pT, in_=Bm, identity=ident), reads=[BBm, Bconst], writes=[BpT])
                Bt, BBt = T128["Bt"].next()
                S.op("act", lambda e, Bt=Bt, pT=pT: e.copy(out=Bt, in_=pT), reads=[BpT], writes=[BBt])
                Pt, BPt = T128["Pt"].next()
                S.op("dve", lambda e, Pt=Pt, Bt=Bt: e.tensor_tensor(out=Pt, in0=Bt, in1=ident, op=ALU.add), reads=[BBt, Bconst], writes=[BPt])
                cB, BcB, cBt, BcBt = Bm, BBm, Bt, BBt
                for lvl in range(1, 6):
                    p1, Bp1 = QP.next()
                    S.op("pe", lambda e, p1=p1, cBt=cBt, cB=cB: e.matmul(p1, lhsT=cBt, rhs=cB, start=True, stop=True),
                         reads=[BcBt, BcB], writes=[Bp1])
                    nB, BnB = PW.next()
                    S.op("act", lambda e, nB=nB, p1=p1: e.copy(out=nB, in_=p1), reads=[Bp1], writes=[BnB])
                    nBt, BnBt = None, None
                    if lvl < 5:
                        p2, Bp2 = QP.next()
                        S.op("pe", lambda e, p2=p2, cBt=cBt, cB=cB: e.matmul(p2, lhsT=cB, rhs=cBt, start=True, stop=True),
                             reads=[BcBt, BcB], writes=[Bp2])
                        nBt, BnBt = PW.next()
                        S.op("dve", lambda e, nBt=nBt, p2=p2: e.tensor_copy(out=nBt, in_=p2), reads=[Bp2], writes=[BnBt])
                    p3, Bp3 = QP.next()
                    S.op("pe", lambda e, p3=p3, nB=nB, Pt=Pt: e.matmul(p3, lhsT=nB, rhs=Pt, start=True, stop=True),
                         reads=[BnB, BPt], writes=[Bp3])
                    S.op("dve", lambda e, Pt=Pt, p3=p3: e.tensor_tensor(out=Pt, in0=Pt, in1=p3, op=ALU.add), reads=[BPt, Bp3], writes=[BPt])
                    cB, BcB, cBt, BcBt = nB, BnB, nBt, BnBt
                pU, BpU = QP.next()
                for hb in range(2):
                    P = slice(hb * 64, hb * 64 + 64)
                    S.op("pe", lambda e, pU=pU, Pt=Pt, d=d, n=n, P=P: e.matmul(pU[P, :], lhsT=Pt[P, P], rhs=d["Vb"][P, n, :], start=True, stop=True),
                         reads=[BPt, d["BVb"][n]], writes=[BpU])
                S.op("act", lambda e, pU=pU, d=d, n=n: e.copy(out=d["U"][:, n, :], in_=pU), reads=[BpU], writes=[d["BU"][n]])
                pW, BpW = QP.next()
                S.op("pe", lambda e, pW=pW, Pt=Pt, d=d, n=n: e.matmul(pW, lhsT=d["Kbd"][:, n, :], rhs=Pt, start=True, stop=True),
                     reads=[BPt, d["BKbd"][n]], writes=[BpW])
                S.op("dve", lambda e, pW=pW, d=d, n=n: e.tensor_copy(out=d["WT"][:, n, :], in_=pW), reads=[BpW], writes=[d["BWT"][n]])
            for c in range(2 * NP):
                n, hb = c // 2, c % 2
                P = slice(hb * 64, hb * 64 + 64)
                cols = slice(n * 128 + hb * 64, n * 128 + hb * 64 + 64)
                pa, Bpa = QP.next()
                S.op("pe", lambda e, pa=pa, d=d, n=n, P=P: e.matmul(pa[P, :], lhsT=d["WT"][:, n, P], rhs=S_sb, start=True, stop=True),
                     reads=[d["BWT"][n], BS], writes=[Bpa])
                vn, Bvn = VN.next()
                S.op("dve", lambda e, vn=vn, pa=pa, d=d, n=n, P=P: e.tensor_tensor(out=vn[P, :], in0=d["U"][P, n, :], in1=pa[P, :], op=ALU.subtract),
                     reads=[d["BU"][n], Bpa], writes=[Bvn])
                po, Bpo = QP.next()
                S.op("pe", lambda e, po=po, d=d, cols=cols, P=P: e.matmul(po[P, :], lhsT=d["QdT"][:, cols], rhs=S_sb, start=True, stop=False),
                     reads=[d["BQdT"][n], BS], writes=[Bpo])
                S.op("pe", lambda e, po=po, d=d, n=n, P=P, vn=vn: e.matmul(po[P, :], lhsT=d["attnT"][P, n, P], rhs=vn[P, :], start=False, stop=True),
                     reads=[d["BattnT"][n], Bvn], writes=[Bpo])
                pS, BpS = QP.next()
                S.op("pe", lambda e, pS=pS, d=d, n=n, P=P, vn=vn: e.matmul(pS, lhsT=d["Kdec"][P, n, :], rhs=vn[P, :], start=True, stop=True),
                     reads=[d["BKdec"][n], Bvn], writes=[BpS])
                dslot = D0 if hb == 0 else D1
                S.op("dve", lambda e, pS=pS, sc=sc, dslot=dslot, n=n: e.scalar_tensor_tensor(
                    out=S_sb, in0=S_sb, scalar=sc[:, dslot, n:n + 1], in1=pS, op0=ALU.mult, op1=ALU.add),
                    reads=[BS, Bsc[dslot], BpS], writes=[BS])
                S.op("act", lambda e, po=po, d=d, n=n, P=P: e.copy(out=d["otm"][P, n, :], in_=po[P, :]), reads=[Bpo], writes=[d["Botm"][n]])
            S.op("act", lambda e, d=d: e.activation(out=d["sqo"], in_=d["otm"], func=AF.Square), reads=d["Botm"], writes=d["Bsqo"])
            S.op("dve", lambda e, d=d: e.tensor_reduce(out=d["sso"], in_=d["sqo"], axis=AX.X, op=ALU.add), reads=d["Bsqo"], writes=[d["Bsso"]])
            S.op("act", lambda e, d=d: e.activation(out=d["sso"], in_=d["sso"], func=AF.Sqrt, bias=cx.epsb[:, 0:1], scale=1.0 / HD),
                 reads=[d["Bsso"], cx.Beps], writes=[d["Bsso"]])
            S.op("dve", lambda e, d=d: e.reciprocal(out=d["sso"], in_=d["sso"]), reads=[d["Bsso"]], writes=[d["Bsso"]])
            S.op("dve", lambda e, d=d: e.tensor_tensor(out=d["otm"], in0=d["otm"], in1=d["sso"].unsqueeze(2).to_broadcast([128, NP, 128]), op=ALU.mult),
                 reads=d["Botm"] + [d["Bsso"]], writes=d["Botm"])
            for n in range(NP):
                blk = slice(n * 128, (n + 1) * 128)
                pT, BpT = QP.next()
                S.op("pe", lambda e, pT=pT, d=d, n=n: e.transpose(out=pT, in_=d["otm"][:, n, :], identity=ident),
                     reads=[d["Botm"][n], Bconst], writes=[BpT])
                S.op("dve", lambda e, pT=pT, d=d, blk=blk: e.scalar_tensor_tensor(
                    out=d["oT"][:, blk], in0=pT, scalar=nw[:, 0:1], in1=d["zs"][:, blk], op0=ALU.mult, op1=ALU.mult),
                    reads=[BpT, Bnw, d["Bzs"]], writes=[d["BoT"]])
            S.dma("sp", lambda e, d=d, u=u, s0=s0: e.dma_start(out=dn_out.get(u)[:, s0:s0 + SEG], in_=d["oT"]),
                  reads=[d["BoT"]], writes=[dn_out.b((u, seg))], owner=d["BoT"], is_out=dn_out.is_out)
    cx.release(m0)


def build_dn_test(NU=1, L=SEQ):
    nc = bass.Bass("TRN2", target_bir_lowering=False)
    dn_in = DT(nc, "dn_in", [NU, 4, 128, L], F32, "ExternalInput")
    dn_ab = DT(nc, "dn_ab", [NU, 128, 2, L // 128], F32, "ExternalInput")
    dn_cw = DT(nc, "dn_cw", [NU, 128, 3, 4], F32, "ExternalInput")
    dn_hp = DT(nc, "dn_hp", [NU, 128, 2], F32, "ExternalInput")
    dn_nw = DT(nc, "dn_nw", [128, 1], F32, "ExternalInput")
    consts = DT(nc, "consts", [128, NCONST, 128], F32, "ExternalInput")
    dn_out = DT(nc, "dn_out", [NU, 128, L], F32, "ExternalOutput")
    with contextlib.ExitStack() as st:
        cx = Ctx(nc, st)
        consts_sb, Bconst = p2_common(cx, consts)
        emit_dn(cx, dn_in, dn_ab, dn_cw, dn_hp, dn_nw, dn_out, consts_sb, Bconst, NU=NU, L=L)
        cx.S.finish("sp")
        cx.S.emit()
    return nc


def emit_sg(cx, suv, sg_wT, sg_bs, sg_vec, sg_out, consts_sb, Bconst, T=TOK, tag="sg", fm_src=None):
    nc, S = cx.nc, cx.S
    S.barrier()
    m0 = cx.mark()
    NG = SG_W // 128
    ident = consts_sb[:, C_ID, :]
    vec = cx.sb(tag + "vec", [128, 3, SG_W], F32)
    Bvec = Buf("sgvec")
    for i in range(3):
        S.dma("sp", lambda e, i=i: e.dma_start(out=vec[:, i, :], in_=sg_vec.ap[i].partition_broadcast(128)),
              reads=[sg_vec.buf], writes=[Bvec])
    wT = cx.sb(tag + "wT", [128, NG, 128], F32)
    BwT = Buf("sgwT")
    S.dma("sp", lambda e: e.dma_start(out=wT, in_=sg_wT.ap.rearrange("g s t -> s g t")), reads=[sg_wT.buf], writes=[BwT])
    S.op("dve", lambda e: e.tensor_tensor(out=wT, in0=wT, in1=consts_sb[:, C_UP:C_UP + 1, :].to_broadcast([128, NG, 128]), op=ALU.mult),
         reads=[BwT, Bconst], writes=[BwT])
    bs = cx.sb(tag + "bs", [128, NG], F32)
    Bbs = Buf("sgbs")
    S.dma("sp", lambda e: e.dma_start(out=bs, in_=sg_bs.ap), reads=[sg_bs.buf], writes=[Bbs])
    R = {nm: Rot(cx, tag + nm, [128, SG_W], F32, 2) for nm in ("u", "v", "y")}
    Rst = Rot(cx, tag + "st", [128, 16], F32, 2)
    Ro = Rot(cx, tag + "o", [128, NG, 128], F32, 2)
    Rfm = Rot(cx, tag + "fm", [128, 2 * NG, 128], F32, 2) if fm_src is not None else None
    for ch in range(T // 128):
        rows = slice(ch * 128, (ch + 1) * 128)
        u, Bu = R["u"].next()
        v, Bv = R["v"].next()
        y, By = R["y"].next()
        st, Bst = Rst.next()
        if fm_src is None:
            S.dma("sp", lambda e, u=u, rows=rows: e.dma_start(out=u, in_=suv.ap[rows, 0:SG_W]), reads=[suv.buf], writes=[Bu])
            S.dma("sp", lambda e, v=v, rows=rows: e.dma_start(out=v, in_=suv.ap[rows, SG_W:2 * SG_W]), reads=[suv.buf], writes=[Bv])
            S.op("act", lambda e, u=u: e.activation(out=u, in_=u, func=AF.Gelu), reads=[Bu], writes=[Bu])
            S.op("act", lambda e, v=v, st=st: e.activation(out=v, in_=v, func=AF.Gelu, accum_out=st[:, 0:1]), reads=[Bv], writes=[Bv, Bst])
        else:
            fm, Bfm = Rfm.next()
            S.dma("sp", lambda e, fm=fm, rows=rows: e.dma_start(out=fm, in_=fm_src.ap.rearrange("(b c) t -> c b t", c=128)[:, :, rows]),
                  reads=[fm_src.buf], writes=[Bfm])
            for blk in range(2 * NG):
                b = blk % 6
                dstt, Bd = (u, Bu) if blk < NG else (v, Bv)
                cs = slice((blk % NG) * 128, (blk % NG + 1) * 128)
                S.op("pe", lambda e, b=b, fm=fm, blk=blk: e.transpose(out=cx.ps[b][:, 0:128], in_=fm[:, blk, :], identity=ident),
                     reads=[Bfm, Bconst], writes=[cx.Bps[b]])
                S.op("act", lambda e, b=b, dstt=dstt, cs=cs: e.activation(out=dstt[:, cs], in_=cx.ps[b][:, 0:128], func=AF.Gelu),
                     reads=[cx.Bps[b]], writes=[Bd])
            S.op("dve", lambda e, v=v, st=st: e.tensor_reduce(out=st[:, 0:1], in_=v, axis=AX.X, op=ALU.add), reads=[Bv], writes=[Bst])
        S.op("dve", lambda e, st=st: e.tensor_scalar(out=st[:, 1:2], in0=st[:, 0:1], scalar1=1.0 / SG_W, scalar2=None, op0=ALU.mult), reads=[Bst], writes=[Bst])
        S.op("dve", lambda e, v=v, st=st: e.tensor_scalar(out=v, in0=v, scalar1=st[:, 1:2], scalar2=None, op0=ALU.subtract), reads=[Bv, Bst], writes=[Bv])
        S.op("act", lambda e, v=v, y=y, st=st: e.activation(out=y, in_=v, func=AF.Square, accum_out=st[:, 2:3]), reads=[Bv], writes=[By, Bst])
        S.op("act", lambda e, st=st: e.activation(out=st[:, 3:4], in_=st[:, 2:3], func=AF.Sqrt, bias=cx.epsb[:, 0:1], scale=1.0 / SG_W),
             reads=[Bst, cx.Beps], writes=[Bst])
        S.op("dve", lambda e, st=st: e.reciprocal(out=st[:, 3:4], in_=st[:, 3:4]), reads=[Bst], writes=[Bst])
        S.op("dve", lambda e, v=v, st=st: e.scalar_tensor_tensor(out=v, in0=v, scalar=st[:, 3:4], in1=vec[:, 0, :], op0=ALU.mult, op1=ALU.mult),
             reads=[Bv, Bst, Bvec], writes=[Bv])
        S.op("dve", lambda e, v=v: e.tensor_tensor(out=v, in0=v, in1=vec[:, 1, :], op=ALU.add), reads=[Bv, Bvec], writes=[Bv])
        for half in range(2):
            b = 6 + half
            for gg in range(4):
                g = half * 4 + gg
                S.op("pe", lambda e, b=b, gg=gg, g=g, v=v: e.matmul(cx.ps[b][:, gg * 128:(gg + 1) * 128], lhsT=wT[:, g, :], rhs=v[:, g * 128:(g + 1) * 128],
                                                                     start=True, stop=True),
                     reads=[BwT, Bv], writes=[cx.Bps[b]])
            for gg in range(4):
                g = half * 4 + gg
                S.op("dve", lambda e, b=b, gg=gg, g=g, u=u, y=y: e.scalar_tensor_tensor(
                    out=y[:, g * 128:(g + 1) * 128], in0=cx.ps[b][:, gg * 128:(gg + 1) * 128], scalar=bs[:, g:g + 1], in1=u[:, g * 128:(g + 1) * 128],
                    op0=ALU.add, op1=ALU.mult), reads=[cx.Bps[b], Bbs, Bu], writes=[By])
        S.op("act", lambda e, u=u, y=y: e.activation(out=u, in_=y, func=AF.Square), reads=[By], writes=[Bu])
        S.op("dve", lambda e, u=u, st=st: e.tensor_reduce(out=st[:, 8:16], in_=u.rearrange("p (g c) -> p g c", g=NG), axis=AX.X, op=ALU.add),
             reads=[Bu], writes=[Bst])
        S.op("act", lambda e, st=st: e.activation(out=st[:, 8:16], in_=st[:, 8:16], func=AF.Sqrt, bias=cx.epsb[:, 0:1], scale=1.0 / 128),
             reads=[Bst, cx.Beps], writes=[Bst])
        S.op("dve", lambda e, st=st: e.reciprocal(out=st[:, 8:16], in_=st[:, 8:16]), reads=[Bst], writes=[Bst])
        S.op("dve", lambda e, y=y, st=st: e.tensor_tensor(out=y.rearrange("p (g c) -> p g c", g=NG), in0=y.rearrange("p (g c) -> p g c", g=NG),
                                                           in1=st[:, 8:16].unsqueeze(2).to_broadcast([128, NG, 128]), op=ALU.mult),
             reads=[By, Bst], writes=[By])
        S.op("dve", lambda e, y=y: e.tensor_tensor(out=y, in0=y, in1=vec[:, 2, :], op=ALU.mult), reads=[By, Bvec], writes=[By])
        o, Bo = Ro.next()
        for g in range(NG):
            b = g % 6
            S.op("pe", lambda e, b=b, g=g, y=y: e.transpose(out=cx.ps[b][:, 0:128], in_=y[:, g * 128:(g + 1) * 128], identity=ident),
                 reads=[By, Bconst], writes=[cx.Bps[b]])
            if g % 2 == 0:
                S.op("act", lambda e, b=b, g=g, o=o: e.copy(out=o[:, g, :], in_=cx.ps[b][:, 0:128]), reads=[cx.Bps[b]], writes=[Bo])
            else:
                S.op("dve", lambda e, b=b, g=g, o=o: e.tensor_copy(out=o[:, g, :], in_=cx.ps[b][:, 0:128]), reads=[cx.Bps[b]], writes=[Bo])
        S.dma("sp", lambda e, o=o, rows=rows: e.dma_start(out=sg_out.ap.rearrange("(g c) t -> c g t", c=128)[:, :, rows], in_=o),
              reads=[Bo], writes=[sg_out.b(ch)], owner=Bo, is_out=sg_out.is_out)
    cx.release(m0)


def build_sg_test(T=TOK):
    nc = bass.Bass("TRN2", target_bir_lowering=False)
    suv = DT(nc, "suv", [T, 2 * SG_W], F32, "ExternalInput")
    sg_wT = DT(nc, "sg_wT", [SG_W // 128, 128, 128], F32, "ExternalInput")
    sg_bs = DT(nc, "sg_bs", [128, SG_W // 128], F32, "ExternalInput")
    sg_vec = DT(nc, "sg_vec", [3, SG_W], F32, "ExternalInput")
    consts = DT(nc, "consts", [128, NCONST, 128], F32, "ExternalInput")
    sg_out = DT(nc, "sg_out", [SG_W, T], F32, "ExternalOutput")
    with contextlib.ExitStack() as st:
        cx = Ctx(nc, st)
        consts_sb, Bconst = p2_common(cx, consts)
        emit_sg(cx, suv, sg_wT, sg_bs, sg_vec, sg_out, consts_sb, Bconst, T=T)
        cx.S.finish("sp")
        cx.S.emit()
    return nc


def build_p2(T=TOK, L=SEQ, NU=3):
    nc = bass.Bass("TRN2", target_bir_lowering=False)
    dn_in = DT(nc, "dn_in", [NU, 4, 128, L], F32, "ExternalInput")
    dn_ab = DT(nc, "dn_ab", [NU, 128, 2, L // 128], F32, "ExternalInput")
    dn_cw = DT(nc, "dn_cw", [NU, 128, 3, 4], F32, "ExternalInput")
    dn_hp = DT(nc, "dn_hp", [NU, 128, 2], F32, "ExternalInput")
    dn_nw = DT(nc, "dn_nw", [128, 1], F32, "ExternalInput")
    lru_in = DT(nc, "lru_in", [NU, 2, 128, L], F32, "ExternalInput")
    lru_pv = DT(nc, "lru_pv", [NU, 128, 9], F32, "ExternalInput")
    lru_w = DT(nc, "lru_w", [NU, 2, 128, 128], F32, "ExternalInput")
    suv = DT(nc, "suv", [T, 2 * SG_W], F32, "ExternalInput")
    sg_wT = DT(nc, "sg_wT", [SG_W // 128, 128, 128], F32, "ExternalInput")
    sg_bs = DT(nc, "sg_bs", [128, SG_W // 128], F32, "ExternalInput")
    sg_vec = DT(nc, "sg_vec", [3, SG_W], F32, "ExternalInput")
    consts = DT(nc, "consts", [128, NCONST, 128], F32, "ExternalInput")
    dn_out = DT(nc, "dn_out", [NU, 128, L], F32, "ExternalOutput")
    lru_out = DT(nc, "lru_out", [NU, 128, L], F32, "ExternalOutput")
    sg_out = DT(nc, "sg_out", [SG_W, T], F32, "ExternalOutput")
    with contextlib.ExitStack() as st:
        cx = Ctx(nc, st)
        consts_sb, Bconst = p2_common(cx, consts)
        emit_lru(cx, lru_in, lru_pv, lru_w, lru_out, consts_sb, Bconst, NU=NU, L=L)
        emit_sg(cx, suv, sg_wT, sg_bs, sg_vec, sg_out, consts_sb, Bconst, T=T)
        emit_dn(cx, dn_in, dn_ab, dn_cw, dn_hp, dn_nw, dn_out, consts_sb, Bconst, NU=NU, L=L)
        cx.S.finish("sp")
        cx.S.emit()
    return nc


_PROGS = {}


def _prog(name, fn):
    if name not in _PROGS:
        _PROGS[name] = fn()
    return _PROGS[name]


def _lay(v, n):
    return np.ascontiguousarray(np.asarray(v, np.float32).reshape(n, 128).T)


def _run(nc, in_maps):
    res = run_bass_kernel_spmd(nc, in_maps, core_ids=list(range(NCORES)))
    return res.results


def kernel_unfused(x, norm_mix, w_in, dn_conv_w, dn_a_log, dn_dt_bias, dn_norm_w,
           lru_conv_w, lru_conv_b, lru_w_a, lru_b_a, lru_w_x, lru_b_x, lru_lambda, lru_norm_w,
           sg_ln_w, sg_ln_b, sg_w_s, sg_b_s, sg_norm_w, w_out,
           norm_ffn, w_up, ffn_conv_w, ffn_conv_b, w_down, norm_final):
    f32 = np.float32
    x = np.asarray(x, f32)
    NSEGC = SEQ // TOK
    consts = make_consts()
    xT = np.ascontiguousarray(x.transpose(0, 2, 1))
    p1 = _prog("p1", build_p1)
    p2 = _prog("p2", build_p2)
    for l in range(DEPTH):
        wl = np.asarray(w_in[l], f32)
        g1 = _lay(norm_mix[l], D_MODEL // 128)
        maps = []
        for c in range(NCORES):
            b, j = divmod(c, NSEGC)
            maps.append({"xT": np.ascontiguousarray(xT[b][:, j * TOK:(j + 1) * TOK]), "gain": g1, "w": wl})
        r1 = _run(p1, maps)
        projT = [np.concatenate([r1[b * NSEGC + j]["projT"] for j in range(NSEGC)], axis=1) for b in range(BATCH)]
        maps = []
        cwl = np.asarray(dn_conv_w[l], f32)
        for c in range(NCORES):
            b, q4 = divmod(c, NSEGC)
            pj = projT[b]
            heads = [3 * q4 + u for u in range(3)]
            dn_in = np.stack([np.stack([pj[part * DN_W + h * 128: part * DN_W + (h + 1) * 128] for part in range(4)]) for h in heads])
            ab = np.stack([np.stack([pj[6156 + h].reshape(SEQ // 128, 128).T, pj[6144 + h].reshape(SEQ // 128, 128).T], axis=1) for h in heads])
            dcw = np.stack([np.stack([cwl[:, part * DN_W + h * 128: part * DN_W + (h + 1) * 128].T for part in range(3)], axis=1) for h in heads])
            hp = np.stack([np.stack([np.full(128, dn_a_log[l][h], f32), np.full(128, dn_dt_bias[l][h], f32)], axis=1) for h in heads])
            lru_in = np.stack([np.stack([pj[6168 + g * 128: 6168 + (g + 1) * 128], pj[7704 + g * 128: 7704 + (g + 1) * 128]]) for g in heads])
            pv = np.stack([np.concatenate([np.asarray(lru_conv_w[l], f32)[:, g * 128:(g + 1) * 128].T] + [
                np.asarray(v[l], f32)[g * 128:(g + 1) * 128, None] for v in (lru_conv_b, lru_b_a, lru_b_x, lru_lambda, lru_norm_w)], axis=1) for g in heads])
            lw = np.stack([np.stack([np.asarray(lru_w_a[l][g], f32), np.asarray(lru_w_x[l][g], f32)]) for g in heads])
            own = r1[c]["projT"]
            suv = np.ascontiguousarray(own[9240:11288].T)
            maps.append({
                "dn_in": np.ascontiguousarray(dn_in, f32), "dn_ab": np.ascontiguousarray(ab, f32), "dn_cw": np.ascontiguousarray(dcw, f32),
                "dn_hp": np.ascontiguousarray(hp, f32), "dn_nw": np.ascontiguousarray(np.asarray(dn_norm_w[l], f32)[:, None]),
                "lru_in": np.ascontiguousarray(lru_in, f32), "lru_pv": np.ascontiguousarray(pv, f32), "lru_w": np.ascontiguousarray(lw, f32),
                "suv": suv, "sg_wT": np.ascontiguousarray(np.asarray(sg_w_s[l], f32).transpose(0, 2, 1)),
                "sg_bs": np.ascontiguousarray(np.asarray(sg_b_s[l], f32).T),
                "sg_vec": np.ascontiguousarray(np.stack([np.asarray(v[l], f32) for v in (sg_ln_w, sg_ln_b, sg_norm_w)])),
                "consts": consts})
        r2 = _run(p2, maps)
        mixT = []
        for b in range(BATCH):
            m = np.zeros((D_MODEL, SEQ), f32)
            for q4 in range(NSEGC):
                c = b * NSEGC + q4
                for u in range(3):
                    h = 3 * q4 + u
                    m[h * 128:(h + 1) * 128] = r2[c]["dn_out"][u]
                    m[DN_W + h * 128: DN_W + (h + 1) * 128] = r2[c]["lru_out"][u]
                m[DN_W + LRU_W:, q4 * TOK:(q4 + 1) * TOK] = r2[c]["sg_out"]
            mixT.append(m)
        final = (l == DEPTH - 1)
        p3 = _prog("p3f" if final else "p3", lambda: build_p3(final=final))
        g2 = _lay(norm_ffn[l], D_MODEL // 128)
        cwf = np.ascontiguousarray(np.asarray(ffn_conv_w[l], f32).reshape(3, 2 * D_FF // 128, 128).transpose(2, 1, 0))
        cbf = _lay(ffn_conv_b[l], 2 * D_FF // 128)
        maps = []
        for c in range(NCORES):
            b, j = divmod(c, NSEGC)
            xh = np.zeros((D_MODEL, TOK + 2), f32)
            mh = np.zeros((D_MODEL, TOK + 2), f32)
            lo = j * TOK - 2
            if j == 0:
                xh[:, 2:] = xT[b][:, 0:TOK]
                mh[:, 2:] = mixT[b][:, 0:TOK]
            else:
                xh[:] = xT[b][:, lo:lo + TOK + 2]
                mh[:] = mixT[b][:, lo:lo + TOK + 2]
            mp = {"xT": xh, "mixT": mh, "w_out": np.asarray(w_out[l], f32), "gain2": g2, "w_up": np.asarray(w_up[l], f32),
                  "cw": cwf, "cb": cbf, "w_down": np.asarray(w_down[l], f32)}
            if final:
                mp["gainf"] = _lay(norm_final, D_MODEL // 128)
            maps.append(mp)
        r3 = _run(p3, maps)
        xT = np.stack([np.concatenate([r3[b * NSEGC + j]["outT"] for j in range(NSEGC)], axis=1) for b in range(BATCH)])
    return np.ascontiguousarray(xT.transpose(0, 2, 1)).astype(f32)


def build_fused(L=SEQ, depth=DEPTH):
    nc = bass.Bass("TRN2", target_bir_lowering=False)
    KC, KD = D_MODEL // 128, D_FF // 128
    NT = L // TOK
    I = "ExternalInput"
    xin = DT(nc, "xin", [D_MODEL, L + 2], F32, I)
    gain1 = DT(nc, "gain1", [depth, 128, KC], F32, I)
    w_in = DT(nc, "w_in", [depth, D_MODEL, D_IN], F32, I)
    dn_cw = DT(nc, "dn_cw", [depth, DN_H, 128, 3, 4], F32, I)
    dn_hp = DT(nc, "dn_hp", [depth, DN_H, 128, 2], F32, I)
    dn_nw = DT(nc, "dn_nw", [depth, 128, 1], F32, I)
    lru_pv = DT(nc, "lru_pv", [depth, DN_H, 128, 9], F32, I)
    lru_w = DT(nc, "lru_w", [depth, DN_H, 2, 128, 128], F32, I)
    sg_wT = DT(nc, "sg_wT", [depth, SG_W // 128, 128, 128], F32, I)
    sg_bs = DT(nc, "sg_bs", [depth, 128, SG_W // 128], F32, I)
    sg_vec = DT(nc, "sg_vec", [depth, 3, SG_W], F32, I)
    w_out = DT(nc, "w_out", [depth, D_MODEL, D_MODEL], F32, I)
    gain2 = DT(nc, "gain2", [depth, 128, KC], F32, I)
    w_up = DT(nc, "w_up", [depth, D_MODEL, 2 * D_FF], F32, I)
    cw = DT(nc, "cw", [depth, 128, 2 * KD, 3], F32, I)
    cb = DT(nc, "cb", [depth, 128, 2 * KD], F32, I)
    w_down = DT(nc, "w_down", [depth, D_FF, D_MODEL], F32, I)
    gainf = DT(nc, "gainf", [128, KC], F32, I)
    consts = DT(nc, "consts", [128, NCONST, 128], F32, I)
    out = DT(nc, "out", [D_MODEL, L], F32, "ExternalOutput")
    projT = DT(nc, "projT", [D_IN, L], F32, "Internal")
    mixT = DT(nc, "mixT", [D_MODEL, L + 2], F32, "Internal")
    xa = DT(nc, "xa", [D_MODEL, L + 2], F32, "Internal")
    xmid = DT(nc, "xmid", [D_MODEL, TOK + 2], F32, "Internal")
    actT = DT(nc, "actT", [D_FF, TOK], BF16, "Internal")
    blocks = p1_blocks()
    with contextlib.ExitStack() as st:
        cx = Ctx(nc, st)
        S = cx.S
        consts_sb, Bconst = p2_common(cx, consts)
        mz = cx.mark()
        z = cx.sb("zero", [128, KC, 2], F32)
        Bz = Buf("zero")
        S.op("dve", lambda e: e.memset(z, 0.0), writes=[Bz])
        for tdst in (mixT, xa):
            S.dma("sp", lambda e, tdst=tdst: e.dma_start(out=tdst.ap[:, 0:2].rearrange("(k p) c -> p k c", p=128), in_=z),
                  reads=[Bz], writes=[tdst.buf])
        cx.release(mz)
        X = xin
        for l in range(depth):
            final = (l == depth - 1)
            for t in range(NT):
                emit_p1(cx, V(X.ap[:, 2 + t * TOK:2 + (t + 1) * TOK]), V(gain1.ap[l]), V(w_in.ap[l]),
                        V(projT.ap[:, t * TOK:(t + 1) * TOK]), KC, TOK, blocks, tag="p1_%d_%d" % (l, t))
            S.barrier()
            pj = projT.ap
            lru_in = V(None, get=lambda u, w: pj[6168 + w * LRU_W + u * 128: 6168 + w * LRU_W + (u + 1) * 128, :])
            lru_out = V(None, get=lambda u: mixT.ap[DN_W + u * 128: DN_W + (u + 1) * 128, 2:2 + L])
            emit_lru(cx, lru_in, V(lru_pv.ap[l]), V(lru_w.ap[l]), lru_out, consts_sb, Bconst, NU=DN_H, L=L, tag="lru%d" % l)
            emit_sg(cx, None, V(sg_wT.ap[l]), V(sg_bs.ap[l]), V(sg_vec.ap[l]), V(mixT.ap[DN_W + LRU_W:, 2:2 + L]),
                    consts_sb, Bconst, T=L, tag="sg%d" % l, fm_src=V(pj[9240:11288, :]))
            dn_in = V(None, get=lambda u, part: pj[part * DN_W + u * 128: part * DN_W + (u + 1) * 128, :])
            dn_ab = V(None, get_row=lambda u, ri: pj[(6156 if ri == 0 else 6144) + u, :])
            dn_out = V(None, get=lambda u: mixT.ap[u * 128:(u + 1) * 128, 2:2 + L])
            emit_dn(cx, dn_in, dn_ab, V(dn_cw.ap[l]), V(dn_hp.ap[l]), V(dn_nw.ap[l]), dn_out, consts_sb, Bconst, NU=DN_H, L=L, tag="dn%d" % l)
            S.barrier()
            for t in range(NT):
                if final:
                    o = V(out.ap[:, t * TOK:(t + 1) * TOK], is_out=True)
                else:
                    o = V(xa.ap[:, 2 + t * TOK:2 + (t + 1) * TOK])
                emit_p3(cx, V(X.ap[:, t * TOK:t * TOK + TOK + 2]), V(mixT.ap[:, t * TOK:t * TOK + TOK + 2]), V(w_out.ap[l]), V(gain2.ap[l]),
                        V(w_up.ap[l]), V(cw.ap[l]), V(cb.ap[l]), V(w_down.ap[l]), o, xmid, actT, TOK,
                        gainf=(gainf if final else None), tag="p3_%d_%d" % (l, t))
                S.barrier()
            X = xa
        S.finish("sp")
        S.emit()
    return nc


def kernel_fused(x, norm_mix, w_in, dn_conv_w, dn_a_log, dn_dt_bias, dn_norm_w,
                 lru_conv_w, lru_conv_b, lru_w_a, lru_b_a, lru_w_x, lru_b_x, lru_lambda, lru_norm_w,
                 sg_ln_w, sg_ln_b, sg_w_s, sg_b_s, sg_norm_w, w_out,
                 norm_ffn, w_up, ffn_conv_w, ffn_conv_b, w_down, norm_final):
    f32 = np.float32
    A = lambda v: np.ascontiguousarray(np.asarray(v, f32))
    x = np.asarray(x, f32)
    KC, KD = D_MODEL // 128, D_FF // 128
    nc = _prog("fused", build_fused)
    shared = {
        "gain1": A(np.stack([_lay(norm_mix[l], KC) for l in range(DEPTH)])),
        "w_in": A(w_in),
        "dn_cw": A(np.asarray(dn_conv_w, f32).reshape(DEPTH, 4, 3, DN_H, 128).transpose(0, 3, 4, 2, 1)),
        "dn_hp": A(np.stack([np.broadcast_to(np.asarray(dn_a_log, f32)[:, :, None], (DEPTH, DN_H, 128)),
                             np.broadcast_to(np.asarray(dn_dt_bias, f32)[:, :, None], (DEPTH, DN_H, 128))], axis=-1)),
        "dn_nw": A(np.asarray(dn_norm_w, f32)[:, :, None]),
        "lru_pv": A(np.concatenate([np.asarray(lru_conv_w, f32).reshape(DEPTH, 4, DN_H, 128).transpose(0, 2, 3, 1)] + [
            np.asarray(v, f32).reshape(DEPTH, DN_H, 128, 1) for v in (lru_conv_b, lru_b_a, lru_b_x, lru_lambda, lru_norm_w)], axis=-1)),
        "lru_w": A(np.stack([np.asarray(lru_w_a, f32), np.asarray(lru_w_x, f32)], axis=2)),
        "sg_wT": A(np.asarray(sg_w_s, f32).transpose(0, 1, 3, 2)),
        "sg_bs": A(np.asarray(sg_b_s, f32).transpose(0, 2, 1)),
        "sg_vec": A(np.stack([np.asarray(v, f32) for v in (sg_ln_w, sg_ln_b, sg_norm_w)], axis=1)),
        "w_out": A(w_out),
        "gain2": A(np.stack([_lay(norm_ffn[l], KC) for l in range(DEPTH)])),
        "w_up": A(w_up),
        "cw": A(np.asarray(ffn_conv_w, f32).reshape(DEPTH, 3, 2 * KD, 128).transpose(0, 3, 2, 1)),
        "cb": A(np.stack([_lay(ffn_conv_b[l], 2 * KD) for l in range(DEPTH)])),
        "w_down": A(w_down),
        "gainf": _lay(norm_final, KC),
        "consts": make_consts(),
    }
    xpad = []
    for b in range(BATCH):
        xp = np.zeros((D_MODEL, SEQ + 2), f32)
        xp[:, 2:] = x[b].T
        xpad.append(xp)
    maps = []
    for c in range(NCORES):
        m = dict(shared)
        m["xin"] = xpad[c % BATCH]
        maps.append(m)
    res = run_bass_kernel_spmd(nc, maps, core_ids=list(range(NCORES))).results
    return np.ascontiguousarray(np.stack([res[b]["out"].T for b in range(BATCH)])).astype(f32)


kernel = kernel_fused
```

```python
import contextlib
import numpy as np
import concourse.bass as bass
import concourse.mybir as mybir
from concourse.bass_utils import run_bass_kernel_spmd

F32 = mybir.dt.float32
BF16 = mybir.dt.bfloat16
AF = mybir.ActivationFunctionType
ALU = mybir.AluOpType
AX = mybir.AxisListType

D_MODEL = 4096
SEQ = 4096
BATCH = 2
DEPTH = 2
HD = 128
DN_W = 1536
LRU_W = 1536
SG_W = 1024
DN_H = 12
D_IN = 11288
D_FF = 11008
EPS = 1e-6
NCORES = 8
TOK = 1024


class Buf:
    __slots__ = ("name", "last_w", "reads", "dsem", "dcnt")

    def __init__(self, name=""):
        self.name = name
        self.last_w = None
        self.reads = {}
        self.dsem = None
        self.dcnt = 0


class Sched:
    ENGS = ("pe", "act", "dve", "pool", "sp")

    def __init__(self, nc, stack):
        self.nc = nc
        self.stack = stack
        self.sem = {}
        self.count = {}
        self.prog = {e: [] for e in self.ENGS}
        self.known = {e: {} for e in self.ENGS}
        for e in self.ENGS:
            self.sem[e] = stack.enter_context(nc.semaphore("s_" + e))
            self.count[e] = 0
        self.ndsem = 0
        self.dpool = {e: [] for e in self.ENGS}
        self.dq = {}
        self.dcount = {}
        self.downers = []
        self.out_events = []
        self.ninstr = 0

    def _need(self, eng, ev, waits):
        if ev is None:
            return
        k, v = ev
        if eng == "pe" and k == "pe":
            return
        if self.known[eng].get(k, 0) >= v:
            return
        if waits.get(k, 0) < v:
            waits[k] = v

    def _waits(self, eng, reads, writes):
        waits = {}
        for b in reads:
            self._need(eng, b.last_w, waits)
        for b in writes:
            self._need(eng, b.last_w, waits)
            for k, v in b.reads.items():
                self._need(eng, (k, v), waits)
        for k, v in waits.items():
            self.known[eng][k] = v
        return waits

    def _commit(self, ev, reads, writes):
        for b in writes:
            b.last_w = ev
            b.reads = {}
        k, v = ev
        for b in reads:
            if b.reads.get(k, 0) < v:
                b.reads[k] = v

    def op(self, eng, fn, reads=(), writes=(), inc=True):
        waits = self._waits(eng, reads, writes)
        self.ninstr += 1
        if inc:
            self.count[eng] += 1
            ev = (eng, self.count[eng])
            self.prog[eng].append((list(waits.items()), fn, (eng, 1)))
        else:
            ev = (eng, self.count[eng] + 1)
            self.prog[eng].append((list(waits.items()), fn, None))
        self._commit(ev, reads, writes)
        return ev

    def dma(self, q, fn, reads=(), writes=(), owner=None, is_out=False):
        if owner is None:
            owner = writes[0] if writes else reads[0]
        if owner.dsem is None:
            if self.dpool[q]:
                key = self.dpool[q].pop()
            else:
                key = "d%d" % self.ndsem
                self.ndsem += 1
                self.sem[key] = self.stack.enter_context(self.nc.semaphore("s_" + key))
                self.dcount[key] = 0
            owner.dsem = key
            self.dq[key] = q
            self.downers.append(owner)
        key = owner.dsem
        assert self.dq[key] == q, "a DMA semaphore is bound to one DMA queue type"
        waits = self._waits(q, reads, writes)
        if self.dcount[key] > 0:
            w2 = {}
            self._need(q, (key, 16 * self.dcount[key]), w2)
            for k, v in w2.items():
                if waits.get(k, 0) < v:
                    waits[k] = v
                self.known[q][k] = max(self.known[q].get(k, 0), v)
        self.dcount[key] += 1
        self.ninstr += 1
        ev = (key, 16 * self.dcount[key])
        self.prog[q].append((list(waits.items()), fn, (key, 16)))
        self._commit(ev, reads, writes)
        if is_out:
            self.out_events.append(ev)
        return ev

    def barrier(self):
        evs = [(e, self.count[e]) for e in self.ENGS if self.count[e] > 0]
        evs += [(k, 16 * c) for k, c in self.dcount.items() if c > 0]
        for eng in self.ENGS:
            waits = {}
            for ev in evs:
                self._need(eng, ev, waits)
            for k, v in waits.items():
                self.known[eng][k] = v
            if waits:
                self.prog[eng].append((list(waits.items()), None, None))
        for b in self.downers:
            self.dpool[self.dq[b.dsem]].append(b.dsem)
            b.dsem = None
        self.downers = []

    def finish(self, eng="sp"):
        waits = {}
        for ev in self.out_events:
            self._need(eng, ev, waits)
        self.prog[eng].append((list(waits.items()), None, None))

    def emit(self):
        nc = self.nc
        sems = self.sem

        def run(engobj, items):
            for waits, fn, inc in items:
                for k, v in waits:
                    engobj.wait_ge(sems[k], v)
                if fn is not None:
                    ins = fn(engobj)
                    if inc is not None:
                        ins.then_inc(sems[inc[0]], inc[1])

        with nc.Block() as block:
            @block.sync
            def _(e):
                run(e, self.prog["sp"])

            @block.tensor
            def _(e):
                run(e, self.prog["pe"])

            @block.scalar
            def _(e):
                run(e, self.prog["act"])

            @block.vector
            def _(e):
                run(e, self.prog["dve"])

            @block.gpsimd
            def _(e):
                run(e, self.prog["pool"])


class Ctx:
    def __init__(self, nc, stack):
        self.nc = nc
        self.st = stack
        self.S = Sched(nc, stack)
        self.ps = []
        self.Bps = []
        for i in range(8):
            self.ps.append(stack.enter_context(nc.psum_tensor("psb%d" % i, [128, 512], F32)))
            self.Bps.append(Buf("ps%d" % i))
        self.bank = 0
        self.ARENA = 206 * 1024
        self.arena = stack.enter_context(nc.sbuf_tensor("arena", [128, self.ARENA // 4], F32))
        self.off = 0

    def sb(self, name, shape, dt):
        esz = 2 if dt == BF16 else 4
        n = 1
        for d in shape[1:]:
            n *= d
        nbytes = (n * esz + 31) // 32 * 32
        assert self.off + nbytes <= self.ARENA, ("SBUF arena overflow", name, self.off, nbytes)
        v = self.arena[0:shape[0], self.off // 4:(self.off + nbytes) // 4]
        self.off += nbytes
        if dt != F32:
            v = v.bitcast(dt)
        v = v[:, 0:n]
        if len(shape) == 3:
            v = v.rearrange("p (a b) -> p a b", a=shape[1])
        elif len(shape) == 4:
            v = v.rearrange("p (a b c) -> p a b c", a=shape[1], b=shape[2])
        return v

    def mark(self):
        return self.off

    def release(self, mark):
        self.S.barrier()
        self.off = mark

    def next_bank(self, lo=0, hi=8):
        b = lo + self.bank % (hi - lo)
        self.bank += 1
        return b


def p1_blocks():
    blocks = []
    c = 0
    while c < 6144:
        blocks.append((c, [(c, 128), (c + 128, 128)]))
        c += 256
    blocks.append((6144, [(6144, 24), (6168, 128), (6296, 128)]))
    c = 6424
    while c < D_IN:
        blocks.append((c, [(c, 128), (c + 128, 128)]))
        c += 256
    assert c == D_IN
    return blocks


def emit_p1(cx, xT, gain, w, projT, KC, T, blocks, tag="p1"):
    nc, S = cx.nc, cx.S
    m_p1 = cx.mark()
    NT = T // 512
    G = 4
    NG = KC // G
    dmodel = KC * 128
    xv = xT.ap.rearrange("(k p) t -> p k t", p=128)
    wv = w.ap.rearrange("(k p) c -> p k c", p=128)

    gain_sb = cx.sb(tag + "gain", [128, KC], F32)
    Bgain = Buf("gain")
    S.dma("sp", lambda e: e.dma_start(out=gain_sb[:], in_=gain.ap), reads=[gain.buf], writes=[Bgain])
    ones = cx.sb(tag + "ones", [128, 128], BF16)
    Bones = Buf("ones")
    S.op("dve", lambda e: e.memset(ones[:], 1.0), writes=[Bones])
    epsb = cx.sb(tag + "eps", [128, 1], F32)
    Beps = Buf("eps")
    S.op("dve", lambda e: e.memset(epsb[:], EPS), writes=[Beps])

    hT = cx.sb(tag + "hT", [128, KC, T], BF16)
    BhT = [[Buf("hT%d_%d" % (tt, g)) for g in range(NG)] for tt in range(NT)]
    rstd = cx.sb(tag + "rstd", [128, T], F32)
    Brstd = [Buf("rstd%d" % tt) for tt in range(NT)]
    xst = [cx.sb(tag + "xst%d" % i, [128, G, 512], F32) for i in range(2)]
    Bxst = [Buf("xst%d" % i) for i in range(2)]
    sq = [cx.sb(tag + "sq%d" % i, [128, G, 512], BF16) for i in range(2)]
    Bsq = [Buf("sq%d" % i) for i in range(2)]

    it = 0
    for tt in range(NT):
        tsl = slice(tt * 512, (tt + 1) * 512)
        bss = 7
        for g in range(NG):
            s = it % 2
            it += 1
            ksl = slice(g * G, (g + 1) * G)
            S.dma("sp", lambda e, s=s, ksl=ksl, tsl=tsl: e.dma_start(out=xst[s][:], in_=xv[:, ksl, tsl]),
                  reads=[xT.buf], writes=[Bxst[s]])
            S.op("act", lambda e, s=s: e.activation(out=sq[s][:], in_=xst[s][:], func=AF.Square),
                 reads=[Bxst[s]], writes=[Bsq[s]])
            S.op("dve", lambda e, s=s, ksl=ksl, tsl=tsl: e.tensor_tensor(
                out=hT[:, ksl, tsl], in0=xst[s][:],
                in1=gain_sb[:, ksl].unsqueeze(2).to_broadcast([128, G, 512]), op=ALU.mult),
                reads=[Bxst[s], Bgain], writes=[BhT[tt][g]])
            for i in range(G):
                first = (g == 0 and i == 0)
                last = (g == NG - 1 and i == G - 1)
                S.op("pe", lambda e, s=s, i=i, first=first, last=last: e.matmul(
                    cx.ps[bss][:], lhsT=ones[:], rhs=sq[s][:, i, :], start=first, stop=last),
                    reads=[Bones, Bsq[s]], writes=[cx.Bps[bss]], inc=(i == G - 1))
        S.op("act", lambda e, tsl=tsl: e.activation(out=rstd[:, tsl], in_=cx.ps[bss][:], func=AF.Sqrt,
                                                      bias=epsb[:, 0:1], scale=1.0 / dmodel),
             reads=[cx.Bps[bss], Beps], writes=[Brstd[tt]])
        S.op("dve", lambda e, tsl=tsl: e.reciprocal(out=rstd[:, tsl], in_=rstd[:, tsl]),
             reads=[Brstd[tt]], writes=[Brstd[tt]])

    WMAX = max(sum(wd for _, wd in chunks) for _, chunks in blocks)
    NSLOT = 3
    wbuf = [cx.sb(tag + "w%d" % i, [128, KC, WMAX], BF16) for i in range(NSLOT)]
    Bw = [Buf("w%d" % i) for i in range(NSLOT)]
    NOST = 3
    ost = [cx.sb(tag + "ost%d" % i, [128, T], F32) for i in range(NOST)]
    Bost = [Buf("ost%d" % i) for i in range(NOST)]
    oi = 0
    for bi, (c0, chunks) in enumerate(blocks):
        s = bi % NSLOT
        wblk = sum(wd for _, wd in chunks)
        S.dma("pool", lambda e, s=s, c0=c0, wblk=wblk: e.dma_start(out=wbuf[s][:, :, 0:wblk], in_=wv[:, :, c0:c0 + wblk]),
              reads=[w.buf], writes=[Bw[s]])
        for (cs, cw) in chunks:
            o = oi % NOST
            oi += 1
            for tt in range(NT):
                tsl = slice(tt * 512, (tt + 1) * 512)
                b = cx.next_bank(0, 7)
                for kc in range(KC):
                    S.op("pe", lambda e, b=b, s=s, kc=kc, cs=cs, cw=cw, c0=c0, tsl=tsl: e.matmul(
                        cx.ps[b][0:cw, :], lhsT=wbuf[s][:, kc, cs - c0:cs - c0 + cw], rhs=hT[:, kc, tsl],
                        start=(kc == 0), stop=(kc == KC - 1)),
                        reads=[Bw[s], BhT[tt][kc // G]], writes=[cx.Bps[b]], inc=(kc == KC - 1))
                S.op("dve", lambda e, b=b, o=o, cw=cw, tsl=tsl: e.tensor_tensor(
                    out=ost[o][0:cw, tsl], in0=cx.ps[b][0:cw, :], in1=rstd[0:cw, tsl], op=ALU.mult),
                    reads=[cx.Bps[b], Brstd[tt]], writes=[Bost[o]])
            S.dma("sp", lambda e, o=o, cs=cs, cw=cw: e.dma_start(out=projT.ap[cs:cs + cw, :], in_=ost[o][0:cw, :]),
                  reads=[Bost[o]], writes=[projT.b(cs)], owner=Bost[o], is_out=projT.is_out)
    cx.release(m_p1)


class DT:
    def __init__(self, nc, name, shape, dt, kind):
        self.t = nc.dram_tensor(name, list(shape), dt, kind=kind)
        self.ap = self.t.ap()
        self.buf = Buf(name)
        self.is_out = (kind == "ExternalOutput")
        self.name = name
        self._b = {}

    def b(self, key):
        if key not in self._b:
            self._b[key] = Buf("%s_%s" % (self.name, key))
        return self._b[key]

    def get(self, *idx):
        a = self.ap
        for i in idx:
            a = a[i]
        return a


class V:
    def __init__(self, ap, is_out=False, get=None, get_row=None, name="v"):
        self.ap = ap
        self.buf = Buf(name)
        self.is_out = is_out
        self.name = name
        self._b = {}
        if get is not None:
            self.get = get
        if get_row is not None:
            self.get_row = get_row

    def b(self, key):
        if key not in self._b:
            self._b[key] = Buf("%s_%s" % (self.name, key))
        return self._b[key]

    def get(self, *idx):
        a = self.ap
        for i in idx:
            a = a[i]
        return a


def build_p1(KC=32, T=TOK, blocks=None, ncol=D_IN):
    if blocks is None:
        blocks = p1_blocks()
    nc = bass.Bass("TRN2", target_bir_lowering=False)
    xT = DT(nc, "xT", [KC * 128, T], F32, "ExternalInput")
    gain = DT(nc, "gain", [128, KC], F32, "ExternalInput")
    w = DT(nc, "w", [KC * 128, ncol], F32, "ExternalInput")
    projT = DT(nc, "projT", [ncol, T], F32, "ExternalOutput")
    with contextlib.ExitStack() as st:
        cx = Ctx(nc, st)
        emit_p1(cx, xT, gain, w, projT, KC, T, blocks)
        cx.S.finish("sp")
        cx.S.emit()
    return nc


def emit_p3(cx, xT, mixT, w_out, gain2, w_up, cw, cb, w_down, outT, xmid, actT, T, gainf=None, tag="p3",
            KC=32, KD=86):
    nc, S = cx.nc, cx.S
    TH = T + 2
    TW = TH // 3
    assert TW * 3 == TH and TW <= 512
    dmodel = KC * 128
    dff = KD * 128
    m_all = cx.mark()

    gain_sb = cx.sb(tag + "gain", [128, KC], F32)
    Bgain = Buf("gain2")
    S.dma("sp", lambda e: e.dma_start(out=gain_sb, in_=gain2.ap), reads=[gain2.buf], writes=[Bgain])
    ones = cx.sb(tag + "ones", [128, 128], BF16)
    Bones = Buf("ones")
    S.op("dve", lambda e: e.memset(ones, 1.0), writes=[Bones])
    epsb = cx.sb(tag + "eps", [128, 1], F32)
    Beps = Buf("eps")
    S.op("dve", lambda e: e.memset(epsb, EPS), writes=[Beps])
    cwsb = cx.sb(tag + "cw", [128, 2 * KD, 3], F32)
    cbsb = cx.sb(tag + "cb", [128, 2 * KD], F32)
    Bcw = Buf("cw")
    S.dma("sp", lambda e: e.dma_start(out=cwsb, in_=cw.ap), reads=[cw.buf], writes=[Bcw])
    Bcb = Buf("cb")
    S.dma("sp", lambda e: e.dma_start(out=cbsb, in_=cb.ap), reads=[cb.buf], writes=[Bcb])

    h2T = cx.sb(tag + "h2T", [128, KC, TH], BF16)
    Bh2 = [Buf("h2_%d" % c) for c in range(KC)]
    rstd2 = cx.sb(tag + "rstd2", [128, TH], F32)
    Brstd2 = Buf("rstd2")

    m1 = cx.mark()
    mT = cx.sb(tag + "mT", [128, KC, TH], BF16)
    NMG = KC // 4
    BmT = [Buf("mT%d" % g) for g in range(NMG)]
    mv = mixT.ap.rearrange("(k p) t -> p k t", p=128)
    for g in range(NMG):
        S.dma("pool", lambda e, g=g: e.dma_start(out=mT[:, 4 * g:4 * g + 4, :], in_=mv[:, 4 * g:4 * g + 4, :]),
              reads=[mixT.buf], writes=[BmT[g]])
    xst = [cx.sb(tag + "xst%d" % i, [128, TH], F32) for i in range(2)]
    Bxst = [Buf("xst%d" % i) for i in range(2)]
    xm = [cx.sb(tag + "xm%d" % i, [128, TH], F32) for i in range(2)]
    Bxm = [Buf("xm%d" % i) for i in range(2)]
    sq = [cx.sb(tag + "sq%d" % i, [128, TH], BF16) for i in range(2)]
    Bsq = [Buf("sq%d" % i) for i in range(2)]
    NSLOT = 3
    NSLOT1 = 2
    wbuf = [cx.sb(tag + "w1_%d" % i, [128, KC, 256], BF16) for i in range(NSLOT1)]
    Bw = [Buf("w1_%d" % i) for i in range(NSLOT1)]
    wv = w_out.ap.rearrange("(k p) c -> p k c", p=128)
    for blk in range(KC // 2):
        c0 = blk * 256
        s = blk % NSLOT1
        S.dma("pool", lambda e, s=s, c0=c0: e.dma_start(out=wbuf[s], in_=wv[:, :, c0:c0 + 256]),
              reads=[w_out.buf], writes=[Bw[s]])
        for ci in range(2):
            chunk = blk * 2 + ci
            i2 = chunk % 2
            rsl = slice(chunk * 128, (chunk + 1) * 128)
            S.dma("sp", lambda e, i2=i2, rsl=rsl: e.dma_start(out=xst[i2], in_=xT.ap[rsl, :]),
                  reads=[xT.buf], writes=[Bxst[i2]])
            for tt in range(3):
                tsl = slice(tt * TW, (tt + 1) * TW)
                b = cx.next_bank(0, 5)
                for kc in range(KC):
                    S.op("pe", lambda e, b=b, s=s, kc=kc, ci=ci, tsl=tsl: e.matmul(
                        cx.ps[b][:, 0:TW], lhsT=wbuf[s][:, kc, ci * 128:(ci + 1) * 128], rhs=mT[:, kc, tsl],
                        start=(kc == 0), stop=(kc == KC - 1)),
                        reads=[Bw[s], BmT[kc // 4]], writes=[cx.Bps[b]], inc=(kc == KC - 1))
                S.op("dve", lambda e, b=b, i2=i2, tsl=tsl: e.tensor_tensor(
                    out=xm[i2][:, tsl], in0=cx.ps[b][:, 0:TW], in1=xst[i2][:, tsl], op=ALU.add),
                    reads=[cx.Bps[b], Bxst[i2]], writes=[Bxm[i2]])
            S.op("act", lambda e, i2=i2: e.activation(out=sq[i2], in_=xm[i2], func=AF.Square),
                 reads=[Bxm[i2]], writes=[Bsq[i2]])
            for tt in range(3):
                tsl = slice(tt * TW, (tt + 1) * TW)
                S.op("pe", lambda e, tt=tt, i2=i2, tsl=tsl, chunk=chunk: e.matmul(
                    cx.ps[5 + tt][:, 0:TW], lhsT=ones, rhs=sq[i2][:, tsl], start=(chunk == 0), stop=(chunk == KC - 1)),
                    reads=[Bones, Bsq[i2]], writes=[cx.Bps[5 + tt]], inc=(tt == 2))
            S.op("dve", lambda e, i2=i2, chunk=chunk: e.tensor_scalar(
                out=h2T[:, chunk, :], in0=xm[i2], scalar1=gain_sb[:, chunk:chunk + 1], scalar2=None, op0=ALU.mult),
                reads=[Bxm[i2], Bgain], writes=[Bh2[chunk]])
            S.dma("sp", lambda e, i2=i2, rsl=rsl: e.dma_start(out=xmid.ap[rsl, :], in_=xm[i2]),
                  reads=[Bxm[i2]], writes=[xmid.b(chunk)], owner=Bxm[i2])
    for tt in range(3):
        tsl = slice(tt * TW, (tt + 1) * TW)
        S.op("act", lambda e, tt=tt, tsl=tsl: e.activation(out=rstd2[:, tsl], in_=cx.ps[5 + tt][:, 0:TW], func=AF.Sqrt,
                                                             bias=epsb[:, 0:1], scale=1.0 / dmodel),
             reads=[cx.Bps[5 + tt], Beps], writes=[Brstd2])
    S.op("dve", lambda e: e.reciprocal(out=rstd2, in_=rstd2), reads=[Brstd2], writes=[Brstd2])

    cx.release(m1)
    wbuf2 = [cx.sb(tag + "w2_%d" % i, [128, KC, 256], BF16) for i in range(NSLOT)]
    Bw2 = [[Buf("w2_%d_%d" % (i, h)) for h in range(2)] for i in range(NSLOT)]
    pre = [[cx.sb(tag + "pre%d%d" % (i, h), [128, TH], F32) for h in range(2)] for i in range(2)]
    Bpre = [[Buf("pre%d%d" % (i, h)) for h in range(2)] for i in range(2)]
    hid = [[cx.sb(tag + "hid%d%d" % (i, h), [128, T], F32) for h in range(2)] for i in range(2)]
    Bhid = [[Buf("hid%d%d" % (i, h)) for h in range(2)] for i in range(2)]
    asb = [cx.sb(tag + "asb%d" % i, [128, T], BF16) for i in range(2)]
    Basb = [Buf("asb%d" % i) for i in range(2)]
    wv2 = w_up.ap.rearrange("(k p) c -> p k c", p=128)
    for j in range(KD):
        s = j % NSLOT
        i2 = j % 2
        for half in range(2):
            S.dma("pool", lambda e, s=s, j=j, half=half: e.dma_start(
                out=wbuf2[s][:, :, half * 128:(half + 1) * 128],
                in_=wv2[:, :, half * dff + j * 128:half * dff + (j + 1) * 128]),
                reads=[w_up.buf], writes=[Bw2[s][half]])
        for half in range(2):
            for tt in range(3):
                tsl = slice(tt * TW, (tt + 1) * TW)
                b = cx.next_bank(0, 8)
                for kc in range(KC):
                    S.op("pe", lambda e, b=b, s=s, kc=kc, half=half, tsl=tsl: e.matmul(
                        cx.ps[b][:, 0:TW], lhsT=wbuf2[s][:, kc, half * 128:(half + 1) * 128], rhs=h2T[:, kc, tsl],
                        start=(kc == 0), stop=(kc == KC - 1)),
                        reads=[Bw2[s][half], Bh2[kc]], writes=[cx.Bps[b]], inc=(kc == KC - 1))
                S.op("dve", lambda e, b=b, i2=i2, half=half, tsl=tsl: e.tensor_tensor(
                    out=pre[i2][half][:, tsl], in0=cx.ps[b][:, 0:TW], in1=rstd2[:, tsl], op=ALU.mult),
                    reads=[cx.Bps[b], Brstd2], writes=[Bpre[i2][half]])
            col = half * KD + j
            P_, H_ = pre[i2][half], hid[i2][half]
            S.op("dve", lambda e, P_=P_, H_=H_, col=col: e.tensor_scalar(
                out=H_, in0=P_[:, 0:T], scalar1=cwsb[:, col, 0:1], scalar2=cbsb[:, col:col + 1], op0=ALU.mult, op1=ALU.add),
                reads=[Bpre[i2][half], Bcw, Bcb], writes=[Bhid[i2][half]])
            S.op("dve", lambda e, P_=P_, H_=H_, col=col: e.scalar_tensor_tensor(
                out=H_, in0=P_[:, 1:T + 1], scalar=cwsb[:, col, 1:2], in1=H_, op0=ALU.mult, op1=ALU.add),
                reads=[Bpre[i2][half], Bcw], writes=[Bhid[i2][half]])
            S.op("dve", lambda e, P_=P_, H_=H_, col=col: e.scalar_tensor_tensor(
                out=H_, in0=P_[:, 2:T + 2], scalar=cwsb[:, col, 2:3], in1=H_, op0=ALU.mult, op1=ALU.add),
                reads=[Bpre[i2][half], Bcw], writes=[Bhid[i2][half]])
        S.op("act", lambda e, i2=i2: e.activation(out=hid[i2][0], in_=hid[i2][0], func=AF.Silu),
             reads=[Bhid[i2][0]], writes=[Bhid[i2][0]])
        S.op("dve", lambda e, i2=i2: e.tensor_tensor(out=asb[i2], in0=hid[i2][0], in1=hid[i2][1], op=ALU.mult),
             reads=[Bhid[i2][0], Bhid[i2][1]], writes=[Basb[i2]])
        S.dma("sp", lambda e, i2=i2, j=j: e.dma_start(out=actT.ap[j * 128:(j + 1) * 128, :], in_=asb[i2]),
              reads=[Basb[i2]], writes=[actT.b(j)], owner=Basb[i2])

    cx.release(m_all)
    gainf_sb = None
    if gainf is not None:
        gainf_sb = cx.sb(tag + "gainf", [128, KC], F32)
        Bgf = Buf("gainf")
        S.dma("sp", lambda e: e.dma_start(out=gainf_sb, in_=gainf.ap), reads=[gainf.buf], writes=[Bgf])
        ones = cx.sb(tag + "ones3", [128, 128], BF16)
        Bones = Buf("ones3")
        S.op("dve", lambda e: e.memset(ones, 1.0), writes=[Bones])
        epsb = cx.sb(tag + "eps3", [128, 1], F32)
        Beps = Buf("eps3")
        S.op("dve", lambda e: e.memset(epsb, EPS), writes=[Beps])
        rstdf = cx.sb(tag + "rstdf", [128, 512], F32)
        Brf = Buf("rstdf")
        sq3 = [cx.sb(tag + "sq3_%d" % i, [128, 512], BF16) for i in range(2)]
        Bsq3 = [Buf("sq3_%d" % i) for i in range(2)]
    GK = 8
    NAG = (KD + GK - 1) // GK
    aT = cx.sb(tag + "aT", [128, KD, 512], BF16)
    BaT = [Buf("aT%d" % g) for g in range(NAG)]
    wbuf3 = [cx.sb(tag + "w3_%d" % i, [128, KD, 256], BF16) for i in range(2)]
    Bw3 = [Buf("w3_%d" % i) for i in range(2)]
    xms = [cx.sb(tag + "xms%d" % i, [128, 512], F32) for i in range(2)]
    Bxms = [Buf("xms%d" % i) for i in range(2)]
    ost = [cx.sb(tag + "ost%d" % i, [128, 512], F32) for i in range(2)]
    Bost = [Buf("ost%d" % i) for i in range(2)]
    av = actT.ap.rearrange("(k p) t -> p k t", p=128)
    wv3 = w_down.ap.rearrange("(k p) c -> p k c", p=128)
    for th in range(T // 512):
        csl = slice(th * 512, (th + 1) * 512)
        for g in range(NAG):
            k0, k1 = g * GK, min(KD, (g + 1) * GK)
            S.dma("sp", lambda e, k0=k0, k1=k1, csl=csl: e.dma_start(out=aT[:, k0:k1, :], in_=av[:, k0:k1, csl]),
                  reads=[actT.b(j) for j in range(k0, k1)], writes=[BaT[g]])
        for blk in range(KC // 2):
            c0 = blk * 256
            s = (th * (KC // 2) + blk) % 2
            S.dma("pool", lambda e, s=s, c0=c0: e.dma_start(out=wbuf3[s], in_=wv3[:, :, c0:c0 + 256]),
                  reads=[w_down.buf], writes=[Bw3[s]])
            for ci in range(2):
                chunk = blk * 2 + ci
                i2 = chunk % 2
                rsl = slice(chunk * 128, (chunk + 1) * 128)
                S.dma("sp", lambda e, i2=i2, rsl=rsl, th=th: e.dma_start(
                    out=xms[i2], in_=xmid.ap[rsl, 2 + th * 512:2 + (th + 1) * 512]),
                    reads=[xmid.b(chunk)], writes=[Bxms[i2]])
                b = cx.next_bank(0, 7)
                for k in range(KD):
                    S.op("pe", lambda e, b=b, s=s, k=k, ci=ci: e.matmul(
                        cx.ps[b][:], lhsT=wbuf3[s][:, k, ci * 128:(ci + 1) * 128], rhs=aT[:, k, :],
                        start=(k == 0), stop=(k == KD - 1)),
                        reads=[Bw3[s], BaT[k // GK]], writes=[cx.Bps[b]], inc=(k == KD - 1))
                S.op("dve", lambda e, b=b, i2=i2: e.tensor_tensor(out=ost[i2], in0=cx.ps[b][:], in1=xms[i2], op=ALU.add),
                     reads=[cx.Bps[b], Bxms[i2]], writes=[Bost[i2]])
                if gainf is not None:
                    S.op("act", lambda e, i2=i2: e.activation(out=sq3[i2], in_=ost[i2], func=AF.Square),
                         reads=[Bost[i2]], writes=[Bsq3[i2]])
                    S.op("pe", lambda e, i2=i2, chunk=chunk: e.matmul(
                        cx.ps[7][:], lhsT=ones, rhs=sq3[i2], start=(chunk == 0), stop=(chunk == KC - 1)),
                        reads=[Bones, Bsq3[i2]], writes=[cx.Bps[7]])
                S.dma("sp", lambda e, i2=i2, rsl=rsl, csl=csl: e.dma_start(out=outT.ap[rsl, csl], in_=ost[i2]),
                      reads=[Bost[i2]], writes=[outT.b((chunk, th))], owner=Bost[i2], is_out=outT.is_out)
        if gainf is not None:
            S.op("act", lambda e: e.activation(out=rstdf, in_=cx.ps[7][:], func=AF.Sqrt, bias=epsb[:, 0:1], scale=1.0 / dmodel),
                 reads=[cx.Bps[7], Beps], writes=[Brf])
            S.op("dve", lambda e: e.reciprocal(out=rstdf, in_=rstdf), reads=[Brf], writes=[Brf])
            for chunk in range(KC):
                i2 = chunk % 2
                rsl = slice(chunk * 128, (chunk + 1) * 128)
                S.dma("sp", lambda e, i2=i2, rsl=rsl, csl=csl: e.dma_start(out=xms[i2], in_=outT.ap[rsl, csl]),
                      reads=[outT.b((chunk, th))], writes=[Bxms[i2]])
                S.op("dve", lambda e, i2=i2, chunk=chunk: e.scalar_tensor_tensor(
                    out=ost[i2], in0=xms[i2], scalar=gainf_sb[:, chunk:chunk + 1], in1=rstdf, op0=ALU.mult, op1=ALU.mult),
                    reads=[Bxms[i2], Bgf, Brf], writes=[Bost[i2]])
                S.dma("sp", lambda e, i2=i2, rsl=rsl, csl=csl: e.dma_start(out=outT.ap[rsl, csl], in_=ost[i2]),
                      reads=[Bost[i2]], writes=[outT.b((chunk, th))], owner=Bost[i2], is_out=outT.is_out)
    cx.release(m_all)


def build_p3(T=TOK, KC=32, KD=86, final=False):
    nc = bass.Bass("TRN2", target_bir_lowering=False)
    d, dff = KC * 128, KD * 128
    xT = DT(nc, "xT", [d, T + 2], F32, "ExternalInput")
    mixT = DT(nc, "mixT", [d, T + 2], F32, "ExternalInput")
    w_out = DT(nc, "w_out", [d, d], F32, "ExternalInput")
    gain2 = DT(nc, "gain2", [128, KC], F32, "ExternalInput")
    w_up = DT(nc, "w_up", [d, 2 * dff], F32, "ExternalInput")
    cw = DT(nc, "cw", [128, 2 * KD, 3], F32, "ExternalInput")
    cb = DT(nc, "cb", [128, 2 * KD], F32, "ExternalInput")
    w_down = DT(nc, "w_down", [dff, d], F32, "ExternalInput")
    gainf = DT(nc, "gainf", [128, KC], F32, "ExternalInput") if final else None
    outT = DT(nc, "outT", [d, T], F32, "ExternalOutput")
    xmid = DT(nc, "xmid", [d, T + 2], F32, "Internal")
    actT = DT(nc, "actT", [dff, T], BF16, "Internal")
    with contextlib.ExitStack() as st:
        cx = Ctx(nc, st)
        emit_p3(cx, xT, mixT, w_out, gain2, w_up, cw, cb, w_down, outT, xmid, actT, T, gainf=gainf, KC=KC, KD=KD)
        cx.S.finish("sp")
        cx.S.emit()
    return nc


NCONST = 9
C_ID, C_ONES, C_TRI, C_SELEND, C_SEL63, C_SEL127, C_MLS, C_MU, C_UP = range(NCONST)


def make_consts():
    c = np.zeros((128, NCONST, 128), np.float32)
    i = np.arange(128)
    blk = i // 64
    same = blk[:, None] == blk[None, :]
    c[:, C_ID, :] = np.eye(128)
    c[:, C_ONES, :] = 1.0
    c[:, C_TRI, :] = (same & (i[:, None] <= i[None, :]))
    c[:, C_SELEND, :] = (i[:, None] == (blk[None, :] * 64 + 63))
    c[:, C_SEL63, :] = (i[:, None] == 63)
    c[:, C_SEL127, :] = (i[:, None] == 127)
    c[:, C_MLS, :] = np.where(same & (i[None, :] < i[:, None]), 0.0, 1e30)
    c[:, C_MU, :] = np.where(same & (i[None, :] >= i[:, None]), 0.0, -1e30)
    c[:, C_UP, :] = (i[None, :] >= i[:, None])
    return c


def emit_lru(cx, lru_in, lru_pv, lru_w, lru_out, consts_sb, Bconst, NU=3, L=SEQ, tag="lru"):
    nc, S = cx.nc, cx.S
    m0 = cx.mark()
    NTL = L // 512
    lxp = cx.sb(tag + "lxp", [128, L + 3], F32)
    xc = cx.sb(tag + "xc", [128, L], F32)
    ra = cx.sb(tag + "ra", [128, L], F32)
    ig = cx.sb(tag + "ig", [128, L], F32)
    tmp = cx.sb(tag + "tmp", [128, L], F32)
    pv = cx.sb(tag + "pv", [128, 9], F32)
    wg = cx.sb(tag + "wg", [128, 2, 128], F32)
    sc = cx.sb(tag + "sc", [128, 4], F32)
    Blxp, Bxc, Bra, Big, Btmp, Bpv, Bwg, Bsc = [Buf(n) for n in ("lxp", "xc", "ra", "ig", "tmp", "pv", "wg", "sc")]
    ones = consts_sb[:, C_ONES, :]
    for u in range(NU):
        S.dma("sp", lambda e, u=u: e.dma_start(out=pv, in_=lru_pv.ap[u]), reads=[lru_pv.buf], writes=[Bpv])
        S.dma("sp", lambda e, u=u: e.dma_start(out=wg, in_=lru_w.ap[u].rearrange("g i j -> i g j")), reads=[lru_w.buf], writes=[Bwg])
        S.op("dve", lambda e: e.memset(lxp[:, 0:3], 0.0), writes=[Blxp])
        S.dma("sp", lambda e, u=u: e.dma_start(out=lxp[:, 3:L + 3], in_=lru_in.get(u, 0)), reads=[lru_in.buf], writes=[Blxp])
        S.op("dve", lambda e: e.tensor_scalar(out=xc, in0=lxp[:, 0:L], scalar1=pv[:, 0:1], scalar2=pv[:, 4:5], op0=ALU.mult, op1=ALU.add),
             reads=[Blxp, Bpv], writes=[Bxc])
        for j in range(1, 4):
            S.op("dve", lambda e, j=j: e.scalar_tensor_tensor(out=xc, in0=lxp[:, j:j + L], scalar=pv[:, j:j + 1], in1=xc, op0=ALU.mult, op1=ALU.add),
                 reads=[Blxp, Bpv, Bxc], writes=[Bxc])
        S.dma("sp", lambda e, u=u: e.dma_start(out=lxp[:, 0:L], in_=lru_in.get(u, 1)), reads=[lru_in.buf], writes=[Blxp])
        S.op("act", lambda e: e.activation(out=sc[:, 0:1], in_=pv[:, 7:8], func=AF.Exp, scale=-1.0), reads=[Bpv], writes=[Bsc])
        S.op("act", lambda e: e.activation(out=sc[:, 0:1], in_=sc[:, 0:1], func=AF.Ln, bias=ones[:, 0:1], scale=1.0), reads=[Bsc, Bconst], writes=[Bsc])
        S.op("dve", lambda e: e.tensor_scalar(out=sc[:, 1:2], in0=sc[:, 0:1], scalar1=-16.0, scalar2=None, op0=ALU.mult), reads=[Bsc], writes=[Bsc])
        S.op("dve", lambda e: e.tensor_scalar(out=sc[:, 0:1], in0=sc[:, 0:1], scalar1=-8.0, scalar2=None, op0=ALU.mult), reads=[Bsc], writes=[Bsc])
        for gi, (dst, Bdst, bcol) in enumerate(((ra, Bra, 5), (ig, Big, 6))):
            for tt in range(NTL):
                tsl = slice(tt * 512, (tt + 1) * 512)
                b = cx.next_bank(0, 8)
                S.op("pe", lambda e, b=b, gi=gi, tsl=tsl: e.matmul(cx.ps[b][:], lhsT=wg[:, gi, :], rhs=xc[:, tsl], start=True, stop=True),
                     reads=[Bwg, Bxc], writes=[cx.Bps[b]])
                S.op("act", lambda e, b=b, dst=dst, tsl=tsl, bcol=bcol: e.activation(
                    out=dst[:, tsl], in_=cx.ps[b][:], func=AF.Sigmoid, bias=pv[:, bcol:bcol + 1], scale=1.0),
                    reads=[cx.Bps[b], Bpv], writes=[Bdst])
        S.op("act", lambda e: e.activation(out=tmp, in_=ra, func=AF.Exp, scale=sc[:, 1:2]), reads=[Bra, Bsc], writes=[Btmp])
        S.op("act", lambda e: e.activation(out=ra, in_=ra, func=AF.Exp, scale=sc[:, 0:1]), reads=[Bra, Bsc], writes=[Bra])
        S.op("dve", lambda e: e.tensor_scalar(out=tmp, in0=tmp, scalar1=-1.0, scalar2=1.0, op0=ALU.mult, op1=ALU.add), reads=[Btmp], writes=[Btmp])
        S.op("act", lambda e: e.activation(out=tmp, in_=tmp, func=AF.Sqrt), reads=[Btmp], writes=[Btmp])
        S.op("dve", lambda e: e.tensor_tensor(out=ig, in0=ig, in1=tmp, op=ALU.mult), reads=[Big, Btmp], writes=[Big])
        S.op("dve", lambda e: e.tensor_tensor(out=ig, in0=ig, in1=xc, op=ALU.mult), reads=[Big, Bxc], writes=[Big])
        S.op("dve", lambda e: e.tensor_tensor_scan(out=tmp, data0=ra, data1=ig, initial=0.0, op0=ALU.mult, op1=ALU.add),
             reads=[Bra, Big], writes=[Btmp])
        S.op("act", lambda e: e.activation(out=lxp[:, 0:L], in_=lxp[:, 0:L], func=AF.Gelu), reads=[Blxp], writes=[Blxp])
        S.op("dve", lambda e: e.tensor_tensor(out=tmp, in0=tmp, in1=lxp[:, 0:L], op=ALU.mult), reads=[Btmp, Blxp], writes=[Btmp])
        S.op("act", lambda e: e.activation(out=xc, in_=tmp, func=AF.Square), reads=[Btmp], writes=[Bxc])
        for tt in range(NTL):
            tsl = slice(tt * 512, (tt + 1) * 512)
            b = cx.next_bank(0, 8)
            S.op("pe", lambda e, b=b, tsl=tsl: e.matmul(cx.ps[b][:], lhsT=ones, rhs=xc[:, tsl], start=True, stop=True),
                 reads=[Bconst, Bxc], writes=[cx.Bps[b]])
            S.op("act", lambda e, b=b, tsl=tsl: e.activation(out=ig[:, tsl], in_=cx.ps[b][:], func=AF.Sqrt, bias=consts_sb[:, C_ID, 0:1] if False else cx.epsb[:, 0:1], scale=1.0 / 128),
                 reads=[cx.Bps[b], cx.Beps], writes=[Big])
        S.op("dve", lambda e: e.reciprocal(out=ig, in_=ig), reads=[Big], writes=[Big])
        S.op("dve", lambda e: e.scalar_tensor_tensor(out=tmp, in0=tmp, scalar=pv[:, 8:9], in1=ig, op0=ALU.mult, op1=ALU.mult),
             reads=[Btmp, Bpv, Big], writes=[Btmp])
        S.dma("sp", lambda e, u=u: e.dma_start(out=lru_out.get(u), in_=tmp), reads=[Btmp], writes=[lru_out.b(u)], owner=Btmp, is_out=lru_out.is_out)
    cx.release(m0)


def p2_common(cx, consts):
    S = cx.S
    consts_sb = cx.sb("consts", [128, NCONST, 128], F32)
    Bconst = Buf("consts")
    S.dma("sp", lambda e: e.dma_start(out=consts_sb, in_=consts.ap), reads=[consts.buf], writes=[Bconst])
    cx.epsb = cx.sb("epsb", [128, 1], F32)
    cx.Beps = Buf("eps")
    S.op("dve", lambda e: e.memset(cx.epsb, EPS), writes=[cx.Beps])
    return consts_sb, Bconst


def build_lru_test(NU=1, L=SEQ):
    nc = bass.Bass("TRN2", target_bir_lowering=False)
    lru_in = DT(nc, "lru_in", [NU, 2, 128, L], F32, "ExternalInput")
    lru_pv = DT(nc, "lru_pv", [NU, 128, 9], F32, "ExternalInput")
    lru_w = DT(nc, "lru_w", [NU, 2, 128, 128], F32, "ExternalInput")
    consts = DT(nc, "consts", [128, NCONST, 128], F32, "ExternalInput")
    lru_out = DT(nc, "lru_out", [NU, 128, L], F32, "ExternalOutput")
    with contextlib.ExitStack() as st:
        cx = Ctx(nc, st)
        consts_sb, Bconst = p2_common(cx, consts)
        emit_lru(cx, lru_in, lru_pv, lru_w, lru_out, consts_sb, Bconst, NU=NU, L=L)
        cx.S.finish("sp")
        cx.S.emit()
    return nc


class QPool:
    def __init__(self, cx, banks):
        self.tiles = []
        for b in banks:
            self.tiles.append((cx.ps[b][:, 0:128], cx.Bps[b]))
        self.i = 0

    def next(self):
        t = self.tiles[self.i % len(self.tiles)]
        self.i += 1
        return t


class Rot:
    def __init__(self, cx, name, shape, dt, n):
        self.items = [(cx.sb("%s%d" % (name, i), shape, dt), Buf("%s%d" % (name, i))) for i in range(n)]
        self.i = 0

    def next(self):
        t = self.items[self.i % len(self.items)]
        self.i += 1
        return t


def emit_dn(cx, dn_in, dn_ab, dn_cw, dn_hp, dn_nw, dn_out, consts_sb, Bconst, NU=3, L=SEQ, tag="dn"):
    nc, S = cx.nc, cx.S
    S.barrier()
    m0 = cx.mark()
    SEG = 512
    NP = SEG // 128
    NSEG = L // SEG
    ident = consts_sb[:, C_ID, :]
    ones = consts_sb[:, C_ONES, :]
    QP = QPool(cx, [0, 1, 2, 3, 4, 5])
    BANK = [6, 7]
    bank_i = [0]

    def full_bank():
        b = BANK[bank_i[0] % 2]
        bank_i[0] += 1
        return cx.ps[b], cx.Bps[b]

    cwsb = cx.sb(tag + "cw", [128, 3, 4], F32)
    hp = cx.sb(tag + "hp", [128, 4], F32)
    nw = cx.sb(tag + "nw", [128, 1], F32)
    Bcw, Bhp, Bnw = Buf("dcw"), Buf("dhp"), Buf("dnw")
    S.dma("sp", lambda e: e.dma_start(out=nw, in_=dn_nw.ap), reads=[dn_nw.buf], writes=[Bnw])
    S_sb = cx.sb(tag + "S", [128, 128], F32)
    BS = Buf("S")
    NSC = 14
    (A_, BETA, NBETA, G_, GC, GL, GL0, GL1, EG, KBD, KDEC, D0, D1, TMP) = range(NSC)

    def mkset(i):
        d = {}
        for nm in ("qf", "kf", "vf", "zs", "sqb", "rn", "oT"):
            d[nm] = cx.sb("%s%s%d" % (tag, nm, i), [128, SEG], F32)
            d["B" + nm] = Buf(nm)
        for nm in ("Kbd", "Kdec", "Vb", "attnT", "U", "WT", "otm", "sqo"):
            d[nm] = cx.sb("%s%s%d" % (tag, nm, i), [128, NP, 128], F32)
            d["B" + nm] = [Buf("%s%d" % (nm, n)) for n in range(NP)]
        d["QdT"] = cx.sb("%sQdT%d" % (tag, i), [128, SEG], F32)
        d["BQdT"] = [Buf("QdT%d" % n) for n in range(NP)]
        d["sc"] = cx.sb("%ssc%d" % (tag, i), [128, NSC, NP], F32)
        d["Bsc"] = [Buf("sc%d" % k) for k in range(NSC)]
        d["ab"] = cx.sb("%sab%d" % (tag, i), [128, 2, NP], F32)
        d["Bab"] = Buf("ab")
        d["sso"] = cx.sb("%ssso%d" % (tag, i), [128, NP], F32)
        d["Bsso"] = Buf("sso")
        return d

    sets = [mkset(0), mkset(1)]
    pads = Rot(cx, tag + "pad", [128, SEG + 3], F32, 2)
    T128 = {nm: Rot(cx, tag + nm, [128, 128], F32, 2) for nm in ("Dg", "tL", "dL", "tU", "dU", "EG", "Bm", "Bt", "Pt")}
    PW = Rot(cx, tag + "pw", [128, 128], F32, 6)
    VN = Rot(cx, tag + "vn", [128, 128], F32, 2)

    for u in range(NU):
        S.dma("sp", lambda e, u=u: e.dma_start(out=cwsb, in_=dn_cw.ap[u]), reads=[dn_cw.buf], writes=[Bcw])
        S.dma("sp", lambda e, u=u: e.dma_start(out=hp[:, 0:2], in_=dn_hp.ap[u]), reads=[dn_hp.buf], writes=[Bhp])
        S.op("act", lambda e: e.activation(out=hp[:, 2:3], in_=hp[:, 0:1], func=AF.Exp), reads=[Bhp], writes=[Bhp])
        S.op("dve", lambda e: e.tensor_scalar(out=hp[:, 2:3], in0=hp[:, 2:3], scalar1=-1.0, scalar2=None, op0=ALU.mult), reads=[Bhp], writes=[Bhp])
        S.op("dve", lambda e: e.memset(S_sb, 0.0), writes=[BS])
        for seg in range(NSEG):
            d = sets[seg % 2]
            s0 = seg * SEG
            sc, Bsc = d["sc"], d["Bsc"]
            for part, nm in ((0, "qf"), (1, "kf"), (2, "vf")):
                pad, Bpad = pads.next()
                dst, Bdst = d[nm], d["B" + nm]
                if seg == 0:
                    S.op("dve", lambda e, pad=pad: e.memset(pad[:, 0:3], 0.0), writes=[Bpad])
                    S.dma("sp", lambda e, pad=pad, u=u, part=part: e.dma_start(out=pad[:, 3:SEG + 3], in_=dn_in.get(u, part)[:, 0:SEG]),
                          reads=[dn_in.buf], writes=[Bpad])
                else:
                    S.dma("sp", lambda e, pad=pad, u=u, part=part, s0=s0: e.dma_start(out=pad, in_=dn_in.get(u, part)[:, s0 - 3:s0 + SEG]),
                          reads=[dn_in.buf], writes=[Bpad])
                S.op("dve", lambda e, pad=pad, dst=dst, part=part: e.tensor_scalar(
                    out=dst, in0=pad[:, 0:SEG], scalar1=cwsb[:, part, 0:1], scalar2=None, op0=ALU.mult),
                    reads=[Bpad, Bcw], writes=[Bdst])
                for j in range(1, 4):
                    S.op("dve", lambda e, pad=pad, dst=dst, part=part, j=j: e.scalar_tensor_tensor(
                        out=dst, in0=pad[:, j:j + SEG], scalar=cwsb[:, part, j:j + 1], in1=dst, op0=ALU.mult, op1=ALU.add),
                        reads=[Bpad, Bcw, Bdst], writes=[Bdst])
                S.op("act", lambda e, dst=dst: e.activation(out=dst, in_=dst, func=AF.Silu), reads=[Bdst], writes=[Bdst])
                if part < 2:
                    S.op("act", lambda e, dst=dst, d=d: e.activation(out=d["sqb"], in_=dst, func=AF.Square), reads=[Bdst], writes=[d["Bsqb"]])
                    ps, Bp = full_bank()
                    S.op("pe", lambda e, ps=ps, d=d: e.matmul(ps[:], lhsT=ones, rhs=d["sqb"], start=True, stop=True),
                         reads=[Bconst, d["Bsqb"]], writes=[Bp])
                    S.op("act", lambda e, ps=ps, d=d: e.activation(out=d["rn"], in_=ps[:], func=AF.Sqrt, bias=cx.epsb[:, 0:1], scale=1.0),
                         reads=[Bp, cx.Beps], writes=[d["Brn"]])
                    S.op("dve", lambda e, d=d: e.reciprocal(out=d["rn"], in_=d["rn"]), reads=[d["Brn"]], writes=[d["Brn"]])
                    qs = (HD ** -0.5) if part == 0 else 1.0
                    S.op("dve", lambda e, dst=dst, d=d, qs=qs: e.scalar_tensor_tensor(
                        out=dst, in0=dst, scalar=qs, in1=d["rn"], op0=ALU.mult, op1=ALU.mult),
                        reads=[Bdst, d["Brn"]], writes=[Bdst])
            S.dma("sp", lambda e, d=d, u=u, s0=s0: e.dma_start(out=d["zs"], in_=dn_in.get(u, 3)[:, s0:s0 + SEG]),
                  reads=[dn_in.buf], writes=[d["Bzs"]])
            S.op("act", lambda e, d=d: e.activation(out=d["zs"], in_=d["zs"], func=AF.Silu), reads=[d["Bzs"]], writes=[d["Bzs"]])
            if hasattr(dn_ab, "get_row"):
                for ri in range(2):
                    S.dma("sp", lambda e, d=d, u=u, s0=s0, ri=ri: e.dma_start(
                        out=d["ab"][:, ri, :], in_=dn_ab.get_row(u, ri)[s0:s0 + SEG].rearrange("(n p) -> p n", p=128),
                        allow_slow_non_contiguous=True),
                        reads=[dn_ab.buf], writes=[d["Bab"]])
            else:
                S.dma("sp", lambda e, d=d, u=u, seg=seg: e.dma_start(out=d["ab"], in_=dn_ab.ap[u][:, :, seg * NP:(seg + 1) * NP]),
                      reads=[dn_ab.buf], writes=[d["Bab"]])
            S.op("act", lambda e, d=d, sc=sc: e.activation(out=sc[:, TMP, :], in_=d["ab"][:, 0, :], func=AF.Exp, bias=hp[:, 1:2], scale=1.0),
                 reads=[d["Bab"], Bhp], writes=[Bsc[TMP]])
            S.op("act", lambda e, sc=sc: e.activation(out=sc[:, TMP, :], in_=sc[:, TMP, :], func=AF.Ln, bias=ones[:, 0:1], scale=1.0),
                 reads=[Bsc[TMP], Bconst], writes=[Bsc[TMP]])
            S.op("dve", lambda e, sc=sc: e.tensor_scalar(out=sc[:, G_, :], in0=sc[:, TMP, :], scalar1=hp[:, 2:3], scalar2=None, op0=ALU.mult),
                 reads=[Bsc[TMP], Bhp], writes=[Bsc[G_]])
            S.op("act", lambda e, d=d, sc=sc: e.activation(out=sc[:, BETA, :], in_=d["ab"][:, 1, :], func=AF.Sigmoid),
                 reads=[d["Bab"]], writes=[Bsc[BETA]])
            S.op("dve", lambda e, sc=sc: e.tensor_scalar(out=sc[:, NBETA, :], in0=sc[:, BETA, :], scalar1=-1.0, scalar2=None, op0=ALU.mult),
                 reads=[Bsc[BETA]], writes=[Bsc[NBETA]])
            pq, Bq = QP.next()
            S.op("pe", lambda e, pq=pq, sc=sc: e.matmul(pq[:, 0:NP], lhsT=consts_sb[:, C_TRI, :], rhs=sc[:, G_, :], start=True, stop=True),
                 reads=[Bconst, Bsc[G_]], writes=[Bq])
            S.op("dve", lambda e, pq=pq, sc=sc: e.tensor_copy(out=sc[:, GC, :], in_=pq[:, 0:NP]), reads=[Bq], writes=[Bsc[GC]])
            for ci, slot in ((C_SELEND, GL), (C_SEL63, GL0), (C_SEL127, GL1)):
                pq, Bq = QP.next()
                S.op("pe", lambda e, pq=pq, sc=sc, ci=ci: e.matmul(pq[:, 0:NP], lhsT=consts_sb[:, ci, :], rhs=sc[:, GC, :], start=True, stop=True),
                     reads=[Bconst, Bsc[GC]], writes=[Bq])
                S.op("dve", lambda e, pq=pq, sc=sc, slot=slot: e.tensor_copy(out=sc[:, slot, :], in_=pq[:, 0:NP]), reads=[Bq], writes=[Bsc[slot]])
            S.op("act", lambda e, sc=sc: e.activation(out=sc[:, EG, :], in_=sc[:, GC, :], func=AF.Exp), reads=[Bsc[GC]], writes=[Bsc[EG]])
            S.op("dve", lambda e, sc=sc: e.tensor_tensor(out=sc[:, KBD, :], in0=sc[:, BETA, :], in1=sc[:, EG, :], op=ALU.mult),
                 reads=[Bsc[BETA], Bsc[EG]], writes=[Bsc[KBD]])
            S.op("dve", lambda e, sc=sc: e.tensor_tensor(out=sc[:, KDEC, :], in0=sc[:, GL, :], in1=sc[:, GC, :], op=ALU.subtract),
                 reads=[Bsc[GL], Bsc[GC]], writes=[Bsc[KDEC]])
            S.op("act", lambda e, sc=sc: e.activation(out=sc[:, KDEC, :], in_=sc[:, KDEC, :], func=AF.Exp), reads=[Bsc[KDEC]], writes=[Bsc[KDEC]])
            S.op("act", lambda e, sc=sc: e.activation(out=sc[:, D0, :], in_=sc[:, GL0, :], func=AF.Exp), reads=[Bsc[GL0]], writes=[Bsc[D0]])
            S.op("act", lambda e, sc=sc: e.activation(out=sc[:, D1, :], in_=sc[:, GL1, :], func=AF.Exp), reads=[Bsc[GL1]], writes=[Bsc[D1]])
            for n in range(NP):
                blk = slice(n * 128, (n + 1) * 128)
                pq, Bq = QP.next()
                S.op("pe", lambda e, pq=pq, d=d, blk=blk: e.transpose(out=pq, in_=d["kf"][:, blk], identity=ident),
                     reads=[d["Bkf"], Bconst], writes=[Bq])
                S.op("dve", lambda e, pq=pq, d=d, n=n, sc=sc: e.tensor_scalar(out=d["Kbd"][:, n, :], in0=pq, scalar1=sc[:, KBD, n:n + 1], scalar2=None, op0=ALU.mult),
                     reads=[Bq, Bsc[KBD]], writes=[d["BKbd"][n]])
                S.op("dve", lambda e, pq=pq, d=d, n=n, sc=sc: e.tensor_scalar(out=d["Kdec"][:, n, :], in0=pq, scalar1=sc[:, KDEC, n:n + 1], scalar2=None, op0=ALU.mult),
                     reads=[Bq, Bsc[KDEC]], writes=[d["BKdec"][n]])
                pq, Bq = QP.next()
                S.op("pe", lambda e, pq=pq, d=d, blk=blk: e.transpose(out=pq, in_=d["vf"][:, blk], identity=ident),
                     reads=[d["Bvf"], Bconst], writes=[Bq])
                S.op("dve", lambda e, pq=pq, d=d, n=n, sc=sc: e.tensor_scalar(out=d["Vb"][:, n, :], in0=pq, scalar1=sc[:, BETA, n:n + 1], scalar2=None, op0=ALU.mult),
                     reads=[Bq, Bsc[BETA]], writes=[d["BVb"][n]])
            for n in range(NP):
                blk = slice(n * 128, (n + 1) * 128)
                gcc = sc[:, GC, n:n + 1]
                Dg, BDg = T128["Dg"].next()
                S.op("dve", lambda e, Dg=Dg, gcc=gcc: e.tensor_scalar(out=Dg, in0=ident, scalar1=gcc, scalar2=None, op0=ALU.mult),
                     reads=[Bconst, Bsc[GC]], writes=[BDg])
                pA, BpA = QP.next()
                S.op("pe", lambda e, pA=pA, Dg=Dg: e.matmul(pA, lhsT=ones, rhs=Dg, start=True, stop=True), reads=[Bconst, BDg], writes=[BpA])
                tL, BtL = T128["tL"].next()
                S.op("dve", lambda e, tL=tL, pA=pA, gcc=gcc: e.scalar_tensor_tensor(
                    out=tL, in0=pA, scalar=gcc, in1=consts_sb[:, C_MLS, :], op0=ALU.subtract, op1=ALU.add),
                    reads=[BpA, Bsc[GC], Bconst], writes=[BtL])
                dL, BdL = T128["dL"].next()
                S.op("act", lambda e, dL=dL, tL=tL: e.activation(out=dL, in_=tL, func=AF.Exp, scale=-1.0), reads=[BtL], writes=[BdL])
                tU, BtU = T128["tU"].next()
                S.op("dve", lambda e, tU=tU, pA=pA, gcc=gcc: e.scalar_tensor_tensor(
                    out=tU, in0=pA, scalar=gcc, in1=consts_sb[:, C_MU, :], op0=ALU.subtract, op1=ALU.add),
                    reads=[BpA, Bsc[GC], Bconst], writes=[BtU])
                dU, BdU = T128["dU"].next()
                S.op("act", lambda e, dU=dU, tU=tU: e.activation(out=dU, in_=tU, func=AF.Exp), reads=[BtU], writes=[BdU])
                EGt, BEG = T128["EG"].next()
                S.op("act", lambda e, EGt=EGt, pA=pA: e.activation(out=EGt, in_=pA, func=AF.Exp), reads=[BpA], writes=[BEG])
                S.op("dve", lambda e, d=d, blk=blk, EGt=EGt: e.tensor_tensor(out=d["QdT"][:, blk], in0=d["qf"][:, blk], in1=EGt, op=ALU.mult),
                     reads=[d["Bqf"], BEG], writes=[d["BQdT"][n]])
                pG, BpG = QP.next()
                S.op("pe", lambda e, pG=pG, d=d, blk=blk: e.matmul(pG, lhsT=d["kf"][:, blk], rhs=d["kf"][:, blk], start=True, stop=True),
                     reads=[d["Bkf"]], writes=[BpG])
                Bm, BBm = T128["Bm"].next()
                S.op("dve", lambda e, Bm=Bm, pG=pG, sc=sc, n=n, dL=dL: e.scalar_tensor_tensor(
                    out=Bm, in0=pG, scalar=sc[:, NBETA, n:n + 1], in1=dL, op0=ALU.mult, op1=ALU.mult),
                    reads=[BpG, Bsc[NBETA], BdL], writes=[BBm])
                pQ, BpQ = QP.next()
                S.op("pe", lambda e, pQ=pQ, d=d, blk=blk: e.matmul(pQ, lhsT=d["kf"][:, blk], rhs=d["qf"][:, blk], start=True, stop=True),
                     reads=[d["Bkf"], d["Bqf"]], writes=[BpQ])
                S.op("dve", lambda e, d=d, n=n, pQ=pQ, dU=dU: e.tensor_tensor(out=d["attnT"][:, n, :], in0=pQ, in1=dU, op=ALU.mult),
                     reads=[BpQ, BdU], writes=[d["BattnT"][n]])
                pT, BpT = QP.next()
                S.op("pe", lambda e, pT=pT, Bm=Bm: e.transpose(out=pT, in_=Bm, identity=ident), reads=[BBm, Bconst], writes=[BpT])
                Bt, BBt = T128["Bt"].next()
                S.op("act", lambda e, Bt=Bt, pT=pT: e.copy(out=Bt, in_=pT), reads=[BpT], writes=[BBt])
                Pt, BPt = T128["Pt"].next()
                S.op("dve", lambda e, Pt=Pt, Bt=Bt: e.tensor_tensor(out=Pt, in0=Bt, in1=ident, op=ALU.add), reads=[BBt, Bconst], writes=[BPt])
                cB, BcB, cBt, BcBt = Bm, BBm, Bt, BBt
                for lvl in range(1, 6):
                    p1, Bp1 = QP.next()
                    S.op("pe", lambda e, p1=p1, cBt=cBt, cB=cB: e.matmul(p1, lhsT=cBt, rhs=cB, start=True, stop=True),
                         reads=[BcBt, BcB], writes=[Bp1])
                    nB, BnB = PW.next()
                    S.op("act", lambda e, nB=nB, p1=p1: e.copy(out=nB, in_=p1), reads=[Bp1], writes=[BnB])
                    nBt, BnBt = None, None
                    if lvl < 5:
                        p2, Bp2 = QP.next()
                        S.op("pe", lambda e, p2=p2, cBt=cBt, cB=cB: e.matmul(p2, lhsT=cB, rhs=cBt, start=True, stop=True),
                             reads=[BcBt, BcB], writes=[Bp2])
                        nBt, BnBt = PW.next()
                        S.op("dve", lambda e, nBt=nBt, p2=p2: e.tensor_copy(out=nBt, in_=p2), reads=[Bp2], writes=[BnBt])
                    p3, Bp3 = QP.next()
                    S.op("pe", lambda e, p3=p3, nB=nB, Pt=Pt: e.matmul(p3, lhsT=nB, rhs=Pt, start=True, stop=True),
                         reads=[BnB, BPt], writes=[Bp3])
                    S.op("dve", lambda e, Pt=Pt, p3=p3: e.tensor_tensor(out=Pt, in0=Pt, in1=p3, op=ALU.add), reads=[BPt, Bp3], writes=[BPt])
                    cB, BcB, cBt, BcBt = nB, BnB, nBt, BnBt
                pU, BpU = QP.next()
                for hb in range(2):
                    P = slice(hb * 64, hb * 64 + 64)
                    S.op("pe", lambda e, pU=pU, Pt=Pt, d=d, n=n, P=P: e.matmul(pU[P, :], lhsT=Pt[P, P], rhs=d["Vb"][P, n, :], start=True, stop=True),
                         reads=[BPt, d["BVb"][n]], writes=[BpU])
                S.op("act", lambda e, pU=pU, d=d, n=n: e.copy(out=d["U"][:, n, :], in_=pU), reads=[BpU], writes=[d["BU"][n]])
                pW, BpW = QP.next()
                S.op("pe", lambda e, pW=pW, Pt=Pt, d=d, n=n: e.matmul(pW, lhsT=d["Kbd"][:, n, :], rhs=Pt, start=True, stop=True),
                     reads=[BPt, d["BKbd"][n]], writes=[BpW])
                S.op("dve", lambda e, pW=pW, d=d, n=n: e.tensor_copy(out=d["WT"][:, n, :], in_=pW), reads=[BpW], writes=[d["BWT"][n]])
            for c in range(2 * NP):
                n, hb = c // 2, c % 2
                P = slice(hb * 64, hb * 64 + 64)
                cols = slice(n * 128 + hb * 64, n * 128 + hb * 64 + 64)
                pa, Bpa = QP.next()
                S.op("pe", lambda e, pa=pa, d=d, n=n, P=P: e.matmul(pa[P, :], lhsT=d["WT"][:, n, P], rhs=S_sb, start=True, stop=True),
                     reads=[d["BWT"][n], BS], writes=[Bpa])
                vn, Bvn = VN.next()
                S.op("dve", lambda e, vn=vn, pa=pa, d=d, n=n, P=P: e.tensor_tensor(out=vn[P, :], in0=d["U"][P, n, :], in1=pa[P, :], op=ALU.subtract),
                     reads=[d["BU"][n], Bpa], writes=[Bvn])
                po, Bpo = QP.next()
                S.op("pe", lambda e, po=po, d=d, cols=cols, P=P: e.matmul(po[P, :], lhsT=d["QdT"][:, cols], rhs=S_sb, start=True, stop=False),
                     reads=[d["BQdT"][n], BS], writes=[Bpo])
                S.op("pe", lambda e, po=po, d=d, n=n, P=P, vn=vn: e.matmul(po[P, :], lhsT=d["attnT"][P, n, P], rhs=vn[P, :], start=False, stop=True),
                     reads=[d["BattnT"][n], Bvn], writes=[Bpo])
                pS, BpS = QP.next()
                S.op("pe", lambda e, pS=pS, d=d, n=n, P=P, vn=vn: e.matmul(pS, lhsT=d["Kdec"][P, n, :], rhs=vn[P, :], start=True, stop=True),
                     reads=[d["BKdec"][n], Bvn], writes=[BpS])
                dslot = D0 if hb == 0 else D1
                S.op("dve", lambda e, pS=pS, sc=sc, dslot=dslot, n=n: e.scalar_tensor_tensor(
                    out=S_sb, in0=S_sb, scalar=sc[:, dslot, n:n + 1], in1=pS, op0=ALU.mult, op1=ALU.add),
                    reads=[BS, Bsc[dslot], BpS], writes=[BS])
                S.op("act", lambda e, po=po, d=d, n=n, P=P: e.copy(out=d["otm"][P, n, :], in_=po[P, :]), reads=[Bpo], writes=[d["Botm"][n]])
            S.op("act", lambda e, d=d: e.activation(out=d["sqo"], in_=d["otm"], func=AF.Square), reads=d["Botm"], writes=d["Bsqo"])
            S.op("dve", lambda e, d=d: e.tensor_reduce(out=d["sso"], in_=d["sqo"], axis=AX.X, op=ALU.add), reads=d["Bsqo"], writes=[d["Bsso"]])
            S.op("act", lambda e, d=d: e.activation(out=d["sso"], in_=d["sso"], func=AF.Sqrt, bias=cx.epsb[:, 0:1], scale=1.0 / HD),
                 reads=[d["Bsso"], cx.Beps], writes=[d["Bsso"]])
            S.op("dve", lambda e, d=d: e.reciprocal(out=d["sso"], in_=d["sso"]), reads=[d["Bsso"]], writes=[d["Bsso"]])
            S.op("dve", lambda e, d=d: e.tensor_tensor(out=d["otm"], in0=d["otm"], in1=d["sso"].unsqueeze(2).to_broadcast([128, NP, 128]), op=ALU.mult),
                 reads=d["Botm"] + [d["Bsso"]], writes=d["Botm"])
            for n in range(NP):
                blk = slice(n * 128, (n + 1) * 128)
                pT, BpT = QP.next()
                S.op("pe", lambda e, pT=pT, d=d, n=n: e.transpose(out=pT, in_=d["otm"][:, n, :], identity=ident),
                     reads=[d["Botm"][n], Bconst], writes=[BpT])
                S.op("dve", lambda e, pT=pT, d=d, blk=blk: e.scalar_tensor_tensor(
                    out=d["oT"][:, blk], in0=pT, scalar=nw[:, 0:1], in1=d["zs"][:, blk], op0=ALU.mult, op1=ALU.mult),
                    reads=[BpT, Bnw, d["Bzs"]], writes=[d["BoT"]])
            S.dma("sp", lambda e, d=d, u=u, s0=s0: e.dma_start(out=dn_out.get(u)[:, s0:s0 + SEG], in_=d["oT"]),
                  reads=[d["BoT"]], writes=[dn_out.b((u, seg))], owner=d["BoT"], is_out=dn_out.is_out)
    cx.release(m0)


def build_dn_test(NU=1, L=SEQ):
    nc = bass.Bass("TRN2", target_bir_lowering=False)
    dn_in = DT(nc, "dn_in", [NU, 4, 128, L], F32, "ExternalInput")
    dn_ab = DT(nc, "dn_ab", [NU, 128, 2, L // 128], F32, "ExternalInput")
    dn_cw = DT(nc, "dn_cw", [NU, 128, 3, 4], F32, "ExternalInput")
    dn_hp = DT(nc, "dn_hp", [NU, 128, 2], F32, "ExternalInput")
    dn_nw = DT(nc, "dn_nw", [128, 1], F32, "ExternalInput")
    consts = DT(nc, "consts", [128, NCONST, 128], F32, "ExternalInput")
    dn_out = DT(nc, "dn_out", [NU, 128, L], F32, "ExternalOutput")
    with contextlib.ExitStack() as st:
        cx = Ctx(nc, st)
        consts_sb, Bconst = p2_common(cx, consts)
        import os as _os
        (emit_dn2 if _os.environ.get("DN2") else emit_dn)(cx, dn_in, dn_ab, dn_cw, dn_hp, dn_nw, dn_out, consts_sb, Bconst, NU=NU, L=L)
        cx.S.finish("sp")
        cx.S.emit()
    return nc


def emit_sg(cx, suv, sg_wT, sg_bs, sg_vec, sg_out, consts_sb, Bconst, T=TOK, tag="sg", fm_src=None):
    nc, S = cx.nc, cx.S
    S.barrier()
    m0 = cx.mark()
    NG = SG_W // 128
    ident = consts_sb[:, C_ID, :]
    vec = cx.sb(tag + "vec", [128, 3, SG_W], F32)
    Bvec = Buf("sgvec")
    for i in range(3):
        S.dma("sp", lambda e, i=i: e.dma_start(out=vec[:, i, :], in_=sg_vec.ap[i].partition_broadcast(128)),
              reads=[sg_vec.buf], writes=[Bvec])
    wT = cx.sb(tag + "wT", [128, NG, 128], F32)
    BwT = Buf("sgwT")
    S.dma("sp", lambda e: e.dma_start(out=wT, in_=sg_wT.ap.rearrange("g s t -> s g t")), reads=[sg_wT.buf], writes=[BwT])
    S.op("dve", lambda e: e.tensor_tensor(out=wT, in0=wT, in1=consts_sb[:, C_UP:C_UP + 1, :].to_broadcast([128, NG, 128]), op=ALU.mult),
         reads=[BwT, Bconst], writes=[BwT])
    bs = cx.sb(tag + "bs", [128, NG], F32)
    Bbs = Buf("sgbs")
    S.dma("sp", lambda e: e.dma_start(out=bs, in_=sg_bs.ap), reads=[sg_bs.buf], writes=[Bbs])
    R = {nm: Rot(cx, tag + nm, [128, SG_W], F32, 2) for nm in ("u", "v", "y")}
    Rst = Rot(cx, tag + "st", [128, 16], F32, 2)
    Ro = Rot(cx, tag + "o", [128, NG, 128], F32, 2)
    Rfm = Rot(cx, tag + "fm", [128, 2 * NG, 128], F32, 2) if fm_src is not None else None
    for ch in range(T // 128):
        rows = slice(ch * 128, (ch + 1) * 128)
        u, Bu = R["u"].next()
        v, Bv = R["v"].next()
        y, By = R["y"].next()
        st, Bst = Rst.next()
        if fm_src is None:
            S.dma("sp", lambda e, u=u, rows=rows: e.dma_start(out=u, in_=suv.ap[rows, 0:SG_W]), reads=[suv.buf], writes=[Bu])
            S.dma("sp", lambda e, v=v, rows=rows: e.dma_start(out=v, in_=suv.ap[rows, SG_W:2 * SG_W]), reads=[suv.buf], writes=[Bv])
            S.op("act", lambda e, u=u: e.activation(out=u, in_=u, func=AF.Gelu), reads=[Bu], writes=[Bu])
            S.op("act", lambda e, v=v, st=st: e.activation(out=v, in_=v, func=AF.Gelu, accum_out=st[:, 0:1]), reads=[Bv], writes=[Bv, Bst])
        else:
            fm, Bfm = Rfm.next()
            S.dma("sp", lambda e, fm=fm, rows=rows: e.dma_start(out=fm, in_=fm_src.ap.rearrange("(b c) t -> c b t", c=128)[:, :, rows]),
                  reads=[fm_src.buf], writes=[Bfm])
            for blk in range(2 * NG):
                b = blk % 6
                dstt, Bd = (u, Bu) if blk < NG else (v, Bv)
                cs = slice((blk % NG) * 128, (blk % NG + 1) * 128)
                S.op("pe", lambda e, b=b, fm=fm, blk=blk: e.transpose(out=cx.ps[b][:, 0:128], in_=fm[:, blk, :], identity=ident),
                     reads=[Bfm, Bconst], writes=[cx.Bps[b]])
                S.op("act", lambda e, b=b, dstt=dstt, cs=cs: e.activation(out=dstt[:, cs], in_=cx.ps[b][:, 0:128], func=AF.Gelu),
                     reads=[cx.Bps[b]], writes=[Bd])
            S.op("dve", lambda e, v=v, st=st: e.tensor_reduce(out=st[:, 0:1], in_=v, axis=AX.X, op=ALU.add), reads=[Bv], writes=[Bst])
        S.op("dve", lambda e, st=st: e.tensor_scalar(out=st[:, 1:2], in0=st[:, 0:1], scalar1=1.0 / SG_W, scalar2=None, op0=ALU.mult), reads=[Bst], writes=[Bst])
        S.op("dve", lambda e, v=v, st=st: e.tensor_scalar(out=v, in0=v, scalar1=st[:, 1:2], scalar2=None, op0=ALU.subtract), reads=[Bv, Bst], writes=[Bv])
        S.op("act", lambda e, v=v, y=y, st=st: e.activation(out=y, in_=v, func=AF.Square, accum_out=st[:, 2:3]), reads=[Bv], writes=[By, Bst])
        S.op("act", lambda e, st=st: e.activation(out=st[:, 3:4], in_=st[:, 2:3], func=AF.Sqrt, bias=cx.epsb[:, 0:1], scale=1.0 / SG_W),
             reads=[Bst, cx.Beps], writes=[Bst])
        S.op("dve", lambda e, st=st: e.reciprocal(out=st[:, 3:4], in_=st[:, 3:4]), reads=[Bst], writes=[Bst])
        S.op("dve", lambda e, v=v, st=st: e.scalar_tensor_tensor(out=v, in0=v, scalar=st[:, 3:4], in1=vec[:, 0, :], op0=ALU.mult, op1=ALU.mult),
             reads=[Bv, Bst, Bvec], writes=[Bv])
        S.op("dve", lambda e, v=v: e.tensor_tensor(out=v, in0=v, in1=vec[:, 1, :], op=ALU.add), reads=[Bv, Bvec], writes=[Bv])
        for half in range(2):
            b = 6 + half
            for gg in range(4):
                g = half * 4 + gg
                S.op("pe", lambda e, b=b, gg=gg, g=g, v=v: e.matmul(cx.ps[b][:, gg * 128:(gg + 1) * 128], lhsT=wT[:, g, :], rhs=v[:, g * 128:(g + 1) * 128],
                                                                     start=True, stop=True),
                     reads=[BwT, Bv], writes=[cx.Bps[b]])
            for gg in range(4):
                g = half * 4 + gg
                S.op("dve", lambda e, b=b, gg=gg, g=g, u=u, y=y: e.scalar_tensor_tensor(
                    out=y[:, g * 128:(g + 1) * 128], in0=cx.ps[b][:, gg * 128:(gg + 1) * 128], scalar=bs[:, g:g + 1], in1=u[:, g * 128:(g + 1) * 128],
                    op0=ALU.add, op1=ALU.mult), reads=[cx.Bps[b], Bbs, Bu], writes=[By])
        S.op("act", lambda e, u=u, y=y: e.activation(out=u, in_=y, func=AF.Square), reads=[By], writes=[Bu])
        S.op("dve", lambda e, u=u, st=st: e.tensor_reduce(out=st[:, 8:16], in_=u.rearrange("p (g c) -> p g c", g=NG), axis=AX.X, op=ALU.add),
             reads=[Bu], writes=[Bst])
        S.op("act", lambda e, st=st: e.activation(out=st[:, 8:16], in_=st[:, 8:16], func=AF.Sqrt, bias=cx.epsb[:, 0:1], scale=1.0 / 128),
             reads=[Bst, cx.Beps], writes=[Bst])
        S.op("dve", lambda e, st=st: e.reciprocal(out=st[:, 8:16], in_=st[:, 8:16]), reads=[Bst], writes=[Bst])
        S.op("dve", lambda e, y=y, st=st: e.tensor_tensor(out=y.rearrange("p (g c) -> p g c", g=NG), in0=y.rearrange("p (g c) -> p g c", g=NG),
                                                           in1=st[:, 8:16].unsqueeze(2).to_broadcast([128, NG, 128]), op=ALU.mult),
             reads=[By, Bst], writes=[By])
        S.op("dve", lambda e, y=y: e.tensor_tensor(out=y, in0=y, in1=vec[:, 2, :], op=ALU.mult), reads=[By, Bvec], writes=[By])
        o, Bo = Ro.next()
        for g in range(NG):
            b = g % 6
            S.op("pe", lambda e, b=b, g=g, y=y: e.transpose(out=cx.ps[b][:, 0:128], in_=y[:, g * 128:(g + 1) * 128], identity=ident),
                 reads=[By, Bconst], writes=[cx.Bps[b]])
            if g % 2 == 0:
                S.op("act", lambda e, b=b, g=g, o=o: e.copy(out=o[:, g, :], in_=cx.ps[b][:, 0:128]), reads=[cx.Bps[b]], writes=[Bo])
            else:
                S.op("dve", lambda e, b=b, g=g, o=o: e.tensor_copy(out=o[:, g, :], in_=cx.ps[b][:, 0:128]), reads=[cx.Bps[b]], writes=[Bo])
        S.dma("sp", lambda e, o=o, rows=rows: e.dma_start(out=sg_out.ap.rearrange("(g c) t -> c g t", c=128)[:, :, rows], in_=o),
              reads=[Bo], writes=[sg_out.b(ch)], owner=Bo, is_out=sg_out.is_out)
    cx.release(m0)


def build_sg_test(T=TOK):
    nc = bass.Bass("TRN2", target_bir_lowering=False)
    suv = DT(nc, "suv", [T, 2 * SG_W], F32, "ExternalInput")
    sg_wT = DT(nc, "sg_wT", [SG_W // 128, 128, 128], F32, "ExternalInput")
    sg_bs = DT(nc, "sg_bs", [128, SG_W // 128], F32, "ExternalInput")
    sg_vec = DT(nc, "sg_vec", [3, SG_W], F32, "ExternalInput")
    consts = DT(nc, "consts", [128, NCONST, 128], F32, "ExternalInput")
    sg_out = DT(nc, "sg_out", [SG_W, T], F32, "ExternalOutput")
    with contextlib.ExitStack() as st:
        cx = Ctx(nc, st)
        consts_sb, Bconst = p2_common(cx, consts)
        emit_sg(cx, suv, sg_wT, sg_bs, sg_vec, sg_out, consts_sb, Bconst, T=T)
        cx.S.finish("sp")
        cx.S.emit()
    return nc


def build_p2(T=TOK, L=SEQ, NU=3):
    nc = bass.Bass("TRN2", target_bir_lowering=False)
    dn_in = DT(nc, "dn_in", [NU, 4, 128, L], F32, "ExternalInput")
    dn_ab = DT(nc, "dn_ab", [NU, 128, 2, L // 128], F32, "ExternalInput")
    dn_cw = DT(nc, "dn_cw", [NU, 128, 3, 4], F32, "ExternalInput")
    dn_hp = DT(nc, "dn_hp", [NU, 128, 2], F32, "ExternalInput")
    dn_nw = DT(nc, "dn_nw", [128, 1], F32, "ExternalInput")
    lru_in = DT(nc, "lru_in", [NU, 2, 128, L], F32, "ExternalInput")
    lru_pv = DT(nc, "lru_pv", [NU, 128, 9], F32, "ExternalInput")
    lru_w = DT(nc, "lru_w", [NU, 2, 128, 128], F32, "ExternalInput")
    suv = DT(nc, "suv", [T, 2 * SG_W], F32, "ExternalInput")
    sg_wT = DT(nc, "sg_wT", [SG_W // 128, 128, 128], F32, "ExternalInput")
    sg_bs = DT(nc, "sg_bs", [128, SG_W // 128], F32, "ExternalInput")
    sg_vec = DT(nc, "sg_vec", [3, SG_W], F32, "ExternalInput")
    consts = DT(nc, "consts", [128, NCONST, 128], F32, "ExternalInput")
    dn_out = DT(nc, "dn_out", [NU, 128, L], F32, "ExternalOutput")
    lru_out = DT(nc, "lru_out", [NU, 128, L], F32, "ExternalOutput")
    sg_out = DT(nc, "sg_out", [SG_W, T], F32, "ExternalOutput")
    with contextlib.ExitStack() as st:
        cx = Ctx(nc, st)
        consts_sb, Bconst = p2_common(cx, consts)
        emit_lru(cx, lru_in, lru_pv, lru_w, lru_out, consts_sb, Bconst, NU=NU, L=L)
        emit_sg(cx, suv, sg_wT, sg_bs, sg_vec, sg_out, consts_sb, Bconst, T=T)
        emit_dn2(cx, dn_in, dn_ab, dn_cw, dn_hp, dn_nw, dn_out, consts_sb, Bconst, NU=NU, L=L)
        cx.S.finish("sp")
        cx.S.emit()
    return nc


_PROGS = {}


def _prog(name, fn):
    if name not in _PROGS:
        _PROGS[name] = fn()
    return _PROGS[name]


def _lay(v, n):
    return np.ascontiguousarray(np.asarray(v, np.float32).reshape(n, 128).T)


def _run(nc, in_maps):
    res = run_bass_kernel_spmd(nc, in_maps, core_ids=list(range(NCORES)))
    return res.results


def kernel_unfused(x, norm_mix, w_in, dn_conv_w, dn_a_log, dn_dt_bias, dn_norm_w,
           lru_conv_w, lru_conv_b, lru_w_a, lru_b_a, lru_w_x, lru_b_x, lru_lambda, lru_norm_w,
           sg_ln_w, sg_ln_b, sg_w_s, sg_b_s, sg_norm_w, w_out,
           norm_ffn, w_up, ffn_conv_w, ffn_conv_b, w_down, norm_final):
    f32 = np.float32
    x = np.asarray(x, f32)
    NSEGC = SEQ // TOK
    consts = make_consts()
    xT = np.ascontiguousarray(x.transpose(0, 2, 1))
    p1 = _prog("p1", build_p1)
    p2 = _prog("p2", build_p2)
    for l in range(DEPTH):
        wl = np.asarray(w_in[l], f32)
        g1 = _lay(norm_mix[l], D_MODEL // 128)
        maps = []
        for c in range(NCORES):
            b, j = divmod(c, NSEGC)
            maps.append({"xT": np.ascontiguousarray(xT[b][:, j * TOK:(j + 1) * TOK]), "gain": g1, "w": wl})
        r1 = _run(p1, maps)
        projT = [np.concatenate([r1[b * NSEGC + j]["projT"] for j in range(NSEGC)], axis=1) for b in range(BATCH)]
        maps = []
        cwl = np.asarray(dn_conv_w[l], f32)
        for c in range(NCORES):
            b, q4 = divmod(c, NSEGC)
            pj = projT[b]
            heads = [3 * q4 + u for u in range(3)]
            dn_in = np.stack([np.stack([pj[part * DN_W + h * 128: part * DN_W + (h + 1) * 128] for part in range(4)]) for h in heads])
            ab = np.stack([np.stack([pj[6156 + h].reshape(SEQ // 128, 128).T, pj[6144 + h].reshape(SEQ // 128, 128).T], axis=1) for h in heads])
            dcw = np.stack([np.stack([cwl[:, part * DN_W + h * 128: part * DN_W + (h + 1) * 128].T for part in range(3)], axis=1) for h in heads])
            hp = np.stack([np.stack([np.full(128, dn_a_log[l][h], f32), np.full(128, dn_dt_bias[l][h], f32)], axis=1) for h in heads])
            lru_in = np.stack([np.stack([pj[6168 + g * 128: 6168 + (g + 1) * 128], pj[7704 + g * 128: 7704 + (g + 1) * 128]]) for g in heads])
            pv = np.stack([np.concatenate([np.asarray(lru_conv_w[l], f32)[:, g * 128:(g + 1) * 128].T] + [
                np.asarray(v[l], f32)[g * 128:(g + 1) * 128, None] for v in (lru_conv_b, lru_b_a, lru_b_x, lru_lambda, lru_norm_w)], axis=1) for g in heads])
            lw = np.stack([np.stack([np.asarray(lru_w_a[l][g], f32), np.asarray(lru_w_x[l][g], f32)]) for g in heads])
            own = r1[c]["projT"]
            suv = np.ascontiguousarray(own[9240:11288].T)
            maps.append({
                "dn_in": np.ascontiguousarray(dn_in, f32), "dn_ab": np.ascontiguousarray(ab, f32), "dn_cw": np.ascontiguousarray(dcw, f32),
                "dn_hp": np.ascontiguousarray(hp, f32), "dn_nw": np.ascontiguousarray(np.asarray(dn_norm_w[l], f32)[:, None]),
                "lru_in": np.ascontiguousarray(lru_in, f32), "lru_pv": np.ascontiguousarray(pv, f32), "lru_w": np.ascontiguousarray(lw, f32),
                "suv": suv, "sg_wT": np.ascontiguousarray(np.asarray(sg_w_s[l], f32).transpose(0, 2, 1)),
                "sg_bs": np.ascontiguousarray(np.asarray(sg_b_s[l], f32).T),
                "sg_vec": np.ascontiguousarray(np.stack([np.asarray(v[l], f32) for v in (sg_ln_w, sg_ln_b, sg_norm_w)])),
                "consts": consts})
        r2 = _run(p2, maps)
        mixT = []
        for b in range(BATCH):
            m = np.zeros((D_MODEL, SEQ), f32)
            for q4 in range(NSEGC):
                c = b * NSEGC + q4
                for u in range(3):
                    h = 3 * q4 + u
                    m[h * 128:(h + 1) * 128] = r2[c]["dn_out"][u]
                    m[DN_W + h * 128: DN_W + (h + 1) * 128] = r2[c]["lru_out"][u]
                m[DN_W + LRU_W:, q4 * TOK:(q4 + 1) * TOK] = r2[c]["sg_out"]
            mixT.append(m)
        final = (l == DEPTH - 1)
        p3 = _prog("p3f" if final else "p3", lambda: build_p3(final=final))
        g2 = _lay(norm_ffn[l], D_MODEL // 128)
        cwf = np.ascontiguousarray(np.asarray(ffn_conv_w[l], f32).reshape(3, 2 * D_FF // 128, 128).transpose(2, 1, 0))
        cbf = _lay(ffn_conv_b[l], 2 * D_FF // 128)
        maps = []
        for c in range(NCORES):
            b, j = divmod(c, NSEGC)
            xh = np.zeros((D_MODEL, TOK + 2), f32)
            mh = np.zeros((D_MODEL, TOK + 2), f32)
            lo = j * TOK - 2
            if j == 0:
                xh[:, 2:] = xT[b][:, 0:TOK]
                mh[:, 2:] = mixT[b][:, 0:TOK]
            else:
                xh[:] = xT[b][:, lo:lo + TOK + 2]
                mh[:] = mixT[b][:, lo:lo + TOK + 2]
            mp = {"xT": xh, "mixT": mh, "w_out": np.asarray(w_out[l], f32), "gain2": g2, "w_up": np.asarray(w_up[l], f32),
                  "cw": cwf, "cb": cbf, "w_down": np.asarray(w_down[l], f32)}
            if final:
                mp["gainf"] = _lay(norm_final, D_MODEL // 128)
            maps.append(mp)
        r3 = _run(p3, maps)
        xT = np.stack([np.concatenate([r3[b * NSEGC + j]["outT"] for j in range(NSEGC)], axis=1) for b in range(BATCH)])
    return np.ascontiguousarray(xT.transpose(0, 2, 1)).astype(f32)


def build_fused(L=SEQ, depth=DEPTH):
    nc = bass.Bass("TRN2", target_bir_lowering=False)
    KC, KD = D_MODEL // 128, D_FF // 128
    NT = L // TOK
    I = "ExternalInput"
    xin = DT(nc, "xin", [D_MODEL, L + 2], F32, I)
    gain1 = DT(nc, "gain1", [depth, 128, KC], F32, I)
    w_in = DT(nc, "w_in", [depth, D_MODEL, D_IN], F32, I)
    dn_cw = DT(nc, "dn_cw", [depth, DN_H, 128, 3, 4], F32, I)
    dn_hp = DT(nc, "dn_hp", [depth, DN_H, 128, 2], F32, I)
    dn_nw = DT(nc, "dn_nw", [depth, 128, 1], F32, I)
    lru_pv = DT(nc, "lru_pv", [depth, DN_H, 128, 9], F32, I)
    lru_w = DT(nc, "lru_w", [depth, DN_H, 2, 128, 128], F32, I)
    sg_wT = DT(nc, "sg_wT", [depth, SG_W // 128, 128, 128], F32, I)
    sg_bs = DT(nc, "sg_bs", [depth, 128, SG_W // 128], F32, I)
    sg_vec = DT(nc, "sg_vec", [depth, 3, SG_W], F32, I)
    w_out = DT(nc, "w_out", [depth, D_MODEL, D_MODEL], F32, I)
    gain2 = DT(nc, "gain2", [depth, 128, KC], F32, I)
    w_up = DT(nc, "w_up", [depth, D_MODEL, 2 * D_FF], F32, I)
    cw = DT(nc, "cw", [depth, 128, 2 * KD, 3], F32, I)
    cb = DT(nc, "cb", [depth, 128, 2 * KD], F32, I)
    w_down = DT(nc, "w_down", [depth, D_FF, D_MODEL], F32, I)
    gainf = DT(nc, "gainf", [128, KC], F32, I)
    consts = DT(nc, "consts", [128, NCONST, 128], F32, I)
    out = DT(nc, "out", [D_MODEL, L], F32, "ExternalOutput")
    projT = DT(nc, "projT", [D_IN, L], F32, "Internal")
    mixT = DT(nc, "mixT", [D_MODEL, L + 2], F32, "Internal")
    xa = DT(nc, "xa", [D_MODEL, L + 2], F32, "Internal")
    xmid = DT(nc, "xmid", [D_MODEL, TOK + 2], F32, "Internal")
    actT = DT(nc, "actT", [D_FF, TOK], BF16, "Internal")
    blocks = p1_blocks()
    with contextlib.ExitStack() as st:
        cx = Ctx(nc, st)
        S = cx.S
        consts_sb, Bconst = p2_common(cx, consts)
        mz = cx.mark()
        z = cx.sb("zero", [128, KC, 2], F32)
        Bz = Buf("zero")
        S.op("dve", lambda e: e.memset(z, 0.0), writes=[Bz])
        for tdst in (mixT, xa):
            S.dma("sp", lambda e, tdst=tdst: e.dma_start(out=tdst.ap[:, 0:2].rearrange("(k p) c -> p k c", p=128), in_=z),
                  reads=[Bz], writes=[tdst.buf])
        cx.release(mz)
        X = xin
        for l in range(depth):
            final = (l == depth - 1)
            for t in range(NT):
                emit_p1(cx, V(X.ap[:, 2 + t * TOK:2 + (t + 1) * TOK]), V(gain1.ap[l]), V(w_in.ap[l]),
                        V(projT.ap[:, t * TOK:(t + 1) * TOK]), KC, TOK, blocks, tag="p1_%d_%d" % (l, t))
            S.barrier()
            pj = projT.ap
            lru_in = V(None, get=lambda u, w: pj[6168 + w * LRU_W + u * 128: 6168 + w * LRU_W + (u + 1) * 128, :])
            lru_out = V(None, get=lambda u: mixT.ap[DN_W + u * 128: DN_W + (u + 1) * 128, 2:2 + L])
            emit_lru(cx, lru_in, V(lru_pv.ap[l]), V(lru_w.ap[l]), lru_out, consts_sb, Bconst, NU=DN_H, L=L, tag="lru%d" % l)
            emit_sg(cx, None, V(sg_wT.ap[l]), V(sg_bs.ap[l]), V(sg_vec.ap[l]), V(mixT.ap[DN_W + LRU_W:, 2:2 + L]),
                    consts_sb, Bconst, T=L, tag="sg%d" % l, fm_src=V(pj[9240:11288, :]))
            dn_in = V(None, get=lambda u, part: pj[part * DN_W + u * 128: part * DN_W + (u + 1) * 128, :])
            dn_ab = V(None, get_row=lambda u, ri: pj[(6156 if ri == 0 else 6144) + u, :])
            dn_out = V(None, get=lambda u: mixT.ap[u * 128:(u + 1) * 128, 2:2 + L])
            emit_dn2(cx, dn_in, dn_ab, V(dn_cw.ap[l]), V(dn_hp.ap[l]), V(dn_nw.ap[l]), dn_out, consts_sb, Bconst, NU=DN_H, L=L, tag="dn%d" % l)
            S.barrier()
            for t in range(NT):
                if final:
                    o = V(out.ap[:, t * TOK:(t + 1) * TOK], is_out=True)
                else:
                    o = V(xa.ap[:, 2 + t * TOK:2 + (t + 1) * TOK])
                emit_p3(cx, V(X.ap[:, t * TOK:t * TOK + TOK + 2]), V(mixT.ap[:, t * TOK:t * TOK + TOK + 2]), V(w_out.ap[l]), V(gain2.ap[l]),
                        V(w_up.ap[l]), V(cw.ap[l]), V(cb.ap[l]), V(w_down.ap[l]), o, xmid, actT, TOK,
                        gainf=(gainf if final else None), tag="p3_%d_%d" % (l, t))
                S.barrier()
            X = xa
        S.finish("sp")
        S.emit()
    return nc


def kernel_fused(x, norm_mix, w_in, dn_conv_w, dn_a_log, dn_dt_bias, dn_norm_w,
                 lru_conv_w, lru_conv_b, lru_w_a, lru_b_a, lru_w_x, lru_b_x, lru_lambda, lru_norm_w,
                 sg_ln_w, sg_ln_b, sg_w_s, sg_b_s, sg_norm_w, w_out,
                 norm_ffn, w_up, ffn_conv_w, ffn_conv_b, w_down, norm_final):
    f32 = np.float32
    A = lambda v: np.ascontiguousarray(np.asarray(v, f32))
    x = np.asarray(x, f32)
    KC, KD = D_MODEL // 128, D_FF // 128
    nc = _prog("fused", build_fused)
    shared = {
        "gain1": A(np.stack([_lay(norm_mix[l], KC) for l in range(DEPTH)])),
        "w_in": A(w_in),
        "dn_cw": A(np.asarray(dn_conv_w, f32).reshape(DEPTH, 4, 3, DN_H, 128).transpose(0, 3, 4, 2, 1)),
        "dn_hp": A(np.stack([np.broadcast_to(np.asarray(dn_a_log, f32)[:, :, None], (DEPTH, DN_H, 128)),
                             np.broadcast_to(np.asarray(dn_dt_bias, f32)[:, :, None], (DEPTH, DN_H, 128))], axis=-1)),
        "dn_nw": A(np.asarray(dn_norm_w, f32)[:, :, None]),
        "lru_pv": A(np.concatenate([np.asarray(lru_conv_w, f32).reshape(DEPTH, 4, DN_H, 128).transpose(0, 2, 3, 1)] + [
            np.asarray(v, f32).reshape(DEPTH, DN_H, 128, 1) for v in (lru_conv_b, lru_b_a, lru_b_x, lru_lambda, lru_norm_w)], axis=-1)),
        "lru_w": A(np.stack([np.asarray(lru_w_a, f32), np.asarray(lru_w_x, f32)], axis=2)),
        "sg_wT": A(np.asarray(sg_w_s, f32).transpose(0, 1, 3, 2)),
        "sg_bs": A(np.asarray(sg_b_s, f32).transpose(0, 2, 1)),
        "sg_vec": A(np.stack([np.asarray(v, f32) for v in (sg_ln_w, sg_ln_b, sg_norm_w)], axis=1)),
        "w_out": A(w_out),
        "gain2": A(np.stack([_lay(norm_ffn[l], KC) for l in range(DEPTH)])),
        "w_up": A(w_up),
        "cw": A(np.asarray(ffn_conv_w, f32).reshape(DEPTH, 3, 2 * KD, 128).transpose(0, 3, 2, 1)),
        "cb": A(np.stack([_lay(ffn_conv_b[l], 2 * KD) for l in range(DEPTH)])),
        "w_down": A(w_down),
        "gainf": _lay(norm_final, KC),
        "consts": make_consts(),
    }
    xpad = []
    for b in range(BATCH):
        xp = np.zeros((D_MODEL, SEQ + 2), f32)
        xp[:, 2:] = x[b].T
        xpad.append(xp)
    maps = []
    for c in range(NCORES):
        m = dict(shared)
        m["xin"] = xpad[c % BATCH]
        maps.append(m)
    res = run_bass_kernel_spmd(nc, maps, core_ids=list(range(NCORES))).results
    return np.ascontiguousarray(np.stack([res[b]["out"].T for b in range(BATCH)])).astype(f32)


kernel = kernel_fused


def _roundrobin(gens):
    gens = list(gens)
    while gens:
        nxt = []
        for g in gens:
            try:
                next(g)
                nxt.append(g)
                yield
            except StopIteration:
                pass
        gens = nxt


def emit_dn2(cx, dn_in, dn_ab, dn_cw, dn_hp, dn_nw, dn_out, consts_sb, Bconst, NU=3, L=SEQ, tag="dn", GU=3):
    nc, S = cx.nc, cx.S
    S.barrier()
    m0 = cx.mark()
    SEG = 512
    NP = SEG // 128
    NSEG = L // SEG
    ident = consts_sb[:, C_ID, :]
    ones = consts_sb[:, C_ONES, :]
    QP = QPool(cx, [0, 1, 2, 3, 4, 5])
    BANK = [6, 7]
    bank_i = [0]

    def full_bank():
        b = BANK[bank_i[0] % 2]
        bank_i[0] += 1
        return cx.ps[b], cx.Bps[b]

    nw = cx.sb(tag + "nw", [128, 1], F32)
    Bnw = Buf("dnw")
    S.dma("sp", lambda e: e.dma_start(out=nw, in_=dn_nw.ap), reads=[dn_nw.buf], writes=[Bnw])
    NSC = 14
    (A_, BETA, NBETA, G_, GC, GL, GL0, GL1, EG, KBD, KDEC, D0, D1, TMP) = range(NSC)

    def mkslot(i):
        d = {}
        for nm in ("qf", "kf", "vf", "zs", "sqb", "rn", "oT", "QdT"):
            d[nm] = cx.sb("%s%s%d" % (tag, nm, i), [128, SEG], F32)
        for nm in ("Kbd", "Kdec", "Vb", "attnT", "U", "WT", "otm", "sqo"):
            d[nm] = cx.sb("%s%s%d" % (tag, nm, i), [128, NP, 128], F32)
        d["sc"] = cx.sb("%ssc%d" % (tag, i), [128, NSC, NP], F32)
        d["ab"] = cx.sb("%sab%d" % (tag, i), [128, 2, NP], F32)
        d["sso"] = cx.sb("%ssso%d" % (tag, i), [128, NP], F32)
        d["pad"] = [cx.sb("%spad%d_%d" % (tag, i, k), [128, SEG + 3], F32) for k in range(2)]
        d["cw"] = cx.sb("%scw%d" % (tag, i), [128, 3, 4], F32)
        d["hp"] = cx.sb("%shp%d" % (tag, i), [128, 4], F32)
        d["S"] = cx.sb("%sS%d" % (tag, i), [128, 128], F32)
        d["vn"] = [cx.sb("%svn%d_%d" % (tag, i, k), [128, 128], F32) for k in range(2)]
        d["tmp"] = [[cx.sb("%st%d_%d_%d" % (tag, i, n, k), [128, 128], F32) for k in range(8)] for n in range(NP)]
        return d

    def fresh_bufs(d):
        for nm in ("qf", "kf", "vf", "zs", "sqb", "rn", "oT", "ab", "sso", "cw", "hp", "S"):
            d["B" + nm] = Buf(nm)
        for nm in ("Kbd", "Kdec", "Vb", "attnT", "U", "WT", "otm", "sqo", "QdT"):
            d["B" + nm] = [Buf("%s%d" % (nm, n)) for n in range(NP)]
        d["Bsc"] = [Buf("sc%d" % k) for k in range(NSC)]
        d["Bpad"] = [Buf("pad0"), Buf("pad1")]
        d["Bvn"] = [Buf("vn0"), Buf("vn1")]
        d["Btmp"] = [[Buf("t%d_%d" % (n, k)) for k in range(8)] for n in range(NP)]

    slots = [mkslot(i) for i in range(min(GU, NU))]

    def pc_gen(d, n):
        sc, Bsc = d["sc"], d["Bsc"]
        blk = slice(n * 128, (n + 1) * 128)
        gcc = sc[:, GC, n:n + 1]
        T, BT = d["tmp"][n], d["Btmp"][n]
        X0, dL, dU, Bm, Bt, nB, nBt, Pt = T
        BX0, BdL, BdU, BBm, BBt, BnB, BnBt, BPt = BT
        S.op("dve", lambda e: e.tensor_scalar(out=X0, in0=ident, scalar1=gcc, scalar2=None, op0=ALU.mult),
             reads=[Bconst, Bsc[GC]], writes=[BX0])
        pA, BpA = QP.next()
        S.op("pe", lambda e: e.matmul(pA, lhsT=ones, rhs=X0, start=True, stop=True), reads=[Bconst, BX0], writes=[BpA])
        S.op("dve", lambda e: e.scalar_tensor_tensor(out=dL, in0=pA, scalar=gcc, in1=consts_sb[:, C_MLS, :], op0=ALU.subtract, op1=ALU.add),
             reads=[BpA, Bsc[GC], Bconst], writes=[BdL])
        S.op("act", lambda e: e.activation(out=dL, in_=dL, func=AF.Exp, scale=-1.0), reads=[BdL], writes=[BdL])
        S.op("dve", lambda e: e.scalar_tensor_tensor(out=dU, in0=pA, scalar=gcc, in1=consts_sb[:, C_MU, :], op0=ALU.subtract, op1=ALU.add),
             reads=[BpA, Bsc[GC], Bconst], writes=[BdU])
        S.op("act", lambda e: e.activation(out=dU, in_=dU, func=AF.Exp), reads=[BdU], writes=[BdU])
        S.op("act", lambda e: e.activation(out=X0, in_=pA, func=AF.Exp), reads=[BpA], writes=[BX0])
        S.op("dve", lambda e: e.tensor_tensor(out=d["QdT"][:, blk], in0=d["qf"][:, blk], in1=X0, op=ALU.mult),
             reads=[d["Bqf"], BX0], writes=[d["BQdT"][n]])
        yield
        pG, BpG = QP.next()
        S.op("pe", lambda e: e.matmul(pG, lhsT=d["kf"][:, blk], rhs=d["kf"][:, blk], start=True, stop=True), reads=[d["Bkf"]], writes=[BpG])
        S.op("dve", lambda e: e.scalar_tensor_tensor(out=Bm, in0=pG, scalar=sc[:, NBETA, n:n + 1], in1=dL, op0=ALU.mult, op1=ALU.mult),
             reads=[BpG, Bsc[NBETA], BdL], writes=[BBm])
        pQ, BpQ = QP.next()
        S.op("pe", lambda e: e.matmul(pQ, lhsT=d["kf"][:, blk], rhs=d["qf"][:, blk], start=True, stop=True),
             reads=[d["Bkf"], d["Bqf"]], writes=[BpQ])
        S.op("dve", lambda e: e.tensor_tensor(out=d["attnT"][:, n, :], in0=pQ, in1=dU, op=ALU.mult), reads=[BpQ, BdU], writes=[d["BattnT"][n]])
        yield
        pT, BpT = QP.next()
        S.op("pe", lambda e: e.transpose(out=pT, in_=Bm, identity=ident), reads=[BBm, Bconst], writes=[BpT])
        S.op("act", lambda e: e.copy(out=Bt, in_=pT), reads=[BpT], writes=[BBt])
        S.op("dve", lambda e: e.tensor_tensor(out=Pt, in0=Bt, in1=ident, op=ALU.add), reads=[BBt, Bconst], writes=[BPt])
        yield
        cur = (Bm, BBm, Bt, BBt)
        nxt = (nB, BnB, nBt, BnBt)
        for lvl in range(1, 6):
            cB, BcB, cBt, BcBt = cur
            tB, BtB, tBt, BtBt = nxt
            p1, Bp1 = QP.next()
            S.op("pe", lambda e, p1=p1, cBt=cBt, cB=cB: e.matmul(p1, lhsT=cBt, rhs=cB, start=True, stop=True), reads=[BcBt, BcB], writes=[Bp1])
            S.op("act", lambda e, tB=tB, p1=p1: e.copy(out=tB, in_=p1), reads=[Bp1], writes=[BtB])
            if lvl < 5:
                p2, Bp2 = QP.next()
                S.op("pe", lambda e, p2=p2, cBt=cBt, cB=cB: e.matmul(p2, lhsT=cB, rhs=cBt, start=True, stop=True), reads=[BcBt, BcB], writes=[Bp2])
                S.op("dve", lambda e, tBt=tBt, p2=p2: e.tensor_copy(out=tBt, in_=p2), reads=[Bp2], writes=[BtBt])
            yield
            p3, Bp3 = QP.next()
            S.op("pe", lambda e, p3=p3, tB=tB: e.matmul(p3, lhsT=tB, rhs=Pt, start=True, stop=True), reads=[BtB, BPt], writes=[Bp3])
            S.op("dve", lambda e, p3=p3: e.tensor_tensor(out=Pt, in0=Pt, in1=p3, op=ALU.add), reads=[BPt, Bp3], writes=[BPt])
            yield
            cur, nxt = nxt, cur
        pU, BpU = QP.next()
        for hb in range(2):
            P = slice(hb * 64, hb * 64 + 64)
            S.op("pe", lambda e, P=P: e.matmul(pU[P, :], lhsT=Pt[P, P], rhs=d["Vb"][P, n, :], start=True, stop=True),
                 reads=[BPt, d["BVb"][n]], writes=[BpU])
        S.op("act", lambda e: e.copy(out=d["U"][:, n, :], in_=pU), reads=[BpU], writes=[d["BU"][n]])
        pW, BpW = QP.next()
        S.op("pe", lambda e: e.matmul(pW, lhsT=d["Kbd"][:, n, :], rhs=Pt, start=True, stop=True), reads=[BPt, d["BKbd"][n]], writes=[BpW])
        S.op("dve", lambda e: e.tensor_copy(out=d["WT"][:, n, :], in_=pW), reads=[BpW], writes=[d["BWT"][n]])
        yield

    def unit_gen(d, u):
        fresh_bufs(d)
        cwsb, hp, S_sb = d["cw"], d["hp"], d["S"]
        Bcw, Bhp, BS = d["Bcw"], d["Bhp"], d["BS"]
        sc, Bsc = d["sc"], d["Bsc"]
        S.dma("sp", lambda e: e.dma_start(out=cwsb, in_=dn_cw.ap[u]), reads=[dn_cw.buf], writes=[Bcw])
        S.dma("sp", lambda e: e.dma_start(out=hp[:, 0:2], in_=dn_hp.ap[u]), reads=[dn_hp.buf], writes=[Bhp])
        S.op("act", lambda e: e.activation(out=hp[:, 2:3], in_=hp[:, 0:1], func=AF.Exp), reads=[Bhp], writes=[Bhp])
        S.op("dve", lambda e: e.tensor_scalar(out=hp[:, 2:3], in0=hp[:, 2:3], scalar1=-1.0, scalar2=None, op0=ALU.mult), reads=[Bhp], writes=[Bhp])
        S.op("dve", lambda e: e.memset(S_sb, 0.0), writes=[BS])
        yield
        padi = 0
        for seg in range(NSEG):
            s0 = seg * SEG
            for part, nm in ((0, "qf"), (1, "kf"), (2, "vf")):
                pad, Bpad = d["pad"][padi % 2], d["Bpad"][padi % 2]
                padi += 1
                dst, Bdst = d[nm], d["B" + nm]
                if seg == 0:
                    S.op("dve", lambda e, pad=pad: e.memset(pad[:, 0:3], 0.0), writes=[Bpad])
                    S.dma("sp", lambda e, pad=pad, part=part: e.dma_start(out=pad[:, 3:SEG + 3], in_=dn_in.get(u, part)[:, 0:SEG]),
                          reads=[dn_in.buf], writes=[Bpad])
                else:
                    S.dma("sp", lambda e, pad=pad, part=part, s0=s0: e.dma_start(out=pad, in_=dn_in.get(u, part)[:, s0 - 3:s0 + SEG]),
                          reads=[dn_in.buf], writes=[Bpad])
                S.op("dve", lambda e, pad=pad, dst=dst, part=part: e.tensor_scalar(
                    out=dst, in0=pad[:, 0:SEG], scalar1=cwsb[:, part, 0:1], scalar2=None, op0=ALU.mult),
                    reads=[Bpad, Bcw], writes=[Bdst])
                for j in range(1, 4):
                    S.op("dve", lambda e, pad=pad, dst=dst, part=part, j=j: e.scalar_tensor_tensor(
                        out=dst, in0=pad[:, j:j + SEG], scalar=cwsb[:, part, j:j + 1], in1=dst, op0=ALU.mult, op1=ALU.add),
                        reads=[Bpad, Bcw, Bdst], writes=[Bdst])
                S.op("act", lambda e, dst=dst: e.activation(out=dst, in_=dst, func=AF.Silu), reads=[Bdst], writes=[Bdst])
                if part < 2:
                    S.op("act", lambda e, dst=dst: e.activation(out=d["sqb"], in_=dst, func=AF.Square), reads=[Bdst], writes=[d["Bsqb"]])
                    ps, Bp = full_bank()
                    S.op("pe", lambda e, ps=ps: e.matmul(ps[:], lhsT=ones, rhs=d["sqb"], start=True, stop=True),
                         reads=[Bconst, d["Bsqb"]], writes=[Bp])
                    S.op("act", lambda e, ps=ps: e.activation(out=d["rn"], in_=ps[:], func=AF.Sqrt, bias=cx.epsb[:, 0:1], scale=1.0),
                         reads=[Bp, cx.Beps], writes=[d["Brn"]])
                    S.op("dve", lambda e: e.reciprocal(out=d["rn"], in_=d["rn"]), reads=[d["Brn"]], writes=[d["Brn"]])
                    qs = (HD ** -0.5) if part == 0 else 1.0
                    S.op("dve", lambda e, dst=dst, qs=qs: e.scalar_tensor_tensor(
                        out=dst, in0=dst, scalar=qs, in1=d["rn"], op0=ALU.mult, op1=ALU.mult),
                        reads=[Bdst, d["Brn"]], writes=[Bdst])
                yield
            S.dma("sp", lambda e, s0=s0: e.dma_start(out=d["zs"], in_=dn_in.get(u, 3)[:, s0:s0 + SEG]), reads=[dn_in.buf], writes=[d["Bzs"]])
            S.op("act", lambda e: e.activation(out=d["zs"], in_=d["zs"], func=AF.Silu), reads=[d["Bzs"]], writes=[d["Bzs"]])
            if hasattr(dn_ab, "get_row"):
                for ri in range(2):
                    S.dma("sp", lambda e, s0=s0, ri=ri: e.dma_start(
                        out=d["ab"][:, ri, :], in_=dn_ab.get_row(u, ri)[s0:s0 + SEG].rearrange("(n p) -> p n", p=128),
                        allow_slow_non_contiguous=True), reads=[dn_ab.buf], writes=[d["Bab"]])
            else:
                S.dma("sp", lambda e, seg=seg: e.dma_start(out=d["ab"], in_=dn_ab.ap[u][:, :, seg * NP:(seg + 1) * NP]),
                      reads=[dn_ab.buf], writes=[d["Bab"]])
            S.op("act", lambda e: e.activation(out=sc[:, TMP, :], in_=d["ab"][:, 0, :], func=AF.Exp, bias=hp[:, 1:2], scale=1.0),
                 reads=[d["Bab"], Bhp], writes=[Bsc[TMP]])
            S.op("act", lambda e: e.activation(out=sc[:, TMP, :], in_=sc[:, TMP, :], func=AF.Ln, bias=ones[:, 0:1], scale=1.0),
                 reads=[Bsc[TMP], Bconst], writes=[Bsc[TMP]])
            S.op("dve", lambda e: e.tensor_scalar(out=sc[:, G_, :], in0=sc[:, TMP, :], scalar1=hp[:, 2:3], scalar2=None, op0=ALU.mult),
                 reads=[Bsc[TMP], Bhp], writes=[Bsc[G_]])
            S.op("act", lambda e: e.activation(out=sc[:, BETA, :], in_=d["ab"][:, 1, :], func=AF.Sigmoid), reads=[d["Bab"]], writes=[Bsc[BETA]])
            S.op("dve", lambda e: e.tensor_scalar(out=sc[:, NBETA, :], in0=sc[:, BETA, :], scalar1=-1.0, scalar2=None, op0=ALU.mult),
                 reads=[Bsc[BETA]], writes=[Bsc[NBETA]])
            pq, Bq = QP.next()
            S.op("pe", lambda e, pq=pq: e.matmul(pq[:, 0:NP], lhsT=consts_sb[:, C_TRI, :], rhs=sc[:, G_, :], start=True, stop=True),
                 reads=[Bconst, Bsc[G_]], writes=[Bq])
            S.op("dve", lambda e, pq=pq: e.tensor_copy(out=sc[:, GC, :], in_=pq[:, 0:NP]), reads=[Bq], writes=[Bsc[GC]])
            yield
            for ci, slot in ((C_SELEND, GL), (C_SEL63, GL0), (C_SEL127, GL1)):
                pq, Bq = QP.next()
                S.op("pe", lambda e, pq=pq, ci=ci: e.matmul(pq[:, 0:NP], lhsT=consts_sb[:, ci, :], rhs=sc[:, GC, :], start=True, stop=True),
                     reads=[Bconst, Bsc[GC]], writes=[Bq])
                S.op("dve", lambda e, pq=pq, slot=slot: e.tensor_copy(out=sc[:, slot, :], in_=pq[:, 0:NP]), reads=[Bq], writes=[Bsc[slot]])
            S.op("act", lambda e: e.activation(out=sc[:, EG, :], in_=sc[:, GC, :], func=AF.Exp), reads=[Bsc[GC]], writes=[Bsc[EG]])
            S.op("dve", lambda e: e.tensor_tensor(out=sc[:, KBD, :], in0=sc[:, BETA, :], in1=sc[:, EG, :], op=ALU.mult),
                 reads=[Bsc[BETA], Bsc[EG]], writes=[Bsc[KBD]])
            S.op("dve", lambda e: e.tensor_tensor(out=sc[:, KDEC, :], in0=sc[:, GL, :], in1=sc[:, GC, :], op=ALU.subtract),
                 reads=[Bsc[GL], Bsc[GC]], writes=[Bsc[KDEC]])
            S.op("act", lambda e: e.activation(out=sc[:, KDEC, :], in_=sc[:, KDEC, :], func=AF.Exp), reads=[Bsc[KDEC]], writes=[Bsc[KDEC]])
            S.op("act", lambda e: e.activation(out=sc[:, D0, :], in_=sc[:, GL0, :], func=AF.Exp), reads=[Bsc[GL0]], writes=[Bsc[D0]])
            S.op("act", lambda e: e.activation(out=sc[:, D1, :], in_=sc[:, GL1, :], func=AF.Exp), reads=[Bsc[GL1]], writes=[Bsc[D1]])
            yield
            for n in range(NP):
                blk = slice(n * 128, (n + 1) * 128)
                pq, Bq = QP.next()
                S.op("pe", lambda e, pq=pq, blk=blk: e.transpose(out=pq, in_=d["kf"][:, blk], identity=ident), reads=[d["Bkf"], Bconst], writes=[Bq])
                S.op("dve", lambda e, pq=pq, n=n: e.tensor_scalar(out=d["Kbd"][:, n, :], in0=pq, scalar1=sc[:, KBD, n:n + 1], scalar2=None, op0=ALU.mult),
                     reads=[Bq, Bsc[KBD]], writes=[d["BKbd"][n]])
                S.op("dve", lambda e, pq=pq, n=n: e.tensor_scalar(out=d["Kdec"][:, n, :], in0=pq, scalar1=sc[:, KDEC, n:n + 1], scalar2=None, op0=ALU.mult),
                     reads=[Bq, Bsc[KDEC]], writes=[d["BKdec"][n]])
                pq, Bq = QP.next()
                S.op("pe", lambda e, pq=pq, blk=blk: e.transpose(out=pq, in_=d["vf"][:, blk], identity=ident), reads=[d["Bvf"], Bconst], writes=[Bq])
                S.op("dve", lambda e, pq=pq, n=n: e.tensor_scalar(out=d["Vb"][:, n, :], in0=pq, scalar1=sc[:, BETA, n:n + 1], scalar2=None, op0=ALU.mult),
                     reads=[Bq, Bsc[BETA]], writes=[d["BVb"][n]])
                yield
            yield from _roundrobin([pc_gen(d, n) for n in range(NP)])
            for c in range(2 * NP):
                n, hb = c // 2, c % 2
                P = slice(hb * 64, hb * 64 + 64)
                cols = slice(n * 128 + hb * 64, n * 128 + hb * 64 + 64)
                pa, Bpa = QP.next()
                S.op("pe", lambda e, pa=pa, n=n, P=P: e.matmul(pa[P, :], lhsT=d["WT"][:, n, P], rhs=S_sb, start=True, stop=True),
                     reads=[d["BWT"][n], BS], writes=[Bpa])
                vn, Bvn = d["vn"][c % 2], d["Bvn"][c % 2]
                S.op("dve", lambda e, vn=vn, pa=pa, n=n, P=P: e.tensor_tensor(out=vn[P, :], in0=d["U"][P, n, :], in1=pa[P, :], op=ALU.subtract),
                     reads=[d["BU"][n], Bpa], writes=[Bvn])
                po, Bpo = QP.next()
                S.op("pe", lambda e, po=po, cols=cols, P=P: e.matmul(po[P, :], lhsT=d["QdT"][:, cols], rhs=S_sb, start=True, stop=False),
                     reads=[d["BQdT"][n], BS], writes=[Bpo])
                S.op("pe", lambda e, po=po, n=n, P=P, vn=vn: e.matmul(po[P, :], lhsT=d["attnT"][P, n, P], rhs=vn[P, :], start=False, stop=True),
                     reads=[d["BattnT"][n], Bvn], writes=[Bpo])
                pS, BpS = QP.next()
                S.op("pe", lambda e, pS=pS, n=n, P=P, vn=vn: e.matmul(pS, lhsT=d["Kdec"][P, n, :], rhs=vn[P, :], start=True, stop=True),
                     reads=[d["BKdec"][n], Bvn], writes=[BpS])
                dslot = D0 if hb == 0 else D1
                S.op("dve", lambda e, pS=pS, dslot=dslot, n=n: e.scalar_tensor_tensor(
                    out=S_sb, in0=S_sb, scalar=sc[:, dslot, n:n + 1], in1=pS, op0=ALU.mult, op1=ALU.add),
                    reads=[BS, Bsc[dslot], BpS], writes=[BS])
                S.op("act", lambda e, po=po, n=n, P=P: e.copy(out=d["otm"][P, n, :], in_=po[P, :]), reads=[Bpo], writes=[d["Botm"][n]])
                yield
            S.op("act", lambda e: e.activation(out=d["sqo"], in_=d["otm"], func=AF.Square), reads=d["Botm"], writes=d["Bsqo"])
            S.op("dve", lambda e: e.tensor_reduce(out=d["sso"], in_=d["sqo"], axis=AX.X, op=ALU.add), reads=d["Bsqo"], writes=[d["Bsso"]])
            S.op("act", lambda e: e.activation(out=d["sso"], in_=d["sso"], func=AF.Sqrt, bias=cx.epsb[:, 0:1], scale=1.0 / HD),
                 reads=[d["Bsso"], cx.Beps], writes=[d["Bsso"]])
            S.op("dve", lambda e: e.reciprocal(out=d["sso"], in_=d["sso"]), reads=[d["Bsso"]], writes=[d["Bsso"]])
            S.op("dve", lambda e: e.tensor_tensor(out=d["otm"], in0=d["otm"], in1=d["sso"].unsqueeze(2).to_broadcast([128, NP, 128]), op=ALU.mult),
                 reads=d["Botm"] + [d["Bsso"]], writes=d["Botm"])
            yield
            for n in range(NP):
                blk = slice(n * 128, (n + 1) * 128)
                pT, BpT = QP.next()
                S.op("pe", lambda e, pT=pT, n=n: e.transpose(out=pT, in_=d["otm"][:, n, :], identity=ident), reads=[d["Botm"][n], Bconst], writes=[BpT])
                S.op("dve", lambda e, pT=pT, blk=blk: e.scalar_tensor_tensor(
                    out=d["oT"][:, blk], in0=pT, scalar=nw[:, 0:1], in1=d["zs"][:, blk], op0=ALU.mult, op1=ALU.mult),
                    reads=[BpT, Bnw, d["Bzs"]], writes=[d["BoT"]])
            S.dma("sp", lambda e, s0=s0: e.dma_start(out=dn_out.get(u)[:, s0:s0 + SEG], in_=d["oT"]),
                  reads=[d["BoT"]], writes=[dn_out.b((u, seg))], owner=d["BoT"], is_out=dn_out.is_out)
            yield

    for g0 in range(0, NU, GU):
        units = list(range(g0, min(NU, g0 + GU)))
        for _ in _roundrobin([unit_gen(slots[i], u) for i, u in enumerate(units)]):
            pass
    cx.release(m0)
```

```python
import contextlib
import numpy as np
import concourse.bass as bass
import concourse.mybir as mybir
from concourse.bass_utils import run_bass_kernel_spmd

F32 = mybir.dt.float32
BF16 = mybir.dt.bfloat16
AF = mybir.ActivationFunctionType
ALU = mybir.AluOpType
AX = mybir.AxisListType

D_MODEL = 4096
SEQ = 4096
BATCH = 2
DEPTH = 2
HD = 128
DN_W = 1536
LRU_W = 1536
SG_W = 1024
DN_H = 12
D_IN = 11288
D_FF = 11008
EPS = 1e-6
NCORES = 8
TOK = 1024


class Buf:
    __slots__ = ("name", "last_w", "reads", "dsem", "dcnt")

    def __init__(self, name=""):
        self.name = name
        self.last_w = None
        self.reads = {}
        self.dsem = None
        self.dcnt = 0


class Sched:
    ENGS = ("pe", "act", "dve", "pool", "sp")

    def __init__(self, nc, stack):
        self.nc = nc
        self.stack = stack
        self.sem = {}
        self.count = {}
        self.prog = {e: [] for e in self.ENGS}
        self.known = {e: {} for e in self.ENGS}
        for e in self.ENGS:
            self.sem[e] = stack.enter_context(nc.semaphore("s_" + e))
            self.count[e] = 0
        self.ndsem = 0
        self.dpool = {e: [] for e in self.ENGS}
        self.dq = {}
        self.dcount = {}
        self.downers = []
        self.out_events = []
        self.ninstr = 0

    def _need(self, eng, ev, waits):
        if ev is None:
            return
        k, v = ev
        if eng == "pe" and k == "pe":
            return
        if self.known[eng].get(k, 0) >= v:
            return
        if waits.get(k, 0) < v:
            waits[k] = v

    def _waits(self, eng, reads, writes):
        waits = {}
        for b in reads:
            self._need(eng, b.last_w, waits)
        for b in writes:
            self._need(eng, b.last_w, waits)
            for k, v in b.reads.items():
                self._need(eng, (k, v), waits)
        for k, v in waits.items():
            self.known[eng][k] = v
        return waits

    def _commit(self, ev, reads, writes):
        for b in writes:
            b.last_w = ev
            b.reads = {}
        k, v = ev
        for b in reads:
            if b.reads.get(k, 0) < v:
                b.reads[k] = v

    def op(self, eng, fn, reads=(), writes=(), inc=True):
        waits = self._waits(eng, reads, writes)
        self.ninstr += 1
        if inc:
            self.count[eng] += 1
            ev = (eng, self.count[eng])
            self.prog[eng].append((list(waits.items()), fn, (eng, 1)))
        else:
            ev = (eng, self.count[eng] + 1)
            self.prog[eng].append((list(waits.items()), fn, None))
        self._commit(ev, reads, writes)
        return ev

    def dma(self, q, fn, reads=(), writes=(), owner=None, is_out=False):
        if owner is None:
            owner = writes[0] if writes else reads[0]
        if owner.dsem is None:
            if self.dpool[q]:
                key = self.dpool[q].pop()
            else:
                key = "d%d" % self.ndsem
                self.ndsem += 1
                self.sem[key] = self.stack.enter_context(self.nc.semaphore("s_" + key))
                self.dcount[key] = 0
            owner.dsem = key
            self.dq[key] = q
            self.downers.append(owner)
        key = owner.dsem
        assert self.dq[key] == q, "a DMA semaphore is bound to one DMA queue type"
        waits = self._waits(q, reads, writes)
        if self.dcount[key] > 0:
            w2 = {}
            self._need(q, (key, 16 * self.dcount[key]), w2)
            for k, v in w2.items():
                if waits.get(k, 0) < v:
                    waits[k] = v
                self.known[q][k] = max(self.known[q].get(k, 0), v)
        self.dcount[key] += 1
        self.ninstr += 1
        ev = (key, 16 * self.dcount[key])
        self.prog[q].append((list(waits.items()), fn, (key, 16)))
        self._commit(ev, reads, writes)
        if is_out:
            self.out_events.append(ev)
        return ev

    def barrier(self):
        evs = [(e, self.count[e]) for e in self.ENGS if self.count[e] > 0]
        evs += [(k, 16 * c) for k, c in self.dcount.items() if c > 0]
        for eng in self.ENGS:
            waits = {}
            for ev in evs:
                self._need(eng, ev, waits)
            for k, v in waits.items():
                self.known[eng][k] = v
            if waits:
                self.prog[eng].append((list(waits.items()), None, None))
        for b in self.downers:
            self.dpool[self.dq[b.dsem]].append(b.dsem)
            b.dsem = None
        self.downers = []

    def finish(self, eng="sp"):
        waits = {}
        for ev in self.out_events:
            self._need(eng, ev, waits)
        self.prog[eng].append((list(waits.items()), None, None))

    def emit(self):
        nc = self.nc
        sems = self.sem

        def run(engobj, items):
            for waits, fn, inc in items:
                for k, v in waits:
                    engobj.wait_ge(sems[k], v)
                if fn is not None:
                    ins = fn(engobj)
                    if inc is not None:
                        ins.then_inc(sems[inc[0]], inc[1])

        with nc.Block() as block:
            @block.sync
            def _(e):
                run(e, self.prog["sp"])

            @block.tensor
            def _(e):
                run(e, self.prog["pe"])

            @block.scalar
            def _(e):
                run(e, self.prog["act"])

            @block.vector
            def _(e):
                run(e, self.prog["dve"])

            @block.gpsimd
            def _(e):
                run(e, self.prog["pool"])


class Ctx:
    def __init__(self, nc, stack):
        self.nc = nc
        self.st = stack
        self.S = Sched(nc, stack)
        self.ps = []
        self.Bps = []
        for i in range(8):
            self.ps.append(stack.enter_context(nc.psum_tensor("psb%d" % i, [128, 512], F32)))
            self.Bps.append(Buf("ps%d" % i))
        self.bank = 0
        self.ARENA = 206 * 1024
        self.arena = stack.enter_context(nc.sbuf_tensor("arena", [128, self.ARENA // 4], F32))
        self.off = 0

    def sb(self, name, shape, dt):
        esz = 2 if dt == BF16 else 4
        n = 1
        for d in shape[1:]:
            n *= d
        nbytes = (n * esz + 31) // 32 * 32
        assert self.off + nbytes <= self.ARENA, ("SBUF arena overflow", name, self.off, nbytes)
        v = self.arena[0:shape[0], self.off // 4:(self.off + nbytes) // 4]
        self.off += nbytes
        if dt != F32:
            v = v.bitcast(dt)
        v = v[:, 0:n]
        if len(shape) == 3:
            v = v.rearrange("p (a b) -> p a b", a=shape[1])
        elif len(shape) == 4:
            v = v.rearrange("p (a b c) -> p a b c", a=shape[1], b=shape[2])
        return v

    def mark(self):
        return self.off

    def release(self, mark):
        self.S.barrier()
        self.off = mark

    def next_bank(self, lo=0, hi=8):
        b = lo + self.bank % (hi - lo)
        self.bank += 1
        return b


def p1_blocks():
    blocks = []
    c = 0
    while c < 6144:
        blocks.append((c, [(c, 128), (c + 128, 128)]))
        c += 256
    blocks.append((6144, [(6144, 24), (6168, 128), (6296, 128)]))
    c = 6424
    while c < D_IN:
        blocks.append((c, [(c, 128), (c + 128, 128)]))
        c += 256
    assert c == D_IN
    return blocks


def emit_p1(cx, xT, gain, w, projT, KC, T, blocks, tag="p1"):
    nc, S = cx.nc, cx.S
    m_p1 = cx.mark()
    NT = T // 512
    G = 4
    NG = KC // G
    dmodel = KC * 128
    xv = xT.ap.rearrange("(k p) t -> p k t", p=128)
    wv = w.ap.rearrange("(k p) c -> p k c", p=128)

    gain_sb = cx.sb(tag + "gain", [128, KC], F32)
    Bgain = Buf("gain")
    S.dma("sp", lambda e: e.dma_start(out=gain_sb[:], in_=gain.ap), reads=[gain.buf], writes=[Bgain])
    ones = cx.sb(tag + "ones", [128, 128], BF16)
    Bones = Buf("ones")
    S.op("dve", lambda e: e.memset(ones[:], 1.0), writes=[Bones])
    epsb = cx.sb(tag + "eps", [128, 1], F32)
    Beps = Buf("eps")
    S.op("dve", lambda e: e.memset(epsb[:], EPS), writes=[Beps])

    hT = cx.sb(tag + "hT", [128, KC, T], BF16)
    BhT = [[Buf("hT%d_%d" % (tt, g)) for g in range(NG)] for tt in range(NT)]
    rstd = cx.sb(tag + "rstd", [128, T], F32)
    Brstd = [Buf("rstd%d" % tt) for tt in range(NT)]
    xst = [cx.sb(tag + "xst%d" % i, [128, G, 512], F32) for i in range(2)]
    Bxst = [Buf("xst%d" % i) for i in range(2)]
    sq = [cx.sb(tag + "sq%d" % i, [128, G, 512], BF16) for i in range(2)]
    Bsq = [Buf("sq%d" % i) for i in range(2)]

    it = 0
    for tt in range(NT):
        tsl = slice(tt * 512, (tt + 1) * 512)
        bss = 7
        for g in range(NG):
            s = it % 2
            it += 1
            ksl = slice(g * G, (g + 1) * G)
            S.dma("sp", lambda e, s=s, ksl=ksl, tsl=tsl: e.dma_start(out=xst[s][:], in_=xv[:, ksl, tsl]),
                  reads=[xT.buf], writes=[Bxst[s]])
            S.op("act", lambda e, s=s: e.activation(out=sq[s][:], in_=xst[s][:], func=AF.Square),
                 reads=[Bxst[s]], writes=[Bsq[s]])
            S.op("dve", lambda e, s=s, ksl=ksl, tsl=tsl: e.tensor_tensor(
                out=hT[:, ksl, tsl], in0=xst[s][:],
                in1=gain_sb[:, ksl].unsqueeze(2).to_broadcast([128, G, 512]), op=ALU.mult),
                reads=[Bxst[s], Bgain], writes=[BhT[tt][g]])
            for i in range(G):
                first = (g == 0 and i == 0)
                last = (g == NG - 1 and i == G - 1)
                S.op("pe", lambda e, s=s, i=i, first=first, last=last: e.matmul(
                    cx.ps[bss][:], lhsT=ones[:], rhs=sq[s][:, i, :], start=first, stop=last),
                    reads=[Bones, Bsq[s]], writes=[cx.Bps[bss]], inc=(i == G - 1))
        S.op("act", lambda e, tsl=tsl: e.activation(out=rstd[:, tsl], in_=cx.ps[bss][:], func=AF.Sqrt,
                                                      bias=epsb[:, 0:1], scale=1.0 / dmodel),
             reads=[cx.Bps[bss], Beps], writes=[Brstd[tt]])
        S.op("dve", lambda e, tsl=tsl: e.reciprocal(out=rstd[:, tsl], in_=rstd[:, tsl]),
             reads=[Brstd[tt]], writes=[Brstd[tt]])

    WMAX = max(sum(wd for _, wd in chunks) for _, chunks in blocks)
    NSLOT = 3
    wbuf = [cx.sb(tag + "w%d" % i, [128, KC, WMAX], BF16) for i in range(NSLOT)]
    Bw = [Buf("w%d" % i) for i in range(NSLOT)]
    NOST = 3
    ost = [cx.sb(tag + "ost%d" % i, [128, T], F32) for i in range(NOST)]
    Bost = [Buf("ost%d" % i) for i in range(NOST)]
    oi = 0
    for bi, (c0, chunks) in enumerate(blocks):
        s = bi % NSLOT
        wblk = sum(wd for _, wd in chunks)
        S.dma("pool", lambda e, s=s, c0=c0, wblk=wblk: e.dma_start(out=wbuf[s][:, :, 0:wblk], in_=wv[:, :, c0:c0 + wblk]),
              reads=[w.buf], writes=[Bw[s]])
        for (cs, cw) in chunks:
            o = oi % NOST
            oi += 1
            for tt in range(NT):
                tsl = slice(tt * 512, (tt + 1) * 512)
                b = cx.next_bank(0, 7)
                for kc in range(KC):
                    S.op("pe", lambda e, b=b, s=s, kc=kc, cs=cs, cw=cw, c0=c0, tsl=tsl: e.matmul(
                        cx.ps[b][0:cw, :], lhsT=wbuf[s][:, kc, cs - c0:cs - c0 + cw], rhs=hT[:, kc, tsl],
                        start=(kc == 0), stop=(kc == KC - 1)),
                        reads=[Bw[s], BhT[tt][kc // G]], writes=[cx.Bps[b]], inc=(kc == KC - 1))
                S.op("dve", lambda e, b=b, o=o, cw=cw, tsl=tsl: e.tensor_tensor(
                    out=ost[o][0:cw, tsl], in0=cx.ps[b][0:cw, :], in1=rstd[0:cw, tsl], op=ALU.mult),
                    reads=[cx.Bps[b], Brstd[tt]], writes=[Bost[o]])
            S.dma("sp", lambda e, o=o, cs=cs, cw=cw: e.dma_start(out=projT.ap[cs:cs + cw, :], in_=ost[o][0:cw, :]),
                  reads=[Bost[o]], writes=[projT.b(cs)], owner=Bost[o], is_out=projT.is_out)
    cx.release(m_p1)


class DT:
    def __init__(self, nc, name, shape, dt, kind):
        self.t = nc.dram_tensor(name, list(shape), dt, kind=kind)
        self.ap = self.t.ap()
        self.buf = Buf(name)
        self.is_out = (kind == "ExternalOutput")
        self.name = name
        self._b = {}

    def b(self, key):
        if key not in self._b:
            self._b[key] = Buf("%s_%s" % (self.name, key))
        return self._b[key]

    def get(self, *idx):
        a = self.ap
        for i in idx:
            a = a[i]
        return a


class V:
    def __init__(self, ap, is_out=False, get=None, get_row=None, name="v"):
        self.ap = ap
        self.buf = Buf(name)
        self.is_out = is_out
        self.name = name
        self._b = {}
        if get is not None:
            self.get = get
        if get_row is not None:
            self.get_row = get_row

    def b(self, key):
        if key not in self._b:
            self._b[key] = Buf("%s_%s" % (self.name, key))
        return self._b[key]

    def get(self, *idx):
        a = self.ap
        for i in idx:
            a = a[i]
        return a


def build_p1(KC=32, T=TOK, blocks=None, ncol=D_IN):
    if blocks is None:
        blocks = p1_blocks()
    nc = bass.Bass("TRN2", target_bir_lowering=False)
    xT = DT(nc, "xT", [KC * 128, T], F32, "ExternalInput")
    gain = DT(nc, "gain", [128, KC], F32, "ExternalInput")
    w = DT(nc, "w", [KC * 128, ncol], F32, "ExternalInput")
    projT = DT(nc, "projT", [ncol, T], F32, "ExternalOutput")
    with contextlib.ExitStack() as st:
        cx = Ctx(nc, st)
        emit_p1(cx, xT, gain, w, projT, KC, T, blocks)
        cx.S.finish("sp")
        cx.S.emit()
    return nc


def emit_p3(cx, xT, mixT, w_out, gain2, w_up, cw, cb, w_down, outT, xmid, actT, T, gainf=None, tag="p3",
            KC=32, KD=86):
    nc, S = cx.nc, cx.S
    TH = T + 2
    TW = TH // 3
    assert TW * 3 == TH and TW <= 512
    dmodel = KC * 128
    dff = KD * 128
    m_all = cx.mark()

    gain_sb = cx.sb(tag + "gain", [128, KC], F32)
    Bgain = Buf("gain2")
    S.dma("sp", lambda e: e.dma_start(out=gain_sb, in_=gain2.ap), reads=[gain2.buf], writes=[Bgain])
    ones = cx.sb(tag + "ones", [128, 128], BF16)
    Bones = Buf("ones")
    S.op("dve", lambda e: e.memset(ones, 1.0), writes=[Bones])
    epsb = cx.sb(tag + "eps", [128, 1], F32)
    Beps = Buf("eps")
    S.op("dve", lambda e: e.memset(epsb, EPS), writes=[Beps])
    cwsb = cx.sb(tag + "cw", [128, 2 * KD, 3], F32)
    cbsb = cx.sb(tag + "cb", [128, 2 * KD], F32)
    Bcw = Buf("cw")
    S.dma("sp", lambda e: e.dma_start(out=cwsb, in_=cw.ap), reads=[cw.buf], writes=[Bcw])
    Bcb = Buf("cb")
    S.dma("sp", lambda e: e.dma_start(out=cbsb, in_=cb.ap), reads=[cb.buf], writes=[Bcb])

    h2T = cx.sb(tag + "h2T", [128, KC, TH], BF16)
    Bh2 = [Buf("h2_%d" % c) for c in range(KC)]
    rstd2 = cx.sb(tag + "rstd2", [128, TH], F32)
    Brstd2 = Buf("rstd2")

    m1 = cx.mark()
    mT = cx.sb(tag + "mT", [128, KC, TH], BF16)
    NMG = KC // 4
    BmT = [Buf("mT%d" % g) for g in range(NMG)]
    mv = mixT.ap.rearrange("(k p) t -> p k t", p=128)
    for g in range(NMG):
        S.dma("pool", lambda e, g=g: e.dma_start(out=mT[:, 4 * g:4 * g + 4, :], in_=mv[:, 4 * g:4 * g + 4, :]),
              reads=[mixT.buf], writes=[BmT[g]])
    xst = [cx.sb(tag + "xst%d" % i, [128, TH], F32) for i in range(2)]
    Bxst = [Buf("xst%d" % i) for i in range(2)]
    xm = [cx.sb(tag + "xm%d" % i, [128, TH], F32) for i in range(2)]
    Bxm = [Buf("xm%d" % i) for i in range(2)]
    sq = [cx.sb(tag + "sq%d" % i, [128, TH], BF16) for i in range(2)]
    Bsq = [Buf("sq%d" % i) for i in range(2)]
    NSLOT = 3
    NSLOT1 = 2
    wbuf = [cx.sb(tag + "w1_%d" % i, [128, KC, 256], BF16) for i in range(NSLOT1)]
    Bw = [Buf("w1_%d" % i) for i in range(NSLOT1)]
    wv = w_out.ap.rearrange("(k p) c -> p k c", p=128)
    for blk in range(KC // 2):
        c0 = blk * 256
        s = blk % NSLOT1
        S.dma("pool", lambda e, s=s, c0=c0: e.dma_start(out=wbuf[s], in_=wv[:, :, c0:c0 + 256]),
              reads=[w_out.buf], writes=[Bw[s]])
        for ci in range(2):
            chunk = blk * 2 + ci
            i2 = chunk % 2
            rsl = slice(chunk * 128, (chunk + 1) * 128)
            S.dma("sp", lambda e, i2=i2, rsl=rsl: e.dma_start(out=xst[i2], in_=xT.ap[rsl, :]),
                  reads=[xT.buf], writes=[Bxst[i2]])
            for tt in range(3):
                tsl = slice(tt * TW, (tt + 1) * TW)
                b = cx.next_bank(0, 5)
                for kc in range(KC):
                    S.op("pe", lambda e, b=b, s=s, kc=kc, ci=ci, tsl=tsl: e.matmul(
                        cx.ps[b][:, 0:TW], lhsT=wbuf[s][:, kc, ci * 128:(ci + 1) * 128], rhs=mT[:, kc, tsl],
                        start=(kc == 0), stop=(kc == KC - 1)),
                        reads=[Bw[s], BmT[kc // 4]], writes=[cx.Bps[b]], inc=(kc == KC - 1))
                S.op("dve", lambda e, b=b, i2=i2, tsl=tsl: e.tensor_tensor(
                    out=xm[i2][:, tsl], in0=cx.ps[b][:, 0:TW], in1=xst[i2][:, tsl], op=ALU.add),
                    reads=[cx.Bps[b], Bxst[i2]], writes=[Bxm[i2]])
            S.op("act", lambda e, i2=i2: e.activation(out=sq[i2], in_=xm[i2], func=AF.Square),
                 reads=[Bxm[i2]], writes=[Bsq[i2]])
            for tt in range(3):
                tsl = slice(tt * TW, (tt + 1) * TW)
                S.op("pe", lambda e, tt=tt, i2=i2, tsl=tsl, chunk=chunk: e.matmul(
                    cx.ps[5 + tt][:, 0:TW], lhsT=ones, rhs=sq[i2][:, tsl], start=(chunk == 0), stop=(chunk == KC - 1)),
                    reads=[Bones, Bsq[i2]], writes=[cx.Bps[5 + tt]], inc=(tt == 2))
            S.op("dve", lambda e, i2=i2, chunk=chunk: e.tensor_scalar(
                out=h2T[:, chunk, :], in0=xm[i2], scalar1=gain_sb[:, chunk:chunk + 1], scalar2=None, op0=ALU.mult),
                reads=[Bxm[i2], Bgain], writes=[Bh2[chunk]])
            S.dma("sp", lambda e, i2=i2, rsl=rsl: e.dma_start(out=xmid.ap[rsl, :], in_=xm[i2]),
                  reads=[Bxm[i2]], writes=[xmid.b(chunk)], owner=Bxm[i2])
    for tt in range(3):
        tsl = slice(tt * TW, (tt + 1) * TW)
        S.op("act", lambda e, tt=tt, tsl=tsl: e.activation(out=rstd2[:, tsl], in_=cx.ps[5 + tt][:, 0:TW], func=AF.Sqrt,
                                                             bias=epsb[:, 0:1], scale=1.0 / dmodel),
             reads=[cx.Bps[5 + tt], Beps], writes=[Brstd2])
    S.op("dve", lambda e: e.reciprocal(out=rstd2, in_=rstd2), reads=[Brstd2], writes=[Brstd2])

    cx.release(m1)
    wbuf2 = [cx.sb(tag + "w2_%d" % i, [128, KC, 256], BF16) for i in range(NSLOT)]
    Bw2 = [[Buf("w2_%d_%d" % (i, h)) for h in range(2)] for i in range(NSLOT)]
    pre = [[cx.sb(tag + "pre%d%d" % (i, h), [128, TH], F32) for h in range(2)] for i in range(2)]
    Bpre = [[Buf("pre%d%d" % (i, h)) for h in range(2)] for i in range(2)]
    hid = [[cx.sb(tag + "hid%d%d" % (i, h), [128, T], F32) for h in range(2)] for i in range(2)]
    Bhid = [[Buf("hid%d%d" % (i, h)) for h in range(2)] for i in range(2)]
    asb = [cx.sb(tag + "asb%d" % i, [128, T], BF16) for i in range(2)]
    Basb = [Buf("asb%d" % i) for i in range(2)]
    wv2 = w_up.ap.rearrange("(k p) c -> p k c", p=128)
    for j in range(KD):
        s = j % NSLOT
        i2 = j % 2
        for half in range(2):
            S.dma("pool", lambda e, s=s, j=j, half=half: e.dma_start(
                out=wbuf2[s][:, :, half * 128:(half + 1) * 128],
                in_=wv2[:, :, half * dff + j * 128:half * dff + (j + 1) * 128]),
                reads=[w_up.buf], writes=[Bw2[s][half]])
        for half in range(2):
            for tt in range(3):
                tsl = slice(tt * TW, (tt + 1) * TW)
                b = cx.next_bank(0, 8)
                for kc in range(KC):
                    S.op("pe", lambda e, b=b, s=s, kc=kc, half=half, tsl=tsl: e.matmul(
                        cx.ps[b][:, 0:TW], lhsT=wbuf2[s][:, kc, half * 128:(half + 1) * 128], rhs=h2T[:, kc, tsl],
                        start=(kc == 0), stop=(kc == KC - 1)),
                        reads=[Bw2[s][half], Bh2[kc]], writes=[cx.Bps[b]], inc=(kc == KC - 1))
                S.op("dve", lambda e, b=b, i2=i2, half=half, tsl=tsl: e.tensor_tensor(
                    out=pre[i2][half][:, tsl], in0=cx.ps[b][:, 0:TW], in1=rstd2[:, tsl], op=ALU.mult),
                    reads=[cx.Bps[b], Brstd2], writes=[Bpre[i2][half]])
            col = half * KD + j
            P_, H_ = pre[i2][half], hid[i2][half]
            S.op("dve", lambda e, P_=P_, H_=H_, col=col: e.tensor_scalar(
                out=H_, in0=P_[:, 0:T], scalar1=cwsb[:, col, 0:1], scalar2=cbsb[:, col:col + 1], op0=ALU.mult, op1=ALU.add),
                reads=[Bpre[i2][half], Bcw, Bcb], writes=[Bhid[i2][half]])
            S.op("dve", lambda e, P_=P_, H_=H_, col=col: e.scalar_tensor_tensor(
                out=H_, in0=P_[:, 1:T + 1], scalar=cwsb[:, col, 1:2], in1=H_, op0=ALU.mult, op1=ALU.add),
                reads=[Bpre[i2][half], Bcw], writes=[Bhid[i2][half]])
            S.op("dve", lambda e, P_=P_, H_=H_, col=col: e.scalar_tensor_tensor(
                out=H_, in0=P_[:, 2:T + 2], scalar=cwsb[:, col, 2:3], in1=H_, op0=ALU.mult, op1=ALU.add),
                reads=[Bpre[i2][half], Bcw], writes=[Bhid[i2][half]])
        S.op("act", lambda e, i2=i2: e.activation(out=hid[i2][0], in_=hid[i2][0], func=AF.Silu),
             reads=[Bhid[i2][0]], writes=[Bhid[i2][0]])
        S.op("dve", lambda e, i2=i2: e.tensor_tensor(out=asb[i2], in0=hid[i2][0], in1=hid[i2][1], op=ALU.mult),
             reads=[Bhid[i2][0], Bhid[i2][1]], writes=[Basb[i2]])
        S.dma("sp", lambda e, i2=i2, j=j: e.dma_start(out=actT.ap[j * 128:(j + 1) * 128, :], in_=asb[i2]),
              reads=[Basb[i2]], writes=[actT.b(j)], owner=Basb[i2])

    cx.release(m_all)
    gainf_sb = None
    if gainf is not None:
        gainf_sb = cx.sb(tag + "gainf", [128, KC], F32)
        Bgf = Buf("gainf")
        S.dma("sp", lambda e: e.dma_start(out=gainf_sb, in_=gainf.ap), reads=[gainf.buf], writes=[Bgf])
        ones = cx.sb(tag + "ones3", [128, 128], BF16)
        Bones = Buf("ones3")
        S.op("dve", lambda e: e.memset(ones, 1.0), writes=[Bones])
        epsb = cx.sb(tag + "eps3", [128, 1], F32)
        Beps = Buf("eps3")
        S.op("dve", lambda e: e.memset(epsb, EPS), writes=[Beps])
        rstdf = cx.sb(tag + "rstdf", [128, 512], F32)
        Brf = Buf("rstdf")
        sq3 = [cx.sb(tag + "sq3_%d" % i, [128, 512], BF16) for i in range(2)]
        Bsq3 = [Buf("sq3_%d" % i) for i in range(2)]
    GK = 8
    NAG = (KD + GK - 1) // GK
    aT = cx.sb(tag + "aT", [128, KD, 512], BF16)
    BaT = [Buf("aT%d" % g) for g in range(NAG)]
    wbuf3 = [cx.sb(tag + "w3_%d" % i, [128, KD, 256], BF16) for i in range(2)]
    Bw3 = [Buf("w3_%d" % i) for i in range(2)]
    xms = [cx.sb(tag + "xms%d" % i, [128, 512], F32) for i in range(2)]
    Bxms = [Buf("xms%d" % i) for i in range(2)]
    ost = [cx.sb(tag + "ost%d" % i, [128, 512], F32) for i in range(2)]
    Bost = [Buf("ost%d" % i) for i in range(2)]
    av = actT.ap.rearrange("(k p) t -> p k t", p=128)
    wv3 = w_down.ap.rearrange("(k p) c -> p k c", p=128)
    for th in range(T // 512):
        csl = slice(th * 512, (th + 1) * 512)
        for g in range(NAG):
            k0, k1 = g * GK, min(KD, (g + 1) * GK)
            S.dma("sp", lambda e, k0=k0, k1=k1, csl=csl: e.dma_start(out=aT[:, k0:k1, :], in_=av[:, k0:k1, csl]),
                  reads=[actT.b(j) for j in range(k0, k1)], writes=[BaT[g]])
        for blk in range(KC // 2):
            c0 = blk * 256
            s = (th * (KC // 2) + blk) % 2
            S.dma("pool", lambda e, s=s, c0=c0: e.dma_start(out=wbuf3[s], in_=wv3[:, :, c0:c0 + 256]),
                  reads=[w_down.buf], writes=[Bw3[s]])
            for ci in range(2):
                chunk = blk * 2 + ci
                i2 = chunk % 2
                rsl = slice(chunk * 128, (chunk + 1) * 128)
                S.dma("sp", lambda e, i2=i2, rsl=rsl, th=th: e.dma_start(
                    out=xms[i2], in_=xmid.ap[rsl, 2 + th * 512:2 + (th + 1) * 512]),
                    reads=[xmid.b(chunk)], writes=[Bxms[i2]])
                b = cx.next_bank(0, 7)
                for k in range(KD):
                    S.op("pe", lambda e, b=b, s=s, k=k, ci=ci: e.matmul(
                        cx.ps[b][:], lhsT=wbuf3[s][:, k, ci * 128:(ci + 1) * 128], rhs=aT[:, k, :],
                        start=(k == 0), stop=(k == KD - 1)),
                        reads=[Bw3[s], BaT[k // GK]], writes=[cx.Bps[b]], inc=(k == KD - 1))
                S.op("dve", lambda e, b=b, i2=i2: e.tensor_tensor(out=ost[i2], in0=cx.ps[b][:], in1=xms[i2], op=ALU.add),
                     reads=[cx.Bps[b], Bxms[i2]], writes=[Bost[i2]])
                if gainf is not None:
                    S.op("act", lambda e, i2=i2: e.activation(out=sq3[i2], in_=ost[i2], func=AF.Square),
                         reads=[Bost[i2]], writes=[Bsq3[i2]])
                    S.op("pe", lambda e, i2=i2, chunk=chunk: e.matmul(
                        cx.ps[7][:], lhsT=ones, rhs=sq3[i2], start=(chunk == 0), stop=(chunk == KC - 1)),
                        reads=[Bones, Bsq3[i2]], writes=[cx.Bps[7]])
                S.dma("sp", lambda e, i2=i2, rsl=rsl, csl=csl: e.dma_start(out=outT.ap[rsl, csl], in_=ost[i2]),
                      reads=[Bost[i2]], writes=[outT.b((chunk, th))], owner=Bost[i2], is_out=outT.is_out)
        if gainf is not None:
            S.op("act", lambda e: e.activation(out=rstdf, in_=cx.ps[7][:], func=AF.Sqrt, bias=epsb[:, 0:1], scale=1.0 / dmodel),
                 reads=[cx.Bps[7], Beps], writes=[Brf])
            S.op("dve", lambda e: e.reciprocal(out=rstdf, in_=rstdf), reads=[Brf], writes=[Brf])
            for chunk in range(KC):
                i2 = chunk % 2
                rsl = slice(chunk * 128, (chunk + 1) * 128)
                S.dma("sp", lambda e, i2=i2, rsl=rsl, csl=csl: e.dma_start(out=xms[i2], in_=outT.ap[rsl, csl]),
                      reads=[outT.b((chunk, th))], writes=[Bxms[i2]])
                S.op("dve", lambda e, i2=i2, chunk=chunk: e.scalar_tensor_tensor(
                    out=ost[i2], in0=xms[i2], scalar=gainf_sb[:, chunk:chunk + 1], in1=rstdf, op0=ALU.mult, op1=ALU.mult),
                    reads=[Bxms[i2], Bgf, Brf], writes=[Bost[i2]])
                S.dma("sp", lambda e, i2=i2, rsl=rsl, csl=csl: e.dma_start(out=outT.ap[rsl, csl], in_=ost[i2]),
                      reads=[Bost[i2]], writes=[outT.b((chunk, th))], owner=Bost[i2], is_out=outT.is_out)
    cx.release(m_all)


def build_p3(T=TOK, KC=32, KD=86, final=False):
    nc = bass.Bass("TRN2", target_bir_lowering=False)
    d, dff = KC * 128, KD * 128
    xT = DT(nc, "xT", [d, T + 2], F32, "ExternalInput")
    mixT = DT(nc, "mixT", [d, T + 2], F32, "ExternalInput")
    w_out = DT(nc, "w_out", [d, d], F32, "ExternalInput")
    gain2 = DT(nc, "gain2", [128, KC], F32, "ExternalInput")
    w_up = DT(nc, "w_up", [d, 2 * dff], F32, "ExternalInput")
    cw = DT(nc, "cw", [128, 2 * KD, 3], F32, "ExternalInput")
    cb = DT(nc, "cb", [128, 2 * KD], F32, "ExternalInput")
    w_down = DT(nc, "w_down", [dff, d], F32, "ExternalInput")
    gainf = DT(nc, "gainf", [128, KC], F32, "ExternalInput") if final else None
    outT = DT(nc, "outT", [d, T], F32, "ExternalOutput")
    xmid = DT(nc, "xmid", [d, T + 2], F32, "Internal")
    actT = DT(nc, "actT", [dff, T], BF16, "Internal")
    with contextlib.ExitStack() as st:
        cx = Ctx(nc, st)
        emit_p3(cx, xT, mixT, w_out, gain2, w_up, cw, cb, w_down, outT, xmid, actT, T, gainf=gainf, KC=KC, KD=KD)
        cx.S.finish("sp")
        cx.S.emit()
    return nc


NCONST = 9
C_ID, C_ONES, C_TRI, C_SELEND, C_SEL63, C_SEL127, C_MLS, C_MU, C_UP = range(NCONST)


def make_consts():
    c = np.zeros((128, NCONST, 128), np.float32)
    i = np.arange(128)
    blk = i // 64
    same = blk[:, None] == blk[None, :]
    c[:, C_ID, :] = np.eye(128)
    c[:, C_ONES, :] = 1.0
    c[:, C_TRI, :] = (same & (i[:, None] <= i[None, :]))
    c[:, C_SELEND, :] = (i[:, None] == (blk[None, :] * 64 + 63))
    c[:, C_SEL63, :] = (i[:, None] == 63)
    c[:, C_SEL127, :] = (i[:, None] == 127)
    c[:, C_MLS, :] = np.where(same & (i[None, :] < i[:, None]), 0.0, 1e30)
    c[:, C_MU, :] = np.where(same & (i[None, :] >= i[:, None]), 0.0, -1e30)
    c[:, C_UP, :] = (i[None, :] >= i[:, None])
    return c


def emit_lru(cx, lru_in, lru_pv, lru_w, lru_out, consts_sb, Bconst, NU=3, L=SEQ, tag="lru"):
    nc, S = cx.nc, cx.S
    m0 = cx.mark()
    NTL = L // 512
    lxp = cx.sb(tag + "lxp", [128, L + 3], F32)
    xc = cx.sb(tag + "xc", [128, L], F32)
    ra = cx.sb(tag + "ra", [128, L], F32)
    ig = cx.sb(tag + "ig", [128, L], F32)
    tmp = cx.sb(tag + "tmp", [128, L], F32)
    pv = cx.sb(tag + "pv", [128, 9], F32)
    wg = cx.sb(tag + "wg", [128, 2, 128], F32)
    sc = cx.sb(tag + "sc", [128, 4], F32)
    Blxp, Bxc, Bra, Big, Btmp, Bpv, Bwg, Bsc = [Buf(n) for n in ("lxp", "xc", "ra", "ig", "tmp", "pv", "wg", "sc")]
    ones = consts_sb[:, C_ONES, :]
    for u in range(NU):
        S.dma("sp", lambda e, u=u: e.dma_start(out=pv, in_=lru_pv.ap[u]), reads=[lru_pv.buf], writes=[Bpv])
        S.dma("sp", lambda e, u=u: e.dma_start(out=wg, in_=lru_w.ap[u].rearrange("g i j -> i g j")), reads=[lru_w.buf], writes=[Bwg])
        S.op("dve", lambda e: e.memset(lxp[:, 0:3], 0.0), writes=[Blxp])
        S.dma("sp", lambda e, u=u: e.dma_start(out=lxp[:, 3:L + 3], in_=lru_in.get(u, 0)), reads=[lru_in.buf], writes=[Blxp])
        S.op("dve", lambda e: e.tensor_scalar(out=xc, in0=lxp[:, 0:L], scalar1=pv[:, 0:1], scalar2=pv[:, 4:5], op0=ALU.mult, op1=ALU.add),
             reads=[Blxp, Bpv], writes=[Bxc])
        for j in range(1, 4):
            S.op("dve", lambda e, j=j: e.scalar_tensor_tensor(out=xc, in0=lxp[:, j:j + L], scalar=pv[:, j:j + 1], in1=xc, op0=ALU.mult, op1=ALU.add),
                 reads=[Blxp, Bpv, Bxc], writes=[Bxc])
        S.dma("sp", lambda e, u=u: e.dma_start(out=lxp[:, 0:L], in_=lru_in.get(u, 1)), reads=[lru_in.buf], writes=[Blxp])
        S.op("act", lambda e: e.activation(out=sc[:, 0:1], in_=pv[:, 7:8], func=AF.Exp, scale=-1.0), reads=[Bpv], writes=[Bsc])
        S.op("act", lambda e: e.activation(out=sc[:, 0:1], in_=sc[:, 0:1], func=AF.Ln, bias=ones[:, 0:1], scale=1.0), reads=[Bsc, Bconst], writes=[Bsc])
        S.op("dve", lambda e: e.tensor_scalar(out=sc[:, 1:2], in0=sc[:, 0:1], scalar1=-16.0, scalar2=None, op0=ALU.mult), reads=[Bsc], writes=[Bsc])
        S.op("dve", lambda e: e.tensor_scalar(out=sc[:, 0:1], in0=sc[:, 0:1], scalar1=-8.0, scalar2=None, op0=ALU.mult), reads=[Bsc], writes=[Bsc])
        for gi, (dst, Bdst, bcol) in enumerate(((ra, Bra, 5), (ig, Big, 6))):
            for tt in range(NTL):
                tsl = slice(tt * 512, (tt + 1) * 512)
                b = cx.next_bank(0, 8)
                S.op("pe", lambda e, b=b, gi=gi, tsl=tsl: e.matmul(cx.ps[b][:], lhsT=wg[:, gi, :], rhs=xc[:, tsl], start=True, stop=True),
                     reads=[Bwg, Bxc], writes=[cx.Bps[b]])
                S.op("act", lambda e, b=b, dst=dst, tsl=tsl, bcol=bcol: e.activation(
                    out=dst[:, tsl], in_=cx.ps[b][:], func=AF.Sigmoid, bias=pv[:, bcol:bcol + 1], scale=1.0),
                    reads=[cx.Bps[b], Bpv], writes=[Bdst])
        S.op("act", lambda e: e.activation(out=tmp, in_=ra, func=AF.Exp, scale=sc[:, 1:2]), reads=[Bra, Bsc], writes=[Btmp])
        S.op("act", lambda e: e.activation(out=ra, in_=ra, func=AF.Exp, scale=sc[:, 0:1]), reads=[Bra, Bsc], writes=[Bra])
        S.op("dve", lambda e: e.tensor_scalar(out=tmp, in0=tmp, scalar1=-1.0, scalar2=1.0, op0=ALU.mult, op1=ALU.add), reads=[Btmp], writes=[Btmp])
        S.op("act", lambda e: e.activation(out=tmp, in_=tmp, func=AF.Sqrt), reads=[Btmp], writes=[Btmp])
        S.op("dve", lambda e: e.tensor_tensor(out=ig, in0=ig, in1=tmp, op=ALU.mult), reads=[Big, Btmp], writes=[Big])
        S.op("dve", lambda e: e.tensor_tensor(out=ig, in0=ig, in1=xc, op=ALU.mult), reads=[Big, Bxc], writes=[Big])
        S.op("dve", lambda e: e.tensor_tensor_scan(out=tmp, data0=ra, data1=ig, initial=0.0, op0=ALU.mult, op1=ALU.add),
             reads=[Bra, Big], writes=[Btmp])
        S.op("act", lambda e: e.activation(out=lxp[:, 0:L], in_=lxp[:, 0:L], func=AF.Gelu), reads=[Blxp], writes=[Blxp])
        S.op("dve", lambda e: e.tensor_tensor(out=tmp, in0=tmp, in1=lxp[:, 0:L], op=ALU.mult), reads=[Btmp, Blxp], writes=[Btmp])
        S.op("act", lambda e: e.activation(out=xc, in_=tmp, func=AF.Square), reads=[Btmp], writes=[Bxc])
        for tt in range(NTL):
            tsl = slice(tt * 512, (tt + 1) * 512)
            b = cx.next_bank(0, 8)
            S.op("pe", lambda e, b=b, tsl=tsl: e.matmul(cx.ps[b][:], lhsT=ones, rhs=xc[:, tsl], start=True, stop=True),
                 reads=[Bconst, Bxc], writes=[cx.Bps[b]])
            S.op("act", lambda e, b=b, tsl=tsl: e.activation(out=ig[:, tsl], in_=cx.ps[b][:], func=AF.Sqrt, bias=consts_sb[:, C_ID, 0:1] if False else cx.epsb[:, 0:1], scale=1.0 / 128),
                 reads=[cx.Bps[b], cx.Beps], writes=[Big])
        S.op("dve", lambda e: e.reciprocal(out=ig, in_=ig), reads=[Big], writes=[Big])
        S.op("dve", lambda e: e.scalar_tensor_tensor(out=tmp, in0=tmp, scalar=pv[:, 8:9], in1=ig, op0=ALU.mult, op1=ALU.mult),
             reads=[Btmp, Bpv, Big], writes=[Btmp])
        S.dma("sp", lambda e, u=u: e.dma_start(out=lru_out.get(u), in_=tmp), reads=[Btmp], writes=[lru_out.b(u)], owner=Btmp, is_out=lru_out.is_out)
    cx.release(m0)


def p2_common(cx, consts):
    S = cx.S
    consts_sb = cx.sb("consts", [128, NCONST, 128], F32)
    Bconst = Buf("consts")
    S.dma("sp", lambda e: e.dma_start(out=consts_sb, in_=consts.ap), reads=[consts.buf], writes=[Bconst])
    cx.epsb = cx.sb("epsb", [128, 1], F32)
    cx.Beps = Buf("eps")
    S.op("dve", lambda e: e.memset(cx.epsb, EPS), writes=[cx.Beps])
    return consts_sb, Bconst


def build_lru_test(NU=1, L=SEQ):
    nc = bass.Bass("TRN2", target_bir_lowering=False)
    lru_in = DT(nc, "lru_in", [NU, 2, 128, L], F32, "ExternalInput")
    lru_pv = DT(nc, "lru_pv", [NU, 128, 9], F32, "ExternalInput")
    lru_w = DT(nc, "lru_w", [NU, 2, 128, 128], F32, "ExternalInput")
    consts = DT(nc, "consts", [128, NCONST, 128], F32, "ExternalInput")
    lru_out = DT(nc, "lru_out", [NU, 128, L], F32, "ExternalOutput")
    with contextlib.ExitStack() as st:
        cx = Ctx(nc, st)
        consts_sb, Bconst = p2_common(cx, consts)
        emit_lru(cx, lru_in, lru_pv, lru_w, lru_out, consts_sb, Bconst, NU=NU, L=L)
        cx.S.finish("sp")
        cx.S.emit()
    return nc


class QPool:
    def __init__(self, cx, banks):
        self.tiles = []
        self.banks = []
        for b in banks:
            self.tiles.append((cx.ps[b][:, 0:128], cx.Bps[b]))
            self.banks.append(cx.ps[b])
        self.i = 0

    def next(self):
        t = self.tiles[self.i % len(self.tiles)]
        self.i += 1
        return t


class Rot:
    def __init__(self, cx, name, shape, dt, n):
        self.items = [(cx.sb("%s%d" % (name, i), shape, dt), Buf("%s%d" % (name, i))) for i in range(n)]
        self.i = 0

    def next(self):
        t = self.items[self.i % len(self.items)]
        self.i += 1
        return t


def emit_dn(cx, dn_in, dn_ab, dn_cw, dn_hp, dn_nw, dn_out, consts_sb, Bconst, NU=3, L=SEQ, tag="dn"):
    nc, S = cx.nc, cx.S
    S.barrier()
    m0 = cx.mark()
    SEG = 512
    NP = SEG // 128
    NSEG = L // SEG
    ident = consts_sb[:, C_ID, :]
    ones = consts_sb[:, C_ONES, :]
    QP = QPool(cx, [0, 1, 2, 3, 4, 5])
    BANK = [6, 7]
    bank_i = [0]

    def full_bank():
        b = BANK[bank_i[0] % 2]
        bank_i[0] += 1
        return cx.ps[b], cx.Bps[b]

    cwsb = cx.sb(tag + "cw", [128, 3, 4], F32)
    hp = cx.sb(tag + "hp", [128, 4], F32)
    nw = cx.sb(tag + "nw", [128, 1], F32)
    Bcw, Bhp, Bnw = Buf("dcw"), Buf("dhp"), Buf("dnw")
    S.dma("sp", lambda e: e.dma_start(out=nw, in_=dn_nw.ap), reads=[dn_nw.buf], writes=[Bnw])
    S_sb = cx.sb(tag + "S", [128, 128], F32)
    BS = Buf("S")
    NSC = 14
    (A_, BETA, NBETA, G_, GC, GL, GL0, GL1, EG, KBD, KDEC, D0, D1, TMP) = range(NSC)

    def mkset(i):
        d = {}
        for nm in ("qf", "kf", "vf", "zs", "sqb", "rn", "oT"):
            d[nm] = cx.sb("%s%s%d" % (tag, nm, i), [128, SEG], F32)
            d["B" + nm] = Buf(nm)
        for nm in ("Kbd", "Kdec", "Vb", "attnT", "U", "WT", "otm", "sqo"):
            d[nm] = cx.sb("%s%s%d" % (tag, nm, i), [128, NP, 128], F32)
            d["B" + nm] = [Buf("%s%d" % (nm, n)) for n in range(NP)]
        d["QdT"] = cx.sb("%sQdT%d" % (tag, i), [128, SEG], F32)
        d["BQdT"] = [Buf("QdT%d" % n) for n in range(NP)]
        d["sc"] = cx.sb("%ssc%d" % (tag, i), [128, NSC, NP], F32)
        d["Bsc"] = [Buf("sc%d" % k) for k in range(NSC)]
        d["ab"] = cx.sb("%sab%d" % (tag, i), [128, 2, NP], F32)
        d["Bab"] = Buf("ab")
        d["sso"] = cx.sb("%ssso%d" % (tag, i), [128, NP], F32)
        d["Bsso"] = Buf("sso")
        return d

    sets = [mkset(0), mkset(1)]
    pads = Rot(cx, tag + "pad", [128, SEG + 3], F32, 2)
    T128 = {nm: Rot(cx, tag + nm, [128, 128], F32, 2) for nm in ("Dg", "tL", "dL", "tU", "dU", "EG", "Bm", "Bt", "Pt")}
    PW = Rot(cx, tag + "pw", [128, 128], F32, 6)
    VN = Rot(cx, tag + "vn", [128, 128], F32, 2)

    for u in range(NU):
        S.dma("sp", lambda e, u=u: e.dma_start(out=cwsb, in_=dn_cw.ap[u]), reads=[dn_cw.buf], writes=[Bcw])
        S.dma("sp", lambda e, u=u: e.dma_start(out=hp[:, 0:2], in_=dn_hp.ap[u]), reads=[dn_hp.buf], writes=[Bhp])
        S.op("act", lambda e: e.activation(out=hp[:, 2:3], in_=hp[:, 0:1], func=AF.Exp), reads=[Bhp], writes=[Bhp])
        S.op("dve", lambda e: e.tensor_scalar(out=hp[:, 2:3], in0=hp[:, 2:3], scalar1=-1.0, scalar2=None, op0=ALU.mult), reads=[Bhp], writes=[Bhp])
        S.op("dve", lambda e: e.memset(S_sb, 0.0), writes=[BS])
        for seg in range(NSEG):
            d = sets[seg % 2]
            s0 = seg * SEG
            sc, Bsc = d["sc"], d["Bsc"]
            for part, nm in ((0, "qf"), (1, "kf"), (2, "vf")):
                pad, Bpad = pads.next()
                dst, Bdst = d[nm], d["B" + nm]
                if seg == 0:
                    S.op("dve", lambda e, pad=pad: e.memset(pad[:, 0:3], 0.0), writes=[Bpad])
                    S.dma("sp", lambda e, pad=pad, u=u, part=part: e.dma_start(out=pad[:, 3:SEG + 3], in_=dn_in.get(u, part)[:, 0:SEG]),
                          reads=[dn_in.buf], writes=[Bpad])
                else:
                    S.dma("sp", lambda e, pad=pad, u=u, part=part, s0=s0: e.dma_start(out=pad, in_=dn_in.get(u, part)[:, s0 - 3:s0 + SEG]),
                          reads=[dn_in.buf], writes=[Bpad])
                S.op("dve", lambda e, pad=pad, dst=dst, part=part: e.tensor_scalar(
                    out=dst, in0=pad[:, 0:SEG], scalar1=cwsb[:, part, 0:1], scalar2=None, op0=ALU.mult),
                    reads=[Bpad, Bcw], writes=[Bdst])
                for j in range(1, 4):
                    S.op("dve", lambda e, pad=pad, dst=dst, part=part, j=j: e.scalar_tensor_tensor(
                        out=dst, in0=pad[:, j:j + SEG], scalar=cwsb[:, part, j:j + 1], in1=dst, op0=ALU.mult, op1=ALU.add),
                        reads=[Bpad, Bcw, Bdst], writes=[Bdst])
                S.op("act", lambda e, dst=dst: e.activation(out=dst, in_=dst, func=AF.Silu), reads=[Bdst], writes=[Bdst])
                if part < 2:
                    S.op("act", lambda e, dst=dst, d=d: e.activation(out=d["sqb"], in_=dst, func=AF.Square), reads=[Bdst], writes=[d["Bsqb"]])
                    ps, Bp = full_bank()
                    S.op("pe", lambda e, ps=ps, d=d: e.matmul(ps[:], lhsT=ones, rhs=d["sqb"], start=True, stop=True),
                         reads=[Bconst, d["Bsqb"]], writes=[Bp])
                    S.op("act", lambda e, ps=ps, d=d: e.activation(out=d["rn"], in_=ps[:], func=AF.Sqrt, bias=cx.epsb[:, 0:1], scale=1.0),
                         reads=[Bp, cx.Beps], writes=[d["Brn"]])
                    S.op("dve", lambda e, d=d: e.reciprocal(out=d["rn"], in_=d["rn"]), reads=[d["Brn"]], writes=[d["Brn"]])
                    qs = (HD ** -0.5) if part == 0 else 1.0
                    S.op("dve", lambda e, dst=dst, d=d, qs=qs: e.scalar_tensor_tensor(
                        out=dst, in0=dst, scalar=qs, in1=d["rn"], op0=ALU.mult, op1=ALU.mult),
                        reads=[Bdst, d["Brn"]], writes=[Bdst])
            S.dma("sp", lambda e, d=d, u=u, s0=s0: e.dma_start(out=d["zs"], in_=dn_in.get(u, 3)[:, s0:s0 + SEG]),
                  reads=[dn_in.buf], writes=[d["Bzs"]])
            S.op("act", lambda e, d=d: e.activation(out=d["zs"], in_=d["zs"], func=AF.Silu), reads=[d["Bzs"]], writes=[d["Bzs"]])
            if hasattr(dn_ab, "get_row"):
                for ri in range(2):
                    S.dma("sp", lambda e, d=d, u=u, s0=s0, ri=ri: e.dma_start(
                        out=d["ab"][:, ri, :], in_=dn_ab.get_row(u, ri)[s0:s0 + SEG].rearrange("(n p) -> p n", p=128),
                        allow_slow_non_contiguous=True),
                        reads=[dn_ab.buf], writes=[d["Bab"]])
            else:
                S.dma("sp", lambda e, d=d, u=u, seg=seg: e.dma_start(out=d["ab"], in_=dn_ab.ap[u][:, :, seg * NP:(seg + 1) * NP]),
                      reads=[dn_ab.buf], writes=[d["Bab"]])
            S.op("act", lambda e, d=d, sc=sc: e.activation(out=sc[:, TMP, :], in_=d["ab"][:, 0, :], func=AF.Exp, bias=hp[:, 1:2], scale=1.0),
                 reads=[d["Bab"], Bhp], writes=[Bsc[TMP]])
            S.op("act", lambda e, sc=sc: e.activation(out=sc[:, TMP, :], in_=sc[:, TMP, :], func=AF.Ln, bias=ones[:, 0:1], scale=1.0),
                 reads=[Bsc[TMP], Bconst], writes=[Bsc[TMP]])
            S.op("dve", lambda e, sc=sc: e.tensor_scalar(out=sc[:, G_, :], in0=sc[:, TMP, :], scalar1=hp[:, 2:3], scalar2=None, op0=ALU.mult),
                 reads=[Bsc[TMP], Bhp], writes=[Bsc[G_]])
            S.op("act", lambda e, d=d, sc=sc: e.activation(out=sc[:, BETA, :], in_=d["ab"][:, 1, :], func=AF.Sigmoid),
                 reads=[d["Bab"]], writes=[Bsc[BETA]])
            S.op("dve", lambda e, sc=sc: e.tensor_scalar(out=sc[:, NBETA, :], in0=sc[:, BETA, :], scalar1=-1.0, scalar2=None, op0=ALU.mult),
                 reads=[Bsc[BETA]], writes=[Bsc[NBETA]])
            pq, Bq = QP.next()
            S.op("pe", lambda e, pq=pq, sc=sc: e.matmul(pq[:, 0:NP], lhsT=consts_sb[:, C_TRI, :], rhs=sc[:, G_, :], start=True, stop=True),
                 reads=[Bconst, Bsc[G_]], writes=[Bq])
            S.op("dve", lambda e, pq=pq, sc=sc: e.tensor_copy(out=sc[:, GC, :], in_=pq[:, 0:NP]), reads=[Bq], writes=[Bsc[GC]])
            for ci, slot in ((C_SELEND, GL), (C_SEL63, GL0), (C_SEL127, GL1)):
                pq, Bq = QP.next()
                S.op("pe", lambda e, pq=pq, sc=sc, ci=ci: e.matmul(pq[:, 0:NP], lhsT=consts_sb[:, ci, :], rhs=sc[:, GC, :], start=True, stop=True),
                     reads=[Bconst, Bsc[GC]], writes=[Bq])
                S.op("dve", lambda e, pq=pq, sc=sc, slot=slot: e.tensor_copy(out=sc[:, slot, :], in_=pq[:, 0:NP]), reads=[Bq], writes=[Bsc[slot]])
            S.op("act", lambda e, sc=sc: e.activation(out=sc[:, EG, :], in_=sc[:, GC, :], func=AF.Exp), reads=[Bsc[GC]], writes=[Bsc[EG]])
            S.op("dve", lambda e, sc=sc: e.tensor_tensor(out=sc[:, KBD, :], in0=sc[:, BETA, :], in1=sc[:, EG, :], op=ALU.mult),
                 reads=[Bsc[BETA], Bsc[EG]], writes=[Bsc[KBD]])
            S.op("dve", lambda e, sc=sc: e.tensor_tensor(out=sc[:, KDEC, :], in0=sc[:, GL, :], in1=sc[:, GC, :], op=ALU.subtract),
                 reads=[Bsc[GL], Bsc[GC]], writes=[Bsc[KDEC]])
            S.op("act", lambda e, sc=sc: e.activation(out=sc[:, KDEC, :], in_=sc[:, KDEC, :], func=AF.Exp), reads=[Bsc[KDEC]], writes=[Bsc[KDEC]])
            S.op("act", lambda e, sc=sc: e.activation(out=sc[:, D0, :], in_=sc[:, GL0, :], func=AF.Exp), reads=[Bsc[GL0]], writes=[Bsc[D0]])
            S.op("act", lambda e, sc=sc: e.activation(out=sc[:, D1, :], in_=sc[:, GL1, :], func=AF.Exp), reads=[Bsc[GL1]], writes=[Bsc[D1]])
            for n in range(NP):
                blk = slice(n * 128, (n + 1) * 128)
                pq, Bq = QP.next()
                S.op("pe", lambda e, pq=pq, d=d, blk=blk: e.transpose(out=pq, in_=d["kf"][:, blk], identity=ident),
                     reads=[d["Bkf"], Bconst], writes=[Bq])
                S.op("dve", lambda e, pq=pq, d=d, n=n, sc=sc: e.tensor_scalar(out=d["Kbd"][:, n, :], in0=pq, scalar1=sc[:, KBD, n:n + 1], scalar2=None, op0=ALU.mult),
                     reads=[Bq, Bsc[KBD]], writes=[d["BKbd"][n]])
                S.op("dve", lambda e, pq=pq, d=d, n=n, sc=sc: e.tensor_scalar(out=d["Kdec"][:, n, :], in0=pq, scalar1=sc[:, KDEC, n:n + 1], scalar2=None, op0=ALU.mult),
                     reads=[Bq, Bsc[KDEC]], writes=[d["BKdec"][n]])
                pq, Bq = QP.next()
                S.op("pe", lambda e, pq=pq, d=d, blk=blk: e.transpose(out=pq, in_=d["vf"][:, blk], identity=ident),
                     reads=[d["Bvf"], Bconst], writes=[Bq])
                S.op("dve", lambda e, pq=pq, d=d, n=n, sc=sc: e.tensor_scalar(out=d["Vb"][:, n, :], in0=pq, scalar1=sc[:, BETA, n:n + 1], scalar2=None, op0=ALU.mult),
                     reads=[Bq, Bsc[BETA]], writes=[d["BVb"][n]])
            for n in range(NP):
                blk = slice(n * 128, (n + 1) * 128)
                gcc = sc[:, GC, n:n + 1]
                Dg, BDg = T128["Dg"].next()
                S.op("dve", lambda e, Dg=Dg, gcc=gcc: e.tensor_scalar(out=Dg, in0=ident, scalar1=gcc, scalar2=None, op0=ALU.mult),
                     reads=[Bconst, Bsc[GC]], writes=[BDg])
                pA, BpA = QP.next()
                S.op("pe", lambda e, pA=pA, Dg=Dg: e.matmul(pA, lhsT=ones, rhs=Dg, start=True, stop=True), reads=[Bconst, BDg], writes=[BpA])
                tL, BtL = T128["tL"].next()
                S.op("dve", lambda e, tL=tL, pA=pA, gcc=gcc: e.scalar_tensor_tensor(
                    out=tL, in0=pA, scalar=gcc, in1=consts_sb[:, C_MLS, :], op0=ALU.subtract, op1=ALU.add),
                    reads=[BpA, Bsc[GC], Bconst], writes=[BtL])
                dL, BdL = T128["dL"].next()
                S.op("act", lambda e, dL=dL, tL=tL: e.activation(out=dL, in_=tL, func=AF.Exp, scale=-1.0), reads=[BtL], writes=[BdL])
                tU, BtU = T128["tU"].next()
                S.op("dve", lambda e, tU=tU, pA=pA, gcc=gcc: e.scalar_tensor_tensor(
                    out=tU, in0=pA, scalar=gcc, in1=consts_sb[:, C_MU, :], op0=ALU.subtract, op1=ALU.add),
                    reads=[BpA, Bsc[GC], Bconst], writes=[BtU])
                dU, BdU = T128["dU"].next()
                S.op("act", lambda e, dU=dU, tU=tU: e.activation(out=dU, in_=tU, func=AF.Exp), reads=[BtU], writes=[BdU])
                EGt, BEG = T128["EG"].next()
                S.op("act", lambda e, EGt=EGt, pA=pA: e.activation(out=EGt, in_=pA, func=AF.Exp), reads=[BpA], writes=[BEG])
                S.op("dve", lambda e, d=d, blk=blk, EGt=EGt: e.tensor_tensor(out=d["QdT"][:, blk], in0=d["qf"][:, blk], in1=EGt, op=ALU.mult),
                     reads=[d["Bqf"], BEG], writes=[d["BQdT"][n]])
                pG, BpG = QP.next()
                S.op("pe", lambda e, pG=pG, d=d, blk=blk: e.matmul(pG, lhsT=d["kf"][:, blk], rhs=d["kf"][:, blk], start=True, stop=True),
                     reads=[d["Bkf"]], writes=[BpG])
                Bm, BBm = T128["Bm"].next()
                S.op("dve", lambda e, Bm=Bm, pG=pG, sc=sc, n=n, dL=dL: e.scalar_tensor_tensor(
                    out=Bm, in0=pG, scalar=sc[:, NBETA, n:n + 1], in1=dL, op0=ALU.mult, op1=ALU.mult),
                    reads=[BpG, Bsc[NBETA], BdL], writes=[BBm])
                pQ, BpQ = QP.next()
                S.op("pe", lambda e, pQ=pQ, d=d, blk=blk: e.matmul(pQ, lhsT=d["kf"][:, blk], rhs=d["qf"][:, blk], start=True, stop=True),
                     reads=[d["Bkf"], d["Bqf"]], writes=[BpQ])
                S.op("dve", lambda e, d=d, n=n, pQ=pQ, dU=dU: e.tensor_tensor(out=d["attnT"][:, n, :], in0=pQ, in1=dU, op=ALU.mult),
                     reads=[BpQ, BdU], writes=[d["BattnT"][n]])
                pT, BpT = QP.next()
                S.op("pe", lambda e, pT=pT, Bm=Bm: e.transpose(out=pT, in_=Bm, identity=ident), reads=[BBm, Bconst], writes=[BpT])
                Bt, BBt = T128["Bt"].next()
                S.op("act", lambda e, Bt=Bt, pT=pT: e.copy(out=Bt, in_=pT), reads=[BpT], writes=[BBt])
                Pt, BPt = T128["Pt"].next()
                S.op("dve", lambda e, Pt=Pt, Bt=Bt: e.tensor_tensor(out=Pt, in0=Bt, in1=ident, op=ALU.add), reads=[BBt, Bconst], writes=[BPt])
                cB, BcB, cBt, BcBt = Bm, BBm, Bt, BBt
                for lvl in range(1, 6):
                    p1, Bp1 = QP.next()
                    S.op("pe", lambda e, p1=p1, cBt=cBt, cB=cB: e.matmul(p1, lhsT=cBt, rhs=cB, start=True, stop=True),
                         reads=[BcBt, BcB], writes=[Bp1])
                    nB, BnB = PW.next()
                    S.op("act", lambda e, nB=nB, p1=p1: e.copy(out=nB, in_=p1), reads=[Bp1], writes=[BnB])
                    nBt, BnBt = None, None
                    if lvl < 5:
                        p2, Bp2 = QP.next()
                        S.op("pe", lambda e, p2=p2, cBt=cBt, cB=cB: e.matmul(p2, lhsT=cB, rhs=cBt, start=True, stop=True),
                             reads=[BcBt, BcB], writes=[Bp2])
                        nBt, BnBt = PW.next()
                        S.op("dve", lambda e, nBt=nBt, p2=p2: e.tensor_copy(out=nBt, in_=p2), reads=[Bp2], writes=[BnBt])
                    p3, Bp3 = QP.next()
                    S.op("pe", lambda e, p3=p3, nB=nB, Pt=Pt: e.matmul(p3, lhsT=nB, rhs=Pt, start=True, stop=True),
                         reads=[BnB, BPt], writes=[Bp3])
                    S.op("dve", lambda e, Pt=Pt, p3=p3: e.tensor_tensor(out=Pt, in0=Pt, in1=p3, op=ALU.add), reads=[BPt, Bp3], writes=[BPt])
                    cB, BcB, cBt, BcBt = nB, BnB, nBt, BnBt
                pU, BpU = QP.next()
                for hb in range(2):
                    P = slice(hb * 64, hb * 64 + 64)
                    S.op("pe", lambda e, pU=pU, Pt=Pt, d=d, n=n, P=P: e.matmul(pU[P, :], lhsT=Pt[P, P], rhs=d["Vb"][P, n, :], start=True, stop=True),
                         reads=[BPt, d["BVb"][n]], writes=[BpU])
                S.op("act", lambda e, pU=pU, d=d, n=n: e.copy(out=d["U"][:, n, :], in_=pU), reads=[BpU], writes=[d["BU"][n]])
                pW, BpW = QP.next()
                S.op("pe", lambda e, pW=pW, Pt=Pt, d=d, n=n: e.matmul(pW, lhsT=d["Kbd"][:, n, :], rhs=Pt, start=True, stop=True),
                     reads=[BPt, d["BKbd"][n]], writes=[BpW])
                S.op("dve", lambda e, pW=pW, d=d, n=n: e.tensor_copy(out=d["WT"][:, n, :], in_=pW), reads=[BpW], writes=[d["BWT"][n]])
            for c in range(2 * NP):
                n, hb = c // 2, c % 2
                P = slice(hb * 64, hb * 64 + 64)
                cols = slice(n * 128 + hb * 64, n * 128 + hb * 64 + 64)
                pa, Bpa = QP.next()
                S.op("pe", lambda e, pa=pa, d=d, n=n, P=P: e.matmul(pa[P, :], lhsT=d["WT"][:, n, P], rhs=S_sb, start=True, stop=True),
                     reads=[d["BWT"][n], BS], writes=[Bpa])
                vn, Bvn = VN.next()
                S.op("dve", lambda e, vn=vn, pa=pa, d=d, n=n, P=P: e.tensor_tensor(out=vn[P, :], in0=d["U"][P, n, :], in1=pa[P, :], op=ALU.subtract),
                     reads=[d["BU"][n], Bpa], writes=[Bvn])
                po, Bpo = QP.next()
                S.op("pe", lambda e, po=po, d=d, cols=cols, P=P: e.matmul(po[P, :], lhsT=d["QdT"][:, cols], rhs=S_sb, start=True, stop=False),
                     reads=[d["BQdT"][n], BS], writes=[Bpo])
                S.op("pe", lambda e, po=po, d=d, n=n, P=P, vn=vn: e.matmul(po[P, :], lhsT=d["attnT"][P, n, P], rhs=vn[P, :], start=False, stop=True),
                     reads=[d["BattnT"][n], Bvn], writes=[Bpo])
                pS, BpS = QP.next()
                S.op("pe", lambda e, pS=pS, d=d, n=n, P=P, vn=vn: e.matmul(pS, lhsT=d["Kdec"][P, n, :], rhs=vn[P, :], start=True, stop=True),
                     reads=[d["BKdec"][n], Bvn], writes=[BpS])
                dslot = D0 if hb == 0 else D1
                S.op("dve", lambda e, pS=pS, sc=sc, dslot=dslot, n=n: e.scalar_tensor_tensor(
                    out=S_sb, in0=S_sb, scalar=sc[:, dslot, n:n + 1], in1=pS, op0=ALU.mult, op1=ALU.add),
                    reads=[BS, Bsc[dslot], BpS], writes=[BS])
                S.op("act", lambda e, po=po, d=d, n=n, P=P: e.copy(out=d["otm"][P, n, :], in_=po[P, :]), reads=[Bpo], writes=[d["Botm"][n]])
            S.op("act", lambda e, d=d: e.activation(out=d["sqo"], in_=d["otm"], func=AF.Square), reads=d["Botm"], writes=d["Bsqo"])
            S.op("dve", lambda e, d=d: e.tensor_reduce(out=d["sso"], in_=d["sqo"], axis=AX.X, op=ALU.add), reads=d["Bsqo"], writes=[d["Bsso"]])
            S.op("act", lambda e, d=d: e.activation(out=d["sso"], in_=d["sso"], func=AF.Sqrt, bias=cx.epsb[:, 0:1], scale=1.0 / HD),
                 reads=[d["Bsso"], cx.Beps], writes=[d["Bsso"]])
            S.op("dve", lambda e, d=d: e.reciprocal(out=d["sso"], in_=d["sso"]), reads=[d["Bsso"]], writes=[d["Bsso"]])
            S.op("dve", lambda e, d=d: e.tensor_tensor(out=d["otm"], in0=d["otm"], in1=d["sso"].unsqueeze(2).to_broadcast([128, NP, 128]), op=ALU.mult),
                 reads=d["Botm"] + [d["Bsso"]], writes=d["Botm"])
            for n in range(NP):
                blk = slice(n * 128, (n + 1) * 128)
                pT, BpT = QP.next()
                S.op("pe", lambda e, pT=pT, d=d, n=n: e.transpose(out=pT, in_=d["otm"][:, n, :], identity=ident),
                     reads=[d["Botm"][n], Bconst], writes=[BpT])
                S.op("dve", lambda e, pT=pT, d=d, blk=blk: e.scalar_tensor_tensor(
                    out=d["oT"][:, blk], in0=pT, scalar=nw[:, 0:1], in1=d["zs"][:, blk], op0=ALU.mult, op1=ALU.mult),
                    reads=[BpT, Bnw, d["Bzs"]], writes=[d["BoT"]])
            S.dma("sp", lambda e, d=d, u=u, s0=s0: e.dma_start(out=dn_out.get(u)[:, s0:s0 + SEG], in_=d["oT"]),
                  reads=[d["BoT"]], writes=[dn_out.b((u, seg))], owner=d["BoT"], is_out=dn_out.is_out)
    cx.release(m0)


def build_dn_test(NU=1, L=SEQ):
    nc = bass.Bass("TRN2", target_bir_lowering=False)
    dn_in = DT(nc, "dn_in", [NU, 4, 128, L], F32, "ExternalInput")
    dn_ab = DT(nc, "dn_ab", [NU, 128, 2, L // 128], F32, "ExternalInput")
    dn_cw = DT(nc, "dn_cw", [NU, 128, 3, 4], F32, "ExternalInput")
    dn_hp = DT(nc, "dn_hp", [NU, 128, 2], F32, "ExternalInput")
    dn_nw = DT(nc, "dn_nw", [128, 1], F32, "ExternalInput")
    consts = DT(nc, "consts", [128, NCONST, 128], F32, "ExternalInput")
    dn_out = DT(nc, "dn_out", [NU, 128, L], F32, "ExternalOutput")
    with contextlib.ExitStack() as st:
        cx = Ctx(nc, st)
        consts_sb, Bconst = p2_common(cx, consts)
        import os as _os
        ({"2": emit_dn2, "3": emit_dn3}.get(_os.environ.get("DN2"), emit_dn))(cx, dn_in, dn_ab, dn_cw, dn_hp, dn_nw, dn_out, consts_sb, Bconst, NU=NU, L=L)
        cx.S.finish("sp")
        cx.S.emit()
    return nc


def emit_sg(cx, suv, sg_wT, sg_bs, sg_vec, sg_out, consts_sb, Bconst, T=TOK, tag="sg", fm_src=None):
    nc, S = cx.nc, cx.S
    S.barrier()
    m0 = cx.mark()
    NG = SG_W // 128
    ident = consts_sb[:, C_ID, :]
    vec = cx.sb(tag + "vec", [128, 3, SG_W], F32)
    Bvec = Buf("sgvec")
    for i in range(3):
        S.dma("sp", lambda e, i=i: e.dma_start(out=vec[:, i, :], in_=sg_vec.ap[i].partition_broadcast(128)),
              reads=[sg_vec.buf], writes=[Bvec])
    wT = cx.sb(tag + "wT", [128, NG, 128], F32)
    BwT = Buf("sgwT")
    S.dma("sp", lambda e: e.dma_start(out=wT, in_=sg_wT.ap.rearrange("g s t -> s g t")), reads=[sg_wT.buf], writes=[BwT])
    S.op("dve", lambda e: e.tensor_tensor(out=wT, in0=wT, in1=consts_sb[:, C_UP:C_UP + 1, :].to_broadcast([128, NG, 128]), op=ALU.mult),
         reads=[BwT, Bconst], writes=[BwT])
    bs = cx.sb(tag + "bs", [128, NG], F32)
    Bbs = Buf("sgbs")
    S.dma("sp", lambda e: e.dma_start(out=bs, in_=sg_bs.ap), reads=[sg_bs.buf], writes=[Bbs])
    R = {nm: Rot(cx, tag + nm, [128, SG_W], F32, 2) for nm in ("u", "v", "y")}
    Rst = Rot(cx, tag + "st", [128, 16], F32, 2)
    Ro = Rot(cx, tag + "o", [128, NG, 128], F32, 2)
    Rfm = Rot(cx, tag + "fm", [128, 2 * NG, 128], F32, 2) if fm_src is not None else None
    for ch in range(T // 128):
        rows = slice(ch * 128, (ch + 1) * 128)
        u, Bu = R["u"].next()
        v, Bv = R["v"].next()
        y, By = R["y"].next()
        st, Bst = Rst.next()
        if fm_src is None:
            S.dma("sp", lambda e, u=u, rows=rows: e.dma_start(out=u, in_=suv.ap[rows, 0:SG_W]), reads=[suv.buf], writes=[Bu])
            S.dma("sp", lambda e, v=v, rows=rows: e.dma_start(out=v, in_=suv.ap[rows, SG_W:2 * SG_W]), reads=[suv.buf], writes=[Bv])
            S.op("act", lambda e, u=u: e.activation(out=u, in_=u, func=AF.Gelu), reads=[Bu], writes=[Bu])
            S.op("act", lambda e, v=v, st=st: e.activation(out=v, in_=v, func=AF.Gelu, accum_out=st[:, 0:1]), reads=[Bv], writes=[Bv, Bst])
        else:
            fm, Bfm = Rfm.next()
            S.dma("sp", lambda e, fm=fm, rows=rows: e.dma_start(out=fm, in_=fm_src.ap.rearrange("(b c) t -> c b t", c=128)[:, :, rows]),
                  reads=[fm_src.buf], writes=[Bfm])
            for blk in range(2 * NG):
                b = blk % 6
                dstt, Bd = (u, Bu) if blk < NG else (v, Bv)
                cs = slice((blk % NG) * 128, (blk % NG + 1) * 128)
                S.op("pe", lambda e, b=b, fm=fm, blk=blk: e.transpose(out=cx.ps[b][:, 0:128], in_=fm[:, blk, :], identity=ident),
                     reads=[Bfm, Bconst], writes=[cx.Bps[b]])
                S.op("act", lambda e, b=b, dstt=dstt, cs=cs: e.activation(out=dstt[:, cs], in_=cx.ps[b][:, 0:128], func=AF.Gelu),
                     reads=[cx.Bps[b]], writes=[Bd])
            S.op("dve", lambda e, v=v, st=st: e.tensor_reduce(out=st[:, 0:1], in_=v, axis=AX.X, op=ALU.add), reads=[Bv], writes=[Bst])
        S.op("dve", lambda e, st=st: e.tensor_scalar(out=st[:, 1:2], in0=st[:, 0:1], scalar1=1.0 / SG_W, scalar2=None, op0=ALU.mult), reads=[Bst], writes=[Bst])
        S.op("dve", lambda e, v=v, st=st: e.tensor_scalar(out=v, in0=v, scalar1=st[:, 1:2], scalar2=None, op0=ALU.subtract), reads=[Bv, Bst], writes=[Bv])
        S.op("act", lambda e, v=v, y=y, st=st: e.activation(out=y, in_=v, func=AF.Square, accum_out=st[:, 2:3]), reads=[Bv], writes=[By, Bst])
        S.op("act", lambda e, st=st: e.activation(out=st[:, 3:4], in_=st[:, 2:3], func=AF.Sqrt, bias=cx.epsb[:, 0:1], scale=1.0 / SG_W),
             reads=[Bst, cx.Beps], writes=[Bst])
        S.op("dve", lambda e, st=st: e.reciprocal(out=st[:, 3:4], in_=st[:, 3:4]), reads=[Bst], writes=[Bst])
        S.op("dve", lambda e, v=v, st=st: e.scalar_tensor_tensor(out=v, in0=v, scalar=st[:, 3:4], in1=vec[:, 0, :], op0=ALU.mult, op1=ALU.mult),
             reads=[Bv, Bst, Bvec], writes=[Bv])
        S.op("dve", lambda e, v=v: e.tensor_tensor(out=v, in0=v, in1=vec[:, 1, :], op=ALU.add), reads=[Bv, Bvec], writes=[Bv])
        for half in range(2):
            b = 6 + half
            for gg in range(4):
                g = half * 4 + gg
                S.op("pe", lambda e, b=b, gg=gg, g=g, v=v: e.matmul(cx.ps[b][:, gg * 128:(gg + 1) * 128], lhsT=wT[:, g, :], rhs=v[:, g * 128:(g + 1) * 128],
                                                                     start=True, stop=True),
                     reads=[BwT, Bv], writes=[cx.Bps[b]])
            for gg in range(4):
                g = half * 4 + gg
                S.op("dve", lambda e, b=b, gg=gg, g=g, u=u, y=y: e.scalar_tensor_tensor(
                    out=y[:, g * 128:(g + 1) * 128], in0=cx.ps[b][:, gg * 128:(gg + 1) * 128], scalar=bs[:, g:g + 1], in1=u[:, g * 128:(g + 1) * 128],
                    op0=ALU.add, op1=ALU.mult), reads=[cx.Bps[b], Bbs, Bu], writes=[By])
        S.op("act", lambda e, u=u, y=y: e.activation(out=u, in_=y, func=AF.Square), reads=[By], writes=[Bu])
        S.op("dve", lambda e, u=u, st=st: e.tensor_reduce(out=st[:, 8:16], in_=u.rearrange("p (g c) -> p g c", g=NG), axis=AX.X, op=ALU.add),
             reads=[Bu], writes=[Bst])
        S.op("act", lambda e, st=st: e.activation(out=st[:, 8:16], in_=st[:, 8:16], func=AF.Sqrt, bias=cx.epsb[:, 0:1], scale=1.0 / 128),
             reads=[Bst, cx.Beps], writes=[Bst])
        S.op("dve", lambda e, st=st: e.reciprocal(out=st[:, 8:16], in_=st[:, 8:16]), reads=[Bst], writes=[Bst])
        S.op("dve", lambda e, y=y, st=st: e.tensor_tensor(out=y.rearrange("p (g c) -> p g c", g=NG), in0=y.rearrange("p (g c) -> p g c", g=NG),
                                                           in1=st[:, 8:16].unsqueeze(2).to_broadcast([128, NG, 128]), op=ALU.mult),
             reads=[By, Bst], writes=[By])
        S.op("dve", lambda e, y=y: e.tensor_tensor(out=y, in0=y, in1=vec[:, 2, :], op=ALU.mult), reads=[By, Bvec], writes=[By])
        o, Bo = Ro.next()
        for g in range(NG):
            b = g % 6
            S.op("pe", lambda e, b=b, g=g, y=y: e.transpose(out=cx.ps[b][:, 0:128], in_=y[:, g * 128:(g + 1) * 128], identity=ident),
                 reads=[By, Bconst], writes=[cx.Bps[b]])
            if g % 2 == 0:
                S.op("act", lambda e, b=b, g=g, o=o: e.copy(out=o[:, g, :], in_=cx.ps[b][:, 0:128]), reads=[cx.Bps[b]], writes=[Bo])
            else:
                S.op("dve", lambda e, b=b, g=g, o=o: e.tensor_copy(out=o[:, g, :], in_=cx.ps[b][:, 0:128]), reads=[cx.Bps[b]], writes=[Bo])
        S.dma("sp", lambda e, o=o, rows=rows: e.dma_start(out=sg_out.ap.rearrange("(g c) t -> c g t", c=128)[:, :, rows], in_=o),
              reads=[Bo], writes=[sg_out.b(ch)], owner=Bo, is_out=sg_out.is_out)
    cx.release(m0)


def build_sg_test(T=TOK):
    nc = bass.Bass("TRN2", target_bir_lowering=False)
    suv = DT(nc, "suv", [T, 2 * SG_W], F32, "ExternalInput")
    sg_wT = DT(nc, "sg_wT", [SG_W // 128, 128, 128], F32, "ExternalInput")
    sg_bs = DT(nc, "sg_bs", [128, SG_W // 128], F32, "ExternalInput")
    sg_vec = DT(nc, "sg_vec", [3, SG_W], F32, "ExternalInput")
    consts = DT(nc, "consts", [128, NCONST, 128], F32, "ExternalInput")
    sg_out = DT(nc, "sg_out", [SG_W, T], F32, "ExternalOutput")
    with contextlib.ExitStack() as st:
        cx = Ctx(nc, st)
        consts_sb, Bconst = p2_common(cx, consts)
        emit_sg(cx, suv, sg_wT, sg_bs, sg_vec, sg_out, consts_sb, Bconst, T=T)
        cx.S.finish("sp")
        cx.S.emit()
    return nc


def build_p2(T=TOK, L=SEQ, NU=3):
    nc = bass.Bass("TRN2", target_bir_lowering=False)
    dn_in = DT(nc, "dn_in", [NU, 4, 128, L], F32, "ExternalInput")
    dn_ab = DT(nc, "dn_ab", [NU, 128, 2, L // 128], F32, "ExternalInput")
    dn_cw = DT(nc, "dn_cw", [NU, 128, 3, 4], F32, "ExternalInput")
    dn_hp = DT(nc, "dn_hp", [NU, 128, 2], F32, "ExternalInput")
    dn_nw = DT(nc, "dn_nw", [128, 1], F32, "ExternalInput")
    lru_in = DT(nc, "lru_in", [NU, 2, 128, L], F32, "ExternalInput")
    lru_pv = DT(nc, "lru_pv", [NU, 128, 9], F32, "ExternalInput")
    lru_w = DT(nc, "lru_w", [NU, 2, 128, 128], F32, "ExternalInput")
    suv = DT(nc, "suv", [T, 2 * SG_W], F32, "ExternalInput")
    sg_wT = DT(nc, "sg_wT", [SG_W // 128, 128, 128], F32, "ExternalInput")
    sg_bs = DT(nc, "sg_bs", [128, SG_W // 128], F32, "ExternalInput")
    sg_vec = DT(nc, "sg_vec", [3, SG_W], F32, "ExternalInput")
    consts = DT(nc, "consts", [128, NCONST, 128], F32, "ExternalInput")
    dn_out = DT(nc, "dn_out", [NU, 128, L], F32, "ExternalOutput")
    lru_out = DT(nc, "lru_out", [NU, 128, L], F32, "ExternalOutput")
    sg_out = DT(nc, "sg_out", [SG_W, T], F32, "ExternalOutput")
    with contextlib.ExitStack() as st:
        cx = Ctx(nc, st)
        consts_sb, Bconst = p2_common(cx, consts)
        emit_lru(cx, lru_in, lru_pv, lru_w, lru_out, consts_sb, Bconst, NU=NU, L=L)
        emit_sg(cx, suv, sg_wT, sg_bs, sg_vec, sg_out, consts_sb, Bconst, T=T)
        emit_dn3(cx, dn_in, dn_ab, dn_cw, dn_hp, dn_nw, dn_out, consts_sb, Bconst, NU=NU, L=L)
        cx.S.finish("sp")
        cx.S.emit()
    return nc


_PROGS = {}


def _prog(name, fn):
    if name not in _PROGS:
        _PROGS[name] = fn()
    return _PROGS[name]


def _lay(v, n):
    return np.ascontiguousarray(np.asarray(v, np.float32).reshape(n, 128).T)


def _run(nc, in_maps):
    res = run_bass_kernel_spmd(nc, in_maps, core_ids=list(range(NCORES)))
    return res.results


def kernel_unfused(x, norm_mix, w_in, dn_conv_w, dn_a_log, dn_dt_bias, dn_norm_w,
           lru_conv_w, lru_conv_b, lru_w_a, lru_b_a, lru_w_x, lru_b_x, lru_lambda, lru_norm_w,
           sg_ln_w, sg_ln_b, sg_w_s, sg_b_s, sg_norm_w, w_out,
           norm_ffn, w_up, ffn_conv_w, ffn_conv_b, w_down, norm_final):
    f32 = np.float32
    x = np.asarray(x, f32)
    NSEGC = SEQ // TOK
    consts = make_consts()
    xT = np.ascontiguousarray(x.transpose(0, 2, 1))
    p1 = _prog("p1", build_p1)
    p2 = _prog("p2", build_p2)
    for l in range(DEPTH):
        wl = np.asarray(w_in[l], f32)
        g1 = _lay(norm_mix[l], D_MODEL // 128)
        maps = []
        for c in range(NCORES):
            b, j = divmod(c, NSEGC)
            maps.append({"xT": np.ascontiguousarray(xT[b][:, j * TOK:(j + 1) * TOK]), "gain": g1, "w": wl})
        r1 = _run(p1, maps)
        projT = [np.concatenate([r1[b * NSEGC + j]["projT"] for j in range(NSEGC)], axis=1) for b in range(BATCH)]
        maps = []
        cwl = np.asarray(dn_conv_w[l], f32)
        for c in range(NCORES):
            b, q4 = divmod(c, NSEGC)
            pj = projT[b]
            heads = [3 * q4 + u for u in range(3)]
            dn_in = np.stack([np.stack([pj[part * DN_W + h * 128: part * DN_W + (h + 1) * 128] for part in range(4)]) for h in heads])
            ab = np.stack([np.stack([pj[6156 + h].reshape(SEQ // 128, 128).T, pj[6144 + h].reshape(SEQ // 128, 128).T], axis=1) for h in heads])
            dcw = np.stack([np.stack([cwl[:, part * DN_W + h * 128: part * DN_W + (h + 1) * 128].T for part in range(3)], axis=1) for h in heads])
            hp = np.stack([np.stack([np.full(128, dn_a_log[l][h], f32), np.full(128, dn_dt_bias[l][h], f32)], axis=1) for h in heads])
            lru_in = np.stack([np.stack([pj[6168 + g * 128: 6168 + (g + 1) * 128], pj[7704 + g * 128: 7704 + (g + 1) * 128]]) for g in heads])
            pv = np.stack([np.concatenate([np.asarray(lru_conv_w[l], f32)[:, g * 128:(g + 1) * 128].T] + [
                np.asarray(v[l], f32)[g * 128:(g + 1) * 128, None] for v in (lru_conv_b, lru_b_a, lru_b_x, lru_lambda, lru_norm_w)], axis=1) for g in heads])
            lw = np.stack([np.stack([np.asarray(lru_w_a[l][g], f32), np.asarray(lru_w_x[l][g], f32)]) for g in heads])
            own = r1[c]["projT"]
            suv = np.ascontiguousarray(own[9240:11288].T)
            maps.append({
                "dn_in": np.ascontiguousarray(dn_in, f32), "dn_ab": np.ascontiguousarray(ab, f32), "dn_cw": np.ascontiguousarray(dcw, f32),
                "dn_hp": np.ascontiguousarray(hp, f32), "dn_nw": np.ascontiguousarray(np.asarray(dn_norm_w[l], f32)[:, None]),
                "lru_in": np.ascontiguousarray(lru_in, f32), "lru_pv": np.ascontiguousarray(pv, f32), "lru_w": np.ascontiguousarray(lw, f32),
                "suv": suv, "sg_wT": np.ascontiguousarray(np.asarray(sg_w_s[l], f32).transpose(0, 2, 1)),
                "sg_bs": np.ascontiguousarray(np.asarray(sg_b_s[l], f32).T),
                "sg_vec": np.ascontiguousarray(np.stack([np.asarray(v[l], f32) for v in (sg_ln_w, sg_ln_b, sg_norm_w)])),
                "consts": consts})
        r2 = _run(p2, maps)
        mixT = []
        for b in range(BATCH):
            m = np.zeros((D_MODEL, SEQ), f32)
            for q4 in range(NSEGC):
                c = b * NSEGC + q4
                for u in range(3):
                    h = 3 * q4 + u
                    m[h * 128:(h + 1) * 128] = r2[c]["dn_out"][u]
                    m[DN_W + h * 128: DN_W + (h + 1) * 128] = r2[c]["lru_out"][u]
                m[DN_W + LRU_W:, q4 * TOK:(q4 + 1) * TOK] = r2[c]["sg_out"]
            mixT.append(m)
        final = (l == DEPTH - 1)
        p3 = _prog("p3f" if final else "p3", lambda: build_p3(final=final))
        g2 = _lay(norm_ffn[l], D_MODEL // 128)
        cwf = np.ascontiguousarray(np.asarray(ffn_conv_w[l], f32).reshape(3, 2 * D_FF // 128, 128).transpose(2, 1, 0))
        cbf = _lay(ffn_conv_b[l], 2 * D_FF // 128)
        maps = []
        for c in range(NCORES):
            b, j = divmod(c, NSEGC)
            xh = np.zeros((D_MODEL, TOK + 2), f32)
            mh = np.zeros((D_MODEL, TOK + 2), f32)
            lo = j * TOK - 2
            if j == 0:
                xh[:, 2:] = xT[b][:, 0:TOK]
                mh[:, 2:] = mixT[b][:, 0:TOK]
            else:
                xh[:] = xT[b][:, lo:lo + TOK + 2]
                mh[:] = mixT[b][:, lo:lo + TOK + 2]
            mp = {"xT": xh, "mixT": mh, "w_out": np.asarray(w_out[l], f32), "gain2": g2, "w_up": np.asarray(w_up[l], f32),
                  "cw": cwf, "cb": cbf, "w_down": np.asarray(w_down[l], f32)}
            if final:
                mp["gainf"] = _lay(norm_final, D_MODEL // 128)
            maps.append(mp)
        r3 = _run(p3, maps)
        xT = np.stack([np.concatenate([r3[b * NSEGC + j]["outT"] for j in range(NSEGC)], axis=1) for b in range(BATCH)])
    return np.ascontiguousarray(xT.transpose(0, 2, 1)).astype(f32)


def build_fused(L=SEQ, depth=DEPTH):
    nc = bass.Bass("TRN2", target_bir_lowering=False)
    KC, KD = D_MODEL // 128, D_FF // 128
    NT = L // TOK
    I = "ExternalInput"
    xin = DT(nc, "xin", [D_MODEL, L + 2], F32, I)
    gain1 = DT(nc, "gain1", [depth, 128, KC], F32, I)
    w_in = DT(nc, "w_in", [depth, D_MODEL, D_IN], F32, I)
    dn_cw = DT(nc, "dn_cw", [depth, DN_H, 128, 3, 4], F32, I)
    dn_hp = DT(nc, "dn_hp", [depth, DN_H, 128, 2], F32, I)
    dn_nw = DT(nc, "dn_nw", [depth, 128, 1], F32, I)
    lru_pv = DT(nc, "lru_pv", [depth, DN_H, 128, 9], F32, I)
    lru_w = DT(nc, "lru_w", [depth, DN_H, 2, 128, 128], F32, I)
    sg_wT = DT(nc, "sg_wT", [depth, SG_W // 128, 128, 128], F32, I)
    sg_bs = DT(nc, "sg_bs", [depth, 128, SG_W // 128], F32, I)
    sg_vec = DT(nc, "sg_vec", [depth, 3, SG_W], F32, I)
    w_out = DT(nc, "w_out", [depth, D_MODEL, D_MODEL], F32, I)
    gain2 = DT(nc, "gain2", [depth, 128, KC], F32, I)
    w_up = DT(nc, "w_up", [depth, D_MODEL, 2 * D_FF], F32, I)
    cw = DT(nc, "cw", [depth, 128, 2 * KD, 3], F32, I)
    cb = DT(nc, "cb", [depth, 128, 2 * KD], F32, I)
    w_down = DT(nc, "w_down", [depth, D_FF, D_MODEL], F32, I)
    gainf = DT(nc, "gainf", [128, KC], F32, I)
    consts = DT(nc, "consts", [128, NCONST, 128], F32, I)
    out = DT(nc, "out", [D_MODEL, L], F32, "ExternalOutput")
    projT = DT(nc, "projT", [D_IN, L], F32, "Internal")
    mixT = DT(nc, "mixT", [D_MODEL, L + 2], F32, "Internal")
    xa = DT(nc, "xa", [D_MODEL, L + 2], F32, "Internal")
    xmid = DT(nc, "xmid", [D_MODEL, TOK + 2], F32, "Internal")
    actT = DT(nc, "actT", [D_FF, TOK], BF16, "Internal")
    blocks = p1_blocks()
    with contextlib.ExitStack() as st:
        cx = Ctx(nc, st)
        S = cx.S
        consts_sb, Bconst = p2_common(cx, consts)
        mz = cx.mark()
        z = cx.sb("zero", [128, KC, 2], F32)
        Bz = Buf("zero")
        S.op("dve", lambda e: e.memset(z, 0.0), writes=[Bz])
        for tdst in (mixT, xa):
            S.dma("sp", lambda e, tdst=tdst: e.dma_start(out=tdst.ap[:, 0:2].rearrange("(k p) c -> p k c", p=128), in_=z),
                  reads=[Bz], writes=[tdst.buf])
        cx.release(mz)
        X = xin
        for l in range(depth):
            final = (l == depth - 1)
            for t in range(NT):
                emit_p1(cx, V(X.ap[:, 2 + t * TOK:2 + (t + 1) * TOK]), V(gain1.ap[l]), V(w_in.ap[l]),
                        V(projT.ap[:, t * TOK:(t + 1) * TOK]), KC, TOK, blocks, tag="p1_%d_%d" % (l, t))
            S.barrier()
            pj = projT.ap
            lru_in = V(None, get=lambda u, w: pj[6168 + w * LRU_W + u * 128: 6168 + w * LRU_W + (u + 1) * 128, :])
            lru_out = V(None, get=lambda u: mixT.ap[DN_W + u * 128: DN_W + (u + 1) * 128, 2:2 + L])
            emit_lru(cx, lru_in, V(lru_pv.ap[l]), V(lru_w.ap[l]), lru_out, consts_sb, Bconst, NU=DN_H, L=L, tag="lru%d" % l)
            emit_sg(cx, None, V(sg_wT.ap[l]), V(sg_bs.ap[l]), V(sg_vec.ap[l]), V(mixT.ap[DN_W + LRU_W:, 2:2 + L]),
                    consts_sb, Bconst, T=L, tag="sg%d" % l, fm_src=V(pj[9240:11288, :]))
            dn_in = V(None, get=lambda u, part: pj[part * DN_W + u * 128: part * DN_W + (u + 1) * 128, :])
            dn_ab = V(None, get_row=lambda u, ri: pj[(6156 if ri == 0 else 6144) + u, :])
            dn_out = V(None, get=lambda u: mixT.ap[u * 128:(u + 1) * 128, 2:2 + L])
            emit_dn3(cx, dn_in, dn_ab, V(dn_cw.ap[l]), V(dn_hp.ap[l]), V(dn_nw.ap[l]), dn_out, consts_sb, Bconst, NU=DN_H, L=L, tag="dn%d" % l)
            S.barrier()
            for t in range(NT):
                if final:
                    o = V(out.ap[:, t * TOK:(t + 1) * TOK], is_out=True)
                else:
                    o = V(xa.ap[:, 2 + t * TOK:2 + (t + 1) * TOK])
                emit_p3(cx, V(X.ap[:, t * TOK:t * TOK + TOK + 2]), V(mixT.ap[:, t * TOK:t * TOK + TOK + 2]), V(w_out.ap[l]), V(gain2.ap[l]),
                        V(w_up.ap[l]), V(cw.ap[l]), V(cb.ap[l]), V(w_down.ap[l]), o, xmid, actT, TOK,
                        gainf=(gainf if final else None), tag="p3_%d_%d" % (l, t))
                S.barrier()
            X = xa
        S.finish("sp")
        S.emit()
    return nc


def kernel_fused(x, norm_mix, w_in, dn_conv_w, dn_a_log, dn_dt_bias, dn_norm_w,
                 lru_conv_w, lru_conv_b, lru_w_a, lru_b_a, lru_w_x, lru_b_x, lru_lambda, lru_norm_w,
                 sg_ln_w, sg_ln_b, sg_w_s, sg_b_s, sg_norm_w, w_out,
                 norm_ffn, w_up, ffn_conv_w, ffn_conv_b, w_down, norm_final):
    f32 = np.float32
    A = lambda v: np.ascontiguousarray(np.asarray(v, f32))
    x = np.asarray(x, f32)
    KC, KD = D_MODEL // 128, D_FF // 128
    nc = _prog("fused", build_fused)
    shared = {
        "gain1": A(np.stack([_lay(norm_mix[l], KC) for l in range(DEPTH)])),
        "w_in": A(w_in),
        "dn_cw": A(np.asarray(dn_conv_w, f32).reshape(DEPTH, 4, 3, DN_H, 128).transpose(0, 3, 4, 2, 1)),
        "dn_hp": A(np.stack([np.broadcast_to(np.asarray(dn_a_log, f32)[:, :, None], (DEPTH, DN_H, 128)),
                             np.broadcast_to(np.asarray(dn_dt_bias, f32)[:, :, None], (DEPTH, DN_H, 128))], axis=-1)),
        "dn_nw": A(np.asarray(dn_norm_w, f32)[:, :, None]),
        "lru_pv": A(np.concatenate([np.asarray(lru_conv_w, f32).reshape(DEPTH, 4, DN_H, 128).transpose(0, 2, 3, 1)] + [
            np.asarray(v, f32).reshape(DEPTH, DN_H, 128, 1) for v in (lru_conv_b, lru_b_a, lru_b_x, lru_lambda, lru_norm_w)], axis=-1)),
        "lru_w": A(np.stack([np.asarray(lru_w_a, f32), np.asarray(lru_w_x, f32)], axis=2)),
        "sg_wT": A(np.asarray(sg_w_s, f32).transpose(0, 1, 3, 2)),
        "sg_bs": A(np.asarray(sg_b_s, f32).transpose(0, 2, 1)),
        "sg_vec": A(np.stack([np.asarray(v, f32) for v in (sg_ln_w, sg_ln_b, sg_norm_w)], axis=1)),
        "w_out": A(w_out),
        "gain2": A(np.stack([_lay(norm_ffn[l], KC) for l in range(DEPTH)])),
        "w_up": A(w_up),
        "cw": A(np.asarray(ffn_conv_w, f32).reshape(DEPTH, 3, 2 * KD, 128).transpose(0, 3, 2, 1)),
        "cb": A(np.stack([_lay(ffn_conv_b[l], 2 * KD) for l in range(DEPTH)])),
        "w_down": A(w_down),
        "gainf": _lay(norm_final, KC),
        "consts": make_consts(),
    }
    xpad = []
    for b in range(BATCH):
        xp = np.zeros((D_MODEL, SEQ + 2), f32)
        xp[:, 2:] = x[b].T
        xpad.append(xp)
    maps = []
    for c in range(NCORES):
        m = dict(shared)
        m["xin"] = xpad[c % BATCH]
        maps.append(m)
    res = run_bass_kernel_spmd(nc, maps, core_ids=list(range(NCORES))).results
    return np.ascontiguousarray(np.stack([res[b]["out"].T for b in range(BATCH)])).astype(f32)


kernel = kernel_unfused


def _roundrobin(gens):
    gens = list(gens)
    while gens:
        nxt = []
        for g in gens:
            try:
                next(g)
                nxt.append(g)
                yield
            except StopIteration:
                pass
        gens = nxt


def emit_dn2(cx, dn_in, dn_ab, dn_cw, dn_hp, dn_nw, dn_out, consts_sb, Bconst, NU=3, L=SEQ, tag="dn", GU=3):
    nc, S = cx.nc, cx.S
    S.barrier()
    m0 = cx.mark()
    SEG = 512
    NP = SEG // 128
    NSEG = L // SEG
    ident = consts_sb[:, C_ID, :]
    ones = consts_sb[:, C_ONES, :]
    QP = QPool(cx, [0, 1, 2, 3, 4, 5])
    BANK = [6, 7]
    bank_i = [0]

    def full_bank():
        b = BANK[bank_i[0] % 2]
        bank_i[0] += 1
        return cx.ps[b], cx.Bps[b]

    nw = cx.sb(tag + "nw", [128, 1], F32)
    Bnw = Buf("dnw")
    S.dma("sp", lambda e: e.dma_start(out=nw, in_=dn_nw.ap), reads=[dn_nw.buf], writes=[Bnw])
    NSC = 14
    (A_, BETA, NBETA, G_, GC, GL, GL0, GL1, EG, KBD, KDEC, D0, D1, TMP) = range(NSC)

    def mkslot(i):
        d = {}
        for nm in ("qf", "kf", "vf", "zs", "sqb", "rn", "oT", "QdT"):
            d[nm] = cx.sb("%s%s%d" % (tag, nm, i), [128, SEG], F32)
        for nm in ("Kbd", "Kdec", "Vb", "attnT", "U", "WT", "otm", "sqo"):
            d[nm] = cx.sb("%s%s%d" % (tag, nm, i), [128, NP, 128], F32)
        d["sc"] = cx.sb("%ssc%d" % (tag, i), [128, NSC, NP], F32)
        d["ab"] = cx.sb("%sab%d" % (tag, i), [128, 2, NP], F32)
        d["sso"] = cx.sb("%ssso%d" % (tag, i), [128, NP], F32)
        d["pad"] = [cx.sb("%spad%d_%d" % (tag, i, k), [128, SEG + 3], F32) for k in range(2)]
        d["cw"] = cx.sb("%scw%d" % (tag, i), [128, 3, 4], F32)
        d["hp"] = cx.sb("%shp%d" % (tag, i), [128, 4], F32)
        d["S"] = cx.sb("%sS%d" % (tag, i), [128, 128], F32)
        d["vn"] = [cx.sb("%svn%d_%d" % (tag, i, k), [128, 128], F32) for k in range(2)]
        d["tmp"] = [[cx.sb("%st%d_%d_%d" % (tag, i, n, k), [128, 128], F32) for k in range(8)] for n in range(NP)]
        return d

    def fresh_bufs(d):
        for nm in ("qf", "kf", "vf", "zs", "sqb", "rn", "oT", "ab", "sso", "cw", "hp", "S"):
            d["B" + nm] = Buf(nm)
        for nm in ("Kbd", "Kdec", "Vb", "attnT", "U", "WT", "otm", "sqo", "QdT"):
            d["B" + nm] = [Buf("%s%d" % (nm, n)) for n in range(NP)]
        d["Bsc"] = [Buf("sc%d" % k) for k in range(NSC)]
        d["Bpad"] = [Buf("pad0"), Buf("pad1")]
        d["Bvn"] = [Buf("vn0"), Buf("vn1")]
        d["Btmp"] = [[Buf("t%d_%d" % (n, k)) for k in range(8)] for n in range(NP)]

    slots = [mkslot(i) for i in range(min(GU, NU))]

    def pc_gen(d, n):
        sc, Bsc = d["sc"], d["Bsc"]
        blk = slice(n * 128, (n + 1) * 128)
        gcc = sc[:, GC, n:n + 1]
        T, BT = d["tmp"][n], d["Btmp"][n]
        X0, dL, dU, Bm, Bt, nB, nBt, Pt = T
        BX0, BdL, BdU, BBm, BBt, BnB, BnBt, BPt = BT
        S.op("dve", lambda e: e.tensor_scalar(out=X0, in0=ident, scalar1=gcc, scalar2=None, op0=ALU.mult),
             reads=[Bconst, Bsc[GC]], writes=[BX0])
        pA, BpA = QP.next()
        S.op("pe", lambda e: e.matmul(pA, lhsT=ones, rhs=X0, start=True, stop=True), reads=[Bconst, BX0], writes=[BpA])
        S.op("dve", lambda e: e.scalar_tensor_tensor(out=dL, in0=pA, scalar=gcc, in1=consts_sb[:, C_MLS, :], op0=ALU.subtract, op1=ALU.add),
             reads=[BpA, Bsc[GC], Bconst], writes=[BdL])
        S.op("act", lambda e: e.activation(out=dL, in_=dL, func=AF.Exp, scale=-1.0), reads=[BdL], writes=[BdL])
        S.op("dve", lambda e: e.scalar_tensor_tensor(out=dU, in0=pA, scalar=gcc, in1=consts_sb[:, C_MU, :], op0=ALU.subtract, op1=ALU.add),
             reads=[BpA, Bsc[GC], Bconst], writes=[BdU])
        S.op("act", lambda e: e.activation(out=dU, in_=dU, func=AF.Exp), reads=[BdU], writes=[BdU])
        S.op("act", lambda e: e.activation(out=X0, in_=pA, func=AF.Exp), reads=[BpA], writes=[BX0])
        S.op("dve", lambda e: e.tensor_tensor(out=d["QdT"][:, blk], in0=d["qf"][:, blk], in1=X0, op=ALU.mult),
             reads=[d["Bqf"], BX0], writes=[d["BQdT"][n]])
        yield
        pG, BpG = QP.next()
        S.op("pe", lambda e: e.matmul(pG, lhsT=d["kf"][:, blk], rhs=d["kf"][:, blk], start=True, stop=True), reads=[d["Bkf"]], writes=[BpG])
        S.op("dve", lambda e: e.scalar_tensor_tensor(out=Bm, in0=pG, scalar=sc[:, NBETA, n:n + 1], in1=dL, op0=ALU.mult, op1=ALU.mult),
             reads=[BpG, Bsc[NBETA], BdL], writes=[BBm])
        pQ, BpQ = QP.next()
        S.op("pe", lambda e: e.matmul(pQ, lhsT=d["kf"][:, blk], rhs=d["qf"][:, blk], start=True, stop=True),
             reads=[d["Bkf"], d["Bqf"]], writes=[BpQ])
        S.op("dve", lambda e: e.tensor_tensor(out=d["attnT"][:, n, :], in0=pQ, in1=dU, op=ALU.mult), reads=[BpQ, BdU], writes=[d["BattnT"][n]])
        yield
        pT, BpT = QP.next()
        S.op("pe", lambda e: e.transpose(out=pT, in_=Bm, identity=ident), reads=[BBm, Bconst], writes=[BpT])
        S.op("act", lambda e: e.copy(out=Bt, in_=pT), reads=[BpT], writes=[BBt])
        S.op("dve", lambda e: e.tensor_tensor(out=Pt, in0=Bt, in1=ident, op=ALU.add), reads=[BBt, Bconst], writes=[BPt])
        yield
        cur = (Bm, BBm, Bt, BBt)
        nxt = (nB, BnB, nBt, BnBt)
        for lvl in range(1, 6):
            cB, BcB, cBt, BcBt = cur
            tB, BtB, tBt, BtBt = nxt
            p1, Bp1 = QP.next()
            S.op("pe", lambda e, p1=p1, cBt=cBt, cB=cB: e.matmul(p1, lhsT=cBt, rhs=cB, start=True, stop=True), reads=[BcBt, BcB], writes=[Bp1])
            S.op("act", lambda e, tB=tB, p1=p1: e.copy(out=tB, in_=p1), reads=[Bp1], writes=[BtB])
            if lvl < 5:
                p2, Bp2 = QP.next()
                S.op("pe", lambda e, p2=p2, cBt=cBt, cB=cB: e.matmul(p2, lhsT=cB, rhs=cBt, start=True, stop=True), reads=[BcBt, BcB], writes=[Bp2])
                S.op("dve", lambda e, tBt=tBt, p2=p2: e.tensor_copy(out=tBt, in_=p2), reads=[Bp2], writes=[BtBt])
            yield
            p3, Bp3 = QP.next()
            S.op("pe", lambda e, p3=p3, tB=tB: e.matmul(p3, lhsT=tB, rhs=Pt, start=True, stop=True), reads=[BtB, BPt], writes=[Bp3])
            S.op("dve", lambda e, p3=p3: e.tensor_tensor(out=Pt, in0=Pt, in1=p3, op=ALU.add), reads=[BPt, Bp3], writes=[BPt])
            yield
            cur, nxt = nxt, cur
        pU, BpU = QP.next()
        for hb in range(2):
            P = slice(hb * 64, hb * 64 + 64)
            S.op("pe", lambda e, P=P: e.matmul(pU[P, :], lhsT=Pt[P, P], rhs=d["Vb"][P, n, :], start=True, stop=True),
                 reads=[BPt, d["BVb"][n]], writes=[BpU])
        S.op("act", lambda e: e.copy(out=d["U"][:, n, :], in_=pU), reads=[BpU], writes=[d["BU"][n]])
        pW, BpW = QP.next()
        S.op("pe", lambda e: e.matmul(pW, lhsT=d["Kbd"][:, n, :], rhs=Pt, start=True, stop=True), reads=[BPt, d["BKbd"][n]], writes=[BpW])
        S.op("dve", lambda e: e.tensor_copy(out=d["WT"][:, n, :], in_=pW), reads=[BpW], writes=[d["BWT"][n]])
        yield

    def unit_gen(d, u):
        fresh_bufs(d)
        cwsb, hp, S_sb = d["cw"], d["hp"], d["S"]
        Bcw, Bhp, BS = d["Bcw"], d["Bhp"], d["BS"]
        sc, Bsc = d["sc"], d["Bsc"]
        S.dma("sp", lambda e: e.dma_start(out=cwsb, in_=dn_cw.ap[u]), reads=[dn_cw.buf], writes=[Bcw])
        S.dma("sp", lambda e: e.dma_start(out=hp[:, 0:2], in_=dn_hp.ap[u]), reads=[dn_hp.buf], writes=[Bhp])
        S.op("act", lambda e: e.activation(out=hp[:, 2:3], in_=hp[:, 0:1], func=AF.Exp), reads=[Bhp], writes=[Bhp])
        S.op("dve", lambda e: e.tensor_scalar(out=hp[:, 2:3], in0=hp[:, 2:3], scalar1=-1.0, scalar2=None, op0=ALU.mult), reads=[Bhp], writes=[Bhp])
        S.op("dve", lambda e: e.memset(S_sb, 0.0), writes=[BS])
        yield
        padi = 0
        for seg in range(NSEG):
            s0 = seg * SEG
            for part, nm in ((0, "qf"), (1, "kf"), (2, "vf")):
                pad, Bpad = d["pad"][padi % 2], d["Bpad"][padi % 2]
                padi += 1
                dst, Bdst = d[nm], d["B" + nm]
                if seg == 0:
                    S.op("dve", lambda e, pad=pad: e.memset(pad[:, 0:3], 0.0), writes=[Bpad])
                    S.dma("sp", lambda e, pad=pad, part=part: e.dma_start(out=pad[:, 3:SEG + 3], in_=dn_in.get(u, part)[:, 0:SEG]),
                          reads=[dn_in.buf], writes=[Bpad])
                else:
                    S.dma("sp", lambda e, pad=pad, part=part, s0=s0: e.dma_start(out=pad, in_=dn_in.get(u, part)[:, s0 - 3:s0 + SEG]),
                          reads=[dn_in.buf], writes=[Bpad])
                S.op("dve", lambda e, pad=pad, dst=dst, part=part: e.tensor_scalar(
                    out=dst, in0=pad[:, 0:SEG], scalar1=cwsb[:, part, 0:1], scalar2=None, op0=ALU.mult),
                    reads=[Bpad, Bcw], writes=[Bdst])
                for j in range(1, 4):
                    S.op("dve", lambda e, pad=pad, dst=dst, part=part, j=j: e.scalar_tensor_tensor(
                        out=dst, in0=pad[:, j:j + SEG], scalar=cwsb[:, part, j:j + 1], in1=dst, op0=ALU.mult, op1=ALU.add),
                        reads=[Bpad, Bcw, Bdst], writes=[Bdst])
                S.op("act", lambda e, dst=dst: e.activation(out=dst, in_=dst, func=AF.Silu), reads=[Bdst], writes=[Bdst])
                if part < 2:
                    S.op("act", lambda e, dst=dst: e.activation(out=d["sqb"], in_=dst, func=AF.Square), reads=[Bdst], writes=[d["Bsqb"]])
                    ps, Bp = full_bank()
                    S.op("pe", lambda e, ps=ps: e.matmul(ps[:], lhsT=ones, rhs=d["sqb"], start=True, stop=True),
                         reads=[Bconst, d["Bsqb"]], writes=[Bp])
                    S.op("act", lambda e, ps=ps: e.activation(out=d["rn"], in_=ps[:], func=AF.Sqrt, bias=cx.epsb[:, 0:1], scale=1.0),
                         reads=[Bp, cx.Beps], writes=[d["Brn"]])
                    S.op("dve", lambda e: e.reciprocal(out=d["rn"], in_=d["rn"]), reads=[d["Brn"]], writes=[d["Brn"]])
                    qs = (HD ** -0.5) if part == 0 else 1.0
                    S.op("dve", lambda e, dst=dst, qs=qs: e.scalar_tensor_tensor(
                        out=dst, in0=dst, scalar=qs, in1=d["rn"], op0=ALU.mult, op1=ALU.mult),
                        reads=[Bdst, d["Brn"]], writes=[Bdst])
                yield
            S.dma("sp", lambda e, s0=s0: e.dma_start(out=d["zs"], in_=dn_in.get(u, 3)[:, s0:s0 + SEG]), reads=[dn_in.buf], writes=[d["Bzs"]])
            S.op("act", lambda e: e.activation(out=d["zs"], in_=d["zs"], func=AF.Silu), reads=[d["Bzs"]], writes=[d["Bzs"]])
            if hasattr(dn_ab, "get_row"):
                for ri in range(2):
                    S.dma("sp", lambda e, s0=s0, ri=ri: e.dma_start(
                        out=d["ab"][:, ri, :], in_=dn_ab.get_row(u, ri)[s0:s0 + SEG].rearrange("(n p) -> p n", p=128),
                        allow_slow_non_contiguous=True), reads=[dn_ab.buf], writes=[d["Bab"]])
            else:
                S.dma("sp", lambda e, seg=seg: e.dma_start(out=d["ab"], in_=dn_ab.ap[u][:, :, seg * NP:(seg + 1) * NP]),
                      reads=[dn_ab.buf], writes=[d["Bab"]])
            S.op("act", lambda e: e.activation(out=sc[:, TMP, :], in_=d["ab"][:, 0, :], func=AF.Exp, bias=hp[:, 1:2], scale=1.0),
                 reads=[d["Bab"], Bhp], writes=[Bsc[TMP]])
            S.op("act", lambda e: e.activation(out=sc[:, TMP, :], in_=sc[:, TMP, :], func=AF.Ln, bias=ones[:, 0:1], scale=1.0),
                 reads=[Bsc[TMP], Bconst], writes=[Bsc[TMP]])
            S.op("dve", lambda e: e.tensor_scalar(out=sc[:, G_, :], in0=sc[:, TMP, :], scalar1=hp[:, 2:3], scalar2=None, op0=ALU.mult),
                 reads=[Bsc[TMP], Bhp], writes=[Bsc[G_]])
            S.op("act", lambda e: e.activation(out=sc[:, BETA, :], in_=d["ab"][:, 1, :], func=AF.Sigmoid), reads=[d["Bab"]], writes=[Bsc[BETA]])
            S.op("dve", lambda e: e.tensor_scalar(out=sc[:, NBETA, :], in0=sc[:, BETA, :], scalar1=-1.0, scalar2=None, op0=ALU.mult),
                 reads=[Bsc[BETA]], writes=[Bsc[NBETA]])
            pq, Bq = QP.next()
            S.op("pe", lambda e, pq=pq: e.matmul(pq[:, 0:NP], lhsT=consts_sb[:, C_TRI, :], rhs=sc[:, G_, :], start=True, stop=True),
                 reads=[Bconst, Bsc[G_]], writes=[Bq])
            S.op("dve", lambda e, pq=pq: e.tensor_copy(out=sc[:, GC, :], in_=pq[:, 0:NP]), reads=[Bq], writes=[Bsc[GC]])
            yield
            for ci, slot in ((C_SELEND, GL), (C_SEL63, GL0), (C_SEL127, GL1)):
                pq, Bq = QP.next()
                S.op("pe", lambda e, pq=pq, ci=ci: e.matmul(pq[:, 0:NP], lhsT=consts_sb[:, ci, :], rhs=sc[:, GC, :], start=True, stop=True),
                     reads=[Bconst, Bsc[GC]], writes=[Bq])
                S.op("dve", lambda e, pq=pq, slot=slot: e.tensor_copy(out=sc[:, slot, :], in_=pq[:, 0:NP]), reads=[Bq], writes=[Bsc[slot]])
            S.op("act", lambda e: e.activation(out=sc[:, EG, :], in_=sc[:, GC, :], func=AF.Exp), reads=[Bsc[GC]], writes=[Bsc[EG]])
            S.op("dve", lambda e: e.tensor_tensor(out=sc[:, KBD, :], in0=sc[:, BETA, :], in1=sc[:, EG, :], op=ALU.mult),
                 reads=[Bsc[BETA], Bsc[EG]], writes=[Bsc[KBD]])
            S.op("dve", lambda e: e.tensor_tensor(out=sc[:, KDEC, :], in0=sc[:, GL, :], in1=sc[:, GC, :], op=ALU.subtract),
                 reads=[Bsc[GL], Bsc[GC]], writes=[Bsc[KDEC]])
            S.op("act", lambda e: e.activation(out=sc[:, KDEC, :], in_=sc[:, KDEC, :], func=AF.Exp), reads=[Bsc[KDEC]], writes=[Bsc[KDEC]])
            S.op("act", lambda e: e.activation(out=sc[:, D0, :], in_=sc[:, GL0, :], func=AF.Exp), reads=[Bsc[GL0]], writes=[Bsc[D0]])
            S.op("act", lambda e: e.activation(out=sc[:, D1, :], in_=sc[:, GL1, :], func=AF.Exp), reads=[Bsc[GL1]], writes=[Bsc[D1]])
            yield
            for n in range(NP):
                blk = slice(n * 128, (n + 1) * 128)
                pq, Bq = QP.next()
                S.op("pe", lambda e, pq=pq, blk=blk: e.transpose(out=pq, in_=d["kf"][:, blk], identity=ident), reads=[d["Bkf"], Bconst], writes=[Bq])
                S.op("dve", lambda e, pq=pq, n=n: e.tensor_scalar(out=d["Kbd"][:, n, :], in0=pq, scalar1=sc[:, KBD, n:n + 1], scalar2=None, op0=ALU.mult),
                     reads=[Bq, Bsc[KBD]], writes=[d["BKbd"][n]])
                S.op("dve", lambda e, pq=pq, n=n: e.tensor_scalar(out=d["Kdec"][:, n, :], in0=pq, scalar1=sc[:, KDEC, n:n + 1], scalar2=None, op0=ALU.mult),
                     reads=[Bq, Bsc[KDEC]], writes=[d["BKdec"][n]])
                pq, Bq = QP.next()
                S.op("pe", lambda e, pq=pq, blk=blk: e.transpose(out=pq, in_=d["vf"][:, blk], identity=ident), reads=[d["Bvf"], Bconst], writes=[Bq])
                S.op("dve", lambda e, pq=pq, n=n: e.tensor_scalar(out=d["Vb"][:, n, :], in0=pq, scalar1=sc[:, BETA, n:n + 1], scalar2=None, op0=ALU.mult),
                     reads=[Bq, Bsc[BETA]], writes=[d["BVb"][n]])
                yield
            yield from _roundrobin([pc_gen(d, n) for n in range(NP)])
            for c in range(2 * NP):
                n, hb = c // 2, c % 2
                P = slice(hb * 64, hb * 64 + 64)
                cols = slice(n * 128 + hb * 64, n * 128 + hb * 64 + 64)
                pa, Bpa = QP.next()
                S.op("pe", lambda e, pa=pa, n=n, P=P: e.matmul(pa[P, :], lhsT=d["WT"][:, n, P], rhs=S_sb, start=True, stop=True),
                     reads=[d["BWT"][n], BS], writes=[Bpa])
                vn, Bvn = d["vn"][c % 2], d["Bvn"][c % 2]
                S.op("dve", lambda e, vn=vn, pa=pa, n=n, P=P: e.tensor_tensor(out=vn[P, :], in0=d["U"][P, n, :], in1=pa[P, :], op=ALU.subtract),
                     reads=[d["BU"][n], Bpa], writes=[Bvn])
                po, Bpo = QP.next()
                S.op("pe", lambda e, po=po, cols=cols, P=P: e.matmul(po[P, :], lhsT=d["QdT"][:, cols], rhs=S_sb, start=True, stop=False),
                     reads=[d["BQdT"][n], BS], writes=[Bpo])
                S.op("pe", lambda e, po=po, n=n, P=P, vn=vn: e.matmul(po[P, :], lhsT=d["attnT"][P, n, P], rhs=vn[P, :], start=False, stop=True),
                     reads=[d["BattnT"][n], Bvn], writes=[Bpo])
                pS, BpS = QP.next()
                S.op("pe", lambda e, pS=pS, n=n, P=P, vn=vn: e.matmul(pS, lhsT=d["Kdec"][P, n, :], rhs=vn[P, :], start=True, stop=True),
                     reads=[d["BKdec"][n], Bvn], writes=[BpS])
                dslot = D0 if hb == 0 else D1
                S.op("dve", lambda e, pS=pS, dslot=dslot, n=n: e.scalar_tensor_tensor(
                    out=S_sb, in0=S_sb, scalar=sc[:, dslot, n:n + 1], in1=pS, op0=ALU.mult, op1=ALU.add),
                    reads=[BS, Bsc[dslot], BpS], writes=[BS])
                S.op("act", lambda e, po=po, n=n, P=P: e.copy(out=d["otm"][P, n, :], in_=po[P, :]), reads=[Bpo], writes=[d["Botm"][n]])
                yield
            S.op("act", lambda e: e.activation(out=d["sqo"], in_=d["otm"], func=AF.Square), reads=d["Botm"], writes=d["Bsqo"])
            S.op("dve", lambda e: e.tensor_reduce(out=d["sso"], in_=d["sqo"], axis=AX.X, op=ALU.add), reads=d["Bsqo"], writes=[d["Bsso"]])
            S.op("act", lambda e: e.activation(out=d["sso"], in_=d["sso"], func=AF.Sqrt, bias=cx.epsb[:, 0:1], scale=1.0 / HD),
                 reads=[d["Bsso"], cx.Beps], writes=[d["Bsso"]])
            S.op("dve", lambda e: e.reciprocal(out=d["sso"], in_=d["sso"]), reads=[d["Bsso"]], writes=[d["Bsso"]])
            S.op("dve", lambda e: e.tensor_tensor(out=d["otm"], in0=d["otm"], in1=d["sso"].unsqueeze(2).to_broadcast([128, NP, 128]), op=ALU.mult),
                 reads=d["Botm"] + [d["Bsso"]], writes=d["Botm"])
            yield
            for n in range(NP):
                blk = slice(n * 128, (n + 1) * 128)
                pT, BpT = QP.next()
                S.op("pe", lambda e, pT=pT, n=n: e.transpose(out=pT, in_=d["otm"][:, n, :], identity=ident), reads=[d["Botm"][n], Bconst], writes=[BpT])
                S.op("dve", lambda e, pT=pT, blk=blk: e.scalar_tensor_tensor(
                    out=d["oT"][:, blk], in0=pT, scalar=nw[:, 0:1], in1=d["zs"][:, blk], op0=ALU.mult, op1=ALU.mult),
                    reads=[BpT, Bnw, d["Bzs"]], writes=[d["BoT"]])
            S.dma("sp", lambda e, s0=s0: e.dma_start(out=dn_out.get(u)[:, s0:s0 + SEG], in_=d["oT"]),
                  reads=[d["BoT"]], writes=[dn_out.b((u, seg))], owner=d["BoT"], is_out=dn_out.is_out)
            yield

    for g0 in range(0, NU, GU):
        units = list(range(g0, min(NU, g0 + GU)))
        for _ in _roundrobin([unit_gen(slots[i], u) for i, u in enumerate(units)]):
            pass
    cx.release(m0)


def emit_dn3(cx, dn_in, dn_ab, dn_cw, dn_hp, dn_nw, dn_out, consts_sb, Bconst, NU=3, L=SEQ, tag="dn", GU=3, SEG=512):
    nc, S = cx.nc, cx.S
    S.barrier()
    m0 = cx.mark()
    NP = SEG // 128
    NSEG = L // SEG
    ident = consts_sb[:, C_ID, :]
    ones = consts_sb[:, C_ONES, :]
    QP = QPool(cx, [0, 1, 2, 3, 4, 5])
    BANK = [6, 7]
    bank_i = [0]

    def full_bank():
        b = BANK[bank_i[0] % 2]
        bank_i[0] += 1
        return cx.ps[b], cx.Bps[b]

    nw = cx.sb(tag + "nw", [128, 1], F32)
    Bnw = Buf("dnw")
    S.dma("sp", lambda e: e.dma_start(out=nw, in_=dn_nw.ap), reads=[dn_nw.buf], writes=[Bnw])
    NSC = 14
    (A_, BETA, NBETA, G_, GC, GL, GL0, GL1, EG, KBD, KDEC, D0, D1, TMP) = range(NSC)

    def mkslot(i):
        d = {}
        for nm in ("qf", "kf", "vf", "zs", "sqb", "rn", "oT", "QdT"):
            d[nm] = cx.sb("%s%s%d" % (tag, nm, i), [128, SEG], F32)
        for nm in ("Kbd", "Kdec", "Vb", "attnT", "U", "WT", "otm", "sqo"):
            d[nm] = cx.sb("%s%s%d" % (tag, nm, i), [128, NP, 128], F32)
        d["sc"] = cx.sb("%ssc%d" % (tag, i), [128, NSC, NP], F32)
        d["ab"] = cx.sb("%sab%d" % (tag, i), [128, 2, NP], F32)
        d["sso"] = cx.sb("%ssso%d" % (tag, i), [128, NP], F32)
        d["pad"] = [cx.sb("%spad%d_%d" % (tag, i, k), [128, SEG + 3], F32) for k in range(2)]
        d["cw"] = cx.sb("%scw%d" % (tag, i), [128, 3, 4], F32)
        d["hp"] = cx.sb("%shp%d" % (tag, i), [128, 4], F32)
        d["S"] = cx.sb("%sS%d" % (tag, i), [128, 128], F32)
        d["vn"] = [cx.sb("%svn%d_%d" % (tag, i, k), [128, 128], F32) for k in range(2)]
        d["tmp"] = [cx.sb("%st%d_%d" % (tag, i, k), [128, NP, 128], F32) for k in range(8)]
        return d

    def fresh_bufs(d):
        for nm in ("qf", "kf", "vf", "zs", "sqb", "rn", "oT", "ab", "sso", "cw", "hp", "S"):
            d["B" + nm] = Buf(nm)
        for nm in ("Kbd", "Kdec", "Vb", "attnT", "U", "WT", "otm", "sqo", "QdT"):
            _b = Buf(nm)
            d["B" + nm] = [_b] * NP
        d["Bsc"] = [Buf("sc%d" % k) for k in range(NSC)]
        d["Bpad"] = [Buf("pad0"), Buf("pad1")]
        d["Bvn"] = [Buf("vn0"), Buf("vn1")]
        d["Btmp"] = [Buf("t%d" % k) for k in range(8)]

    slots = [mkslot(i) for i in range(min(GU, NU))]

    def q4(ps, n):
        return ps[:, n * 128:(n + 1) * 128]

    def v3(ps):
        return ps[:, 0:NP * 128].rearrange("p (n c) -> p n c", n=NP)

    def bc(col_ap):
        return col_ap.unsqueeze(2).to_broadcast([128, NP, 128])

    def cb3(ci):
        return consts_sb[:, ci:ci + 1, :].to_broadcast([128, NP, 128])

    def fb():
        ap_, B_ = QP.next()
        b = QP.tiles[(QP.i - 1) % len(QP.tiles)]
        return QP.banks[(QP.i - 1) % len(QP.tiles)], B_

    def b1_gen(d):
        sc, Bsc = d["sc"], d["Bsc"]
        X0, dL, dU, Bm, Bt, nB, nBt, Pt = d["tmp"]
        BX0, BdL, BdU, BBm, BBt, BnB, BnBt, BPt = d["Btmp"]
        blks = [slice(n * 128, (n + 1) * 128) for n in range(NP)]
        gcb = bc(sc[:, GC, :])
        S.op("dve", lambda e: e.tensor_tensor(out=X0, in0=cb3(C_ID), in1=gcb, op=ALU.mult), reads=[Bconst, Bsc[GC]], writes=[BX0])
        pA, BpA = fb()
        for n in range(NP):
            S.op("pe", lambda e, n=n: e.matmul(q4(pA, n), lhsT=ones, rhs=X0[:, n, :], start=True, stop=True), reads=[Bconst, BX0], writes=[BpA])
        S.op("dve", lambda e: e.tensor_tensor(out=dL, in0=v3(pA), in1=gcb, op=ALU.subtract), reads=[BpA, Bsc[GC]], writes=[BdL])
        S.op("dve", lambda e: e.tensor_tensor(out=dU, in0=dL, in1=cb3(C_MU), op=ALU.add), reads=[BdL, Bconst], writes=[BdU])
        S.op("dve", lambda e: e.tensor_tensor(out=dL, in0=dL, in1=cb3(C_MLS), op=ALU.add), reads=[BdL, Bconst], writes=[BdL])
        S.op("act", lambda e: e.activation(out=dL, in_=dL, func=AF.Exp, scale=-1.0), reads=[BdL], writes=[BdL])
        S.op("act", lambda e: e.activation(out=dU, in_=dU, func=AF.Exp), reads=[BdU], writes=[BdU])
        S.op("act", lambda e: e.activation(out=X0, in_=v3(pA), func=AF.Exp), reads=[BpA], writes=[BX0])
        S.op("dve", lambda e: e.tensor_tensor(out=d["QdT"], in0=d["qf"], in1=X0.rearrange("p n c -> p (n c)"), op=ALU.mult),
             reads=[d["Bqf"], BX0], writes=[d["BQdT"][0]])
        yield
        pG, BpG = fb()
        for n in range(NP):
            S.op("pe", lambda e, n=n: e.matmul(q4(pG, n), lhsT=d["kf"][:, blks[n]], rhs=d["kf"][:, blks[n]], start=True, stop=True),
                 reads=[d["Bkf"]], writes=[BpG])
        S.op("dve", lambda e: e.tensor_tensor(out=Bm, in0=v3(pG), in1=bc(sc[:, NBETA, :]), op=ALU.mult), reads=[BpG, Bsc[NBETA]], writes=[BBm])
        S.op("dve", lambda e: e.tensor_tensor(out=Bm, in0=Bm, in1=dL, op=ALU.mult), reads=[BBm, BdL], writes=[BBm])
        pQ, BpQ = fb()
        for n in range(NP):
            S.op("pe", lambda e, n=n: e.matmul(q4(pQ, n), lhsT=d["kf"][:, blks[n]], rhs=d["qf"][:, blks[n]], start=True, stop=True),
                 reads=[d["Bkf"], d["Bqf"]], writes=[BpQ])
        S.op("dve", lambda e: e.tensor_tensor(out=d["attnT"], in0=v3(pQ), in1=dU, op=ALU.mult), reads=[BpQ, BdU], writes=[d["BattnT"][0]])
        yield
        pT, BpT = fb()
        for n in range(NP):
            S.op("pe", lambda e, n=n: e.transpose(out=q4(pT, n), in_=Bm[:, n, :], identity=ident), reads=[BBm, Bconst], writes=[BpT])
        S.op("act", lambda e: e.copy(out=Bt, in_=v3(pT)), reads=[BpT], writes=[BBt])
        S.op("dve", lambda e: e.tensor_tensor(out=Pt, in0=Bt, in1=cb3(C_ID), op=ALU.add), reads=[BBt, Bconst], writes=[BPt])
        yield
        cur = (Bm, BBm, Bt, BBt)
        nxt = (nB, BnB, nBt, BnBt)
        for lvl in range(1, 6):
            cB, BcB, cBt, BcBt = cur
            tB, BtB, tBt, BtBt = nxt
            p1, Bp1 = fb()
            for n in range(NP):
                S.op("pe", lambda e, n=n, p1=p1, cBt=cBt, cB=cB: e.matmul(q4(p1, n), lhsT=cBt[:, n, :], rhs=cB[:, n, :], start=True, stop=True),
                     reads=[BcBt, BcB], writes=[Bp1])
            S.op("act", lambda e, tB=tB, p1=p1: e.copy(out=tB, in_=v3(p1)), reads=[Bp1], writes=[BtB])
            if lvl < 5:
                p2, Bp2 = fb()
                for n in range(NP):
                    S.op("pe", lambda e, n=n, p2=p2, cBt=cBt, cB=cB: e.matmul(q4(p2, n), lhsT=cB[:, n, :], rhs=cBt[:, n, :], start=True, stop=True),
                         reads=[BcBt, BcB], writes=[Bp2])
                S.op("dve", lambda e, tBt=tBt, p2=p2: e.tensor_copy(out=tBt, in_=v3(p2)), reads=[Bp2], writes=[BtBt])
            yield
            p3, Bp3 = fb()
            for n in range(NP):
                S.op("pe", lambda e, n=n, p3=p3, tB=tB: e.matmul(q4(p3, n), lhsT=tB[:, n, :], rhs=Pt[:, n, :], start=True, stop=True),
                     reads=[BtB, BPt], writes=[Bp3])
            S.op("dve", lambda e, p3=p3: e.tensor_tensor(out=Pt, in0=Pt, in1=v3(p3), op=ALU.add), reads=[BPt, Bp3], writes=[BPt])
            yield
            cur, nxt = nxt, cur
        pU, BpU = fb()
        for n in range(NP):
            for hb in range(2):
                P = slice(hb * 64, hb * 64 + 64)
                S.op("pe", lambda e, n=n, P=P: e.matmul(pU[P, n * 128:(n + 1) * 128], lhsT=Pt[P, n, P], rhs=d["Vb"][P, n, :], start=True, stop=True),
                     reads=[BPt, d["BVb"][0]], writes=[BpU])
        S.op("act", lambda e: e.copy(out=d["U"], in_=v3(pU)), reads=[BpU], writes=[d["BU"][0]])
        pW, BpW = fb()
        for n in range(NP):
            S.op("pe", lambda e, n=n: e.matmul(q4(pW, n), lhsT=d["Kbd"][:, n, :], rhs=Pt[:, n, :], start=True, stop=True),
                 reads=[BPt, d["BKbd"][0]], writes=[BpW])
        S.op("dve", lambda e: e.tensor_copy(out=d["WT"], in_=v3(pW)), reads=[BpW], writes=[d["BWT"][0]])
        yield

    def unit_gen(d, u):
        fresh_bufs(d)
        cwsb, hp, S_sb = d["cw"], d["hp"], d["S"]
        Bcw, Bhp, BS = d["Bcw"], d["Bhp"], d["BS"]
        sc, Bsc = d["sc"], d["Bsc"]
        S.dma("sp", lambda e: e.dma_start(out=cwsb, in_=dn_cw.ap[u]), reads=[dn_cw.buf], writes=[Bcw])
        S.dma("sp", lambda e: e.dma_start(out=hp[:, 0:2], in_=dn_hp.ap[u]), reads=[dn_hp.buf], writes=[Bhp])
        S.op("act", lambda e: e.activation(out=hp[:, 2:3], in_=hp[:, 0:1], func=AF.Exp), reads=[Bhp], writes=[Bhp])
        S.op("dve", lambda e: e.tensor_scalar(out=hp[:, 2:3], in0=hp[:, 2:3], scalar1=-1.0, scalar2=None, op0=ALU.mult), reads=[Bhp], writes=[Bhp])
        S.op("dve", lambda e: e.memset(S_sb, 0.0), writes=[BS])
        yield
        padi = 0
        for seg in range(NSEG):
            s0 = seg * SEG
            for part, nm in ((0, "qf"), (1, "kf"), (2, "vf")):
                pad, Bpad = d["pad"][padi % 2], d["Bpad"][padi % 2]
                padi += 1
                dst, Bdst = d[nm], d["B" + nm]
                if seg == 0:
                    S.op("dve", lambda e, pad=pad: e.memset(pad[:, 0:3], 0.0), writes=[Bpad])
                    S.dma("sp", lambda e, pad=pad, part=part: e.dma_start(out=pad[:, 3:SEG + 3], in_=dn_in.get(u, part)[:, 0:SEG]),
                          reads=[dn_in.buf], writes=[Bpad])
                else:
                    S.dma("sp", lambda e, pad=pad, part=part, s0=s0: e.dma_start(out=pad, in_=dn_in.get(u, part)[:, s0 - 3:s0 + SEG]),
                          reads=[dn_in.buf], writes=[Bpad])
                S.op("dve", lambda e, pad=pad, dst=dst, part=part: e.tensor_scalar(
                    out=dst, in0=pad[:, 0:SEG], scalar1=cwsb[:, part, 0:1], scalar2=None, op0=ALU.mult),
                    reads=[Bpad, Bcw], writes=[Bdst])
                for j in range(1, 4):
                    S.op("dve", lambda e, pad=pad, dst=dst, part=part, j=j: e.scalar_tensor_tensor(
                        out=dst, in0=pad[:, j:j + SEG], scalar=cwsb[:, part, j:j + 1], in1=dst, op0=ALU.mult, op1=ALU.add),
                        reads=[Bpad, Bcw, Bdst], writes=[Bdst])
                S.op("act", lambda e, dst=dst: e.activation(out=dst, in_=dst, func=AF.Silu), reads=[Bdst], writes=[Bdst])
                if part < 2:
                    S.op("act", lambda e, dst=dst: e.activation(out=d["sqb"], in_=dst, func=AF.Square), reads=[Bdst], writes=[d["Bsqb"]])
                    ps, Bp = full_bank()
                    S.op("pe", lambda e, ps=ps: e.matmul(ps[:, 0:SEG], lhsT=ones, rhs=d["sqb"], start=True, stop=True),
                         reads=[Bconst, d["Bsqb"]], writes=[Bp])
                    S.op("act", lambda e, ps=ps: e.activation(out=d["rn"], in_=ps[:, 0:SEG], func=AF.Sqrt, bias=cx.epsb[:, 0:1], scale=1.0),
                         reads=[Bp, cx.Beps], writes=[d["Brn"]])
                    S.op("dve", lambda e: e.reciprocal(out=d["rn"], in_=d["rn"]), reads=[d["Brn"]], writes=[d["Brn"]])
                    qs = (HD ** -0.5) if part == 0 else 1.0
                    S.op("dve", lambda e, dst=dst, qs=qs: e.scalar_tensor_tensor(
                        out=dst, in0=dst, scalar=qs, in1=d["rn"], op0=ALU.mult, op1=ALU.mult),
                        reads=[Bdst, d["Brn"]], writes=[Bdst])
                yield
            S.dma("sp", lambda e, s0=s0: e.dma_start(out=d["zs"], in_=dn_in.get(u, 3)[:, s0:s0 + SEG]), reads=[dn_in.buf], writes=[d["Bzs"]])
            S.op("act", lambda e: e.activation(out=d["zs"], in_=d["zs"], func=AF.Silu), reads=[d["Bzs"]], writes=[d["Bzs"]])
            if hasattr(dn_ab, "get_row"):
                for ri in range(2):
                    S.dma("sp", lambda e, s0=s0, ri=ri: e.dma_start(
                        out=d["ab"][:, ri, :], in_=dn_ab.get_row(u, ri)[s0:s0 + SEG].rearrange("(n p) -> p n", p=128),
                        allow_slow_non_contiguous=True), reads=[dn_ab.buf], writes=[d["Bab"]])
            else:
                S.dma("sp", lambda e, seg=seg: e.dma_start(out=d["ab"], in_=dn_ab.ap[u][:, :, seg * NP:(seg + 1) * NP]),
                      reads=[dn_ab.buf], writes=[d["Bab"]])
            S.op("act", lambda e: e.activation(out=sc[:, TMP, :], in_=d["ab"][:, 0, :], func=AF.Exp, bias=hp[:, 1:2], scale=1.0),
                 reads=[d["Bab"], Bhp], writes=[Bsc[TMP]])
            S.op("act", lambda e: e.activation(out=sc[:, TMP, :], in_=sc[:, TMP, :], func=AF.Ln, bias=ones[:, 0:1], scale=1.0),
                 reads=[Bsc[TMP], Bconst], writes=[Bsc[TMP]])
            S.op("dve", lambda e: e.tensor_scalar(out=sc[:, G_, :], in0=sc[:, TMP, :], scalar1=hp[:, 2:3], scalar2=None, op0=ALU.mult),
                 reads=[Bsc[TMP], Bhp], writes=[Bsc[G_]])
            S.op("act", lambda e: e.activation(out=sc[:, BETA, :], in_=d["ab"][:, 1, :], func=AF.Sigmoid), reads=[d["Bab"]], writes=[Bsc[BETA]])
            S.op("dve", lambda e: e.tensor_scalar(out=sc[:, NBETA, :], in0=sc[:, BETA, :], scalar1=-1.0, scalar2=None, op0=ALU.mult),
                 reads=[Bsc[BETA]], writes=[Bsc[NBETA]])
            pq, Bq = QP.next()
            S.op("pe", lambda e, pq=pq: e.matmul(pq[:, 0:NP], lhsT=consts_sb[:, C_TRI, :], rhs=sc[:, G_, :], start=True, stop=True),
                 reads=[Bconst, Bsc[G_]], writes=[Bq])
            S.op("dve", lambda e, pq=pq: e.tensor_copy(out=sc[:, GC, :], in_=pq[:, 0:NP]), reads=[Bq], writes=[Bsc[GC]])
            yield
            for ci, slot in ((C_SELEND, GL), (C_SEL63, GL0), (C_SEL127, GL1)):
                pq, Bq = QP.next()
                S.op("pe", lambda e, pq=pq, ci=ci: e.matmul(pq[:, 0:NP], lhsT=consts_sb[:, ci, :], rhs=sc[:, GC, :], start=True, stop=True),
                     reads=[Bconst, Bsc[GC]], writes=[Bq])
                S.op("dve", lambda e, pq=pq, slot=slot: e.tensor_copy(out=sc[:, slot, :], in_=pq[:, 0:NP]), reads=[Bq], writes=[Bsc[slot]])
            S.op("act", lambda e: e.activation(out=sc[:, EG, :], in_=sc[:, GC, :], func=AF.Exp), reads=[Bsc[GC]], writes=[Bsc[EG]])
            S.op("dve", lambda e: e.tensor_tensor(out=sc[:, KBD, :], in0=sc[:, BETA, :], in1=sc[:, EG, :], op=ALU.mult),
                 reads=[Bsc[BETA], Bsc[EG]], writes=[Bsc[KBD]])
            S.op("dve", lambda e: e.tensor_tensor(out=sc[:, KDEC, :], in0=sc[:, GL, :], in1=sc[:, GC, :], op=ALU.subtract),
                 reads=[Bsc[GL], Bsc[GC]], writes=[Bsc[KDEC]])
            S.op("act", lambda e: e.activation(out=sc[:, KDEC, :], in_=sc[:, KDEC, :], func=AF.Exp), reads=[Bsc[KDEC]], writes=[Bsc[KDEC]])
            S.op("act", lambda e: e.activation(out=sc[:, D0, :], in_=sc[:, GL0, :], func=AF.Exp), reads=[Bsc[GL0]], writes=[Bsc[D0]])
            S.op("act", lambda e: e.activation(out=sc[:, D1, :], in_=sc[:, GL1, :], func=AF.Exp), reads=[Bsc[GL1]], writes=[Bsc[D1]])
            yield
            pK, BpK = fb()
            for n in range(NP):
                S.op("pe", lambda e, n=n, pK=pK: e.transpose(out=q4(pK, n), in_=d["kf"][:, n * 128:(n + 1) * 128], identity=ident),
                     reads=[d["Bkf"], Bconst], writes=[BpK])
            S.op("dve", lambda e, pK=pK: e.tensor_tensor(out=d["Kbd"], in0=v3(pK), in1=bc(sc[:, KBD, :]), op=ALU.mult),
                 reads=[BpK, Bsc[KBD]], writes=[d["BKbd"][0]])
            S.op("dve", lambda e, pK=pK: e.tensor_tensor(out=d["Kdec"], in0=v3(pK), in1=bc(sc[:, KDEC, :]), op=ALU.mult),
                 reads=[BpK, Bsc[KDEC]], writes=[d["BKdec"][0]])
            pV, BpV = fb()
            for n in range(NP):
                S.op("pe", lambda e, n=n, pV=pV: e.transpose(out=q4(pV, n), in_=d["vf"][:, n * 128:(n + 1) * 128], identity=ident),
                     reads=[d["Bvf"], Bconst], writes=[BpV])
            S.op("dve", lambda e, pV=pV: e.tensor_tensor(out=d["Vb"], in0=v3(pV), in1=bc(sc[:, BETA, :]), op=ALU.mult),
                 reads=[BpV, Bsc[BETA]], writes=[d["BVb"][0]])
            yield
            yield from b1_gen(d)
            for c in range(2 * NP):
                n, hb = c // 2, c % 2
                P = slice(hb * 64, hb * 64 + 64)
                cols = slice(n * 128 + hb * 64, n * 128 + hb * 64 + 64)
                pa, Bpa = QP.next()
                S.op("pe", lambda e, pa=pa, n=n, P=P: e.matmul(pa[P, :], lhsT=d["WT"][:, n, P], rhs=S_sb, start=True, stop=True),
                     reads=[d["BWT"][n], BS], writes=[Bpa])
                vn, Bvn = d["vn"][c % 2], d["Bvn"][c % 2]
                S.op("dve", lambda e, vn=vn, pa=pa, n=n, P=P: e.tensor_tensor(out=vn[P, :], in0=d["U"][P, n, :], in1=pa[P, :], op=ALU.subtract),
                     reads=[d["BU"][n], Bpa], writes=[Bvn])
                po, Bpo = QP.next()
                S.op("pe", lambda e, po=po, cols=cols, P=P: e.matmul(po[P, :], lhsT=d["QdT"][:, cols], rhs=S_sb, start=True, stop=False),
                     reads=[d["BQdT"][n], BS], writes=[Bpo])
                S.op("pe", lambda e, po=po, n=n, P=P, vn=vn: e.matmul(po[P, :], lhsT=d["attnT"][P, n, P], rhs=vn[P, :], start=False, stop=True),
                     reads=[d["BattnT"][n], Bvn], writes=[Bpo])
                pS, BpS = QP.next()
                S.op("pe", lambda e, pS=pS, n=n, P=P, vn=vn: e.matmul(pS, lhsT=d["Kdec"][P, n, :], rhs=vn[P, :], start=True, stop=True),
                     reads=[d["BKdec"][n], Bvn], writes=[BpS])
                dslot = D0 if hb == 0 else D1
                S.op("dve", lambda e, pS=pS, dslot=dslot, n=n: e.scalar_tensor_tensor(
                    out=S_sb, in0=S_sb, scalar=sc[:, dslot, n:n + 1], in1=pS, op0=ALU.mult, op1=ALU.add),
                    reads=[BS, Bsc[dslot], BpS], writes=[BS])
                S.op("act", lambda e, po=po, n=n, P=P: e.copy(out=d["otm"][P, n, :], in_=po[P, :]), reads=[Bpo], writes=[d["Botm"][n]])
                yield
            S.op("act", lambda e: e.activation(out=d["sqo"], in_=d["otm"], func=AF.Square), reads=d["Botm"], writes=d["Bsqo"])
            S.op("dve", lambda e: e.tensor_reduce(out=d["sso"], in_=d["sqo"], axis=AX.X, op=ALU.add), reads=d["Bsqo"], writes=[d["Bsso"]])
            S.op("act", lambda e: e.activation(out=d["sso"], in_=d["sso"], func=AF.Sqrt, bias=cx.epsb[:, 0:1], scale=1.0 / HD),
                 reads=[d["Bsso"], cx.Beps], writes=[d["Bsso"]])
            S.op("dve", lambda e: e.reciprocal(out=d["sso"], in_=d["sso"]), reads=[d["Bsso"]], writes=[d["Bsso"]])
            S.op("dve", lambda e: e.tensor_tensor(out=d["otm"], in0=d["otm"], in1=d["sso"].unsqueeze(2).to_broadcast([128, NP, 128]), op=ALU.mult),
                 reads=d["Botm"] + [d["Bsso"]], writes=d["Botm"])
            yield
            pO, BpO = fb()
            for n in range(NP):
                S.op("pe", lambda e, n=n, pO=pO: e.transpose(out=q4(pO, n), in_=d["otm"][:, n, :], identity=ident),
                     reads=[d["Botm"][0], Bconst], writes=[BpO])
            S.op("dve", lambda e, pO=pO: e.scalar_tensor_tensor(out=d["oT"], in0=pO[:, 0:SEG], scalar=nw[:, 0:1], in1=d["zs"], op0=ALU.mult, op1=ALU.mult),
                 reads=[BpO, Bnw, d["Bzs"]], writes=[d["BoT"]])
            S.dma("sp", lambda e, s0=s0: e.dma_start(out=dn_out.get(u)[:, s0:s0 + SEG], in_=d["oT"]),
                  reads=[d["BoT"]], writes=[dn_out.b((u, seg))], owner=d["BoT"], is_out=dn_out.is_out)
            yield

    for g0 in range(0, NU, GU):
        units = list(range(g0, min(NU, g0 + GU)))
        for _ in _roundrobin([unit_gen(slots[i], u) for i, u in enumerate(units)]):
            pass
    cx.release(m0)
```
